# Optimizing a Trainium2 kernel written in Bass

```python
import math
import jax, jax.numpy as jnp
from jax import lax
import numpy as np

D_MODEL = 1024
BATCH = 16
SEQ = 2048
DEPTH = 4
DEC_BATCH = 8
DEC_SEQ = 16
PAST_LEN = 2048

CHUNK = 64
N_MIXERS = 4
Q_BLOCK = 128
EPS = 1e-6

GM_CHUNK = 128
D_GM = D_MODEL
GM_GROUPS = 8
GM_GW = D_GM // GM_GROUPS
DIFF_HEADS = 8
DIFF_HD = D_MODEL // (2 * DIFF_HEADS)
ROPE_THETA = 500000.0
ROT_DIM = DIFF_HD // 4
D_SC = D_MODEL
SC_W = 3
SB_HEADS = 16
SB_HD = D_MODEL // SB_HEADS
D_FF = 2816
FFN_W = 3

N_A = (DEPTH + 3) // N_MIXERS
N_B = (DEPTH + 2) // N_MIXERS
N_C = (DEPTH + 1) // N_MIXERS
N_D = DEPTH // N_MIXERS

kernel_name = 'hybrid_streaming_encoder_step'


def rms_norm(x, g):
    xf = x.astype(jnp.float32)
    y = xf * lax.rsqrt(jnp.mean(xf * xf, axis=-1, keepdims=True) + EPS)
    return (y * g.astype(jnp.float32)).astype(x.dtype)


def layer_norm(x, g, b):
    xf = x.astype(jnp.float32)
    mu = jnp.mean(xf, axis=-1, keepdims=True)
    xc = xf - mu
    y = xc * lax.rsqrt(jnp.mean(xc * xc, axis=-1, keepdims=True) + EPS)
    return (y * g.astype(jnp.float32) + b.astype(jnp.float32)).astype(x.dtype)


def causal_dwconv(x, prev, w):
    width, S = w.shape[0], x.shape[1]
    xp = jnp.concatenate([prev.astype(x.dtype), x], axis=1)
    y = w[width - 1] * xp[:, width - 1:width - 1 + S]
    for k in range(width - 1):
        y = y + w[k] * xp[:, k:k + S]
    return y, xp[:, S:]


def partial_rope(x, pos):
    half = ROT_DIM // 2
    inv = ROPE_THETA ** (-jnp.arange(0, ROT_DIM, 2, dtype=jnp.float32) / ROT_DIM)
    ang = pos.astype(jnp.float32)[:, None] * inv[None, :]
    ang = ang.reshape((ang.shape[0],) + (1,) * (x.ndim - 3) + (half,))
    cos, sin = jnp.cos(ang), jnp.sin(ang)
    xf = x.astype(jnp.float32)
    x1, x2, rest = xf[..., :half], xf[..., half:ROT_DIM], xf[..., ROT_DIM:]
    out = jnp.concatenate([x1 * cos - x2 * sin, x2 * cos + x1 * sin, rest], axis=-1)
    return out.astype(x.dtype)


def sweep_blocks(fn, q, pos):
    B, S = q.shape[0], q.shape[1]
    nb = S // Q_BLOCK
    qb = jnp.moveaxis(q.reshape((B, nb, Q_BLOCK) + q.shape[2:]), 1, 0)
    pb = pos.reshape(nb, Q_BLOCK)
    ob = lax.map(lambda a: fn(a[0], a[1]), (qb, pb))
    return jnp.moveaxis(ob, 0, 1).reshape((B, S) + ob.shape[3:])


def diff_attn_block(q, k, v, q_pos, lam):
    K = k.shape[1]
    s = jnp.einsum('bqhcd,bkhcd->bhcqk', q.astype(jnp.float32), k.astype(jnp.float32)) * (DIFF_HD ** -0.5)
    mask = (jnp.arange(K)[None, :] // CHUNK) <= (q_pos[:, None] // CHUNK)
    p = jax.nn.softmax(jnp.where(mask, s, -jnp.inf), axis=-1)
    a = p[:, :, 0] - lam * p[:, :, 1]
    return jnp.einsum('bhqk,bkhe->bqhe', a, v.astype(jnp.float32))


def stick_breaking_block(q, k, v, q_pos):
    K = k.shape[1]
    z = jnp.einsum('bqhd,bkhd->bhqk', q.astype(jnp.float32), k.astype(jnp.float32)) * (SB_HD ** -0.5)
    mask = jnp.arange(K)[None, :] < q_pos[:, None]
    log_beta = jax.nn.log_sigmoid(z)
    log_keep = jnp.where(mask, jax.nn.log_sigmoid(-z), 0.0)
    after = lax.cumsum(log_keep, axis=3, reverse=True) - log_keep
    w = jnp.where(mask, jnp.exp(log_beta + after), 0.0)
    return jnp.einsum('bhqk,bkhd->bqhd', w, v.astype(jnp.float32))


def gmlp_mix(h, w_in, ln_g, ln_b, ws, bs, w_out):
    B, S, _ = h.shape
    L = GM_CHUNK if S >= GM_CHUNK else S
    u, v = jnp.split(jax.nn.gelu(h @ w_in), 2, axis=-1)
    v = layer_norm(v, ln_g, ln_b)
    vb = v.reshape(B, S // L, L, GM_GROUPS, GM_GW)
    w_s = jnp.tril(ws[:, :L, :L])
    mixed = jnp.einsum('gts,bnsgc->bntgc', w_s, vb) + bs[:, :L].T[None, None, :, :, None]
    return (u * mixed.reshape(B, S, D_GM)) @ w_out, v


def diff_mix(h, k_cache, v_cache, pos, w_qkv, lam_p, subln_g, w_out, lam_init):
    B, S, _ = h.shape
    q, k, v = jnp.split(h @ w_qkv, 3, axis=-1)
    q = partial_rope(q.reshape(B, S, DIFF_HEADS, 2, DIFF_HD), pos)
    k = partial_rope(k.reshape(B, S, DIFF_HEADS, 2, DIFF_HD), pos)
    v = v.reshape(B, S, DIFF_HEADS, 2 * DIFF_HD)
    lp = lam_p.astype(jnp.float32)
    lam = jnp.exp(jnp.sum(lp[0] * lp[1])) - jnp.exp(jnp.sum(lp[2] * lp[3])) + lam_init
    if k_cache is None:
        keys, vals = k, v
        o = sweep_blocks(lambda qb, pb: diff_attn_block(qb, keys, vals, pb, lam), q, pos)
    else:
        keys = jnp.concatenate([k_cache.astype(k.dtype), k], axis=1)
        vals = jnp.concatenate([v_cache.astype(v.dtype), v], axis=1)
        o = diff_attn_block(q, keys, vals, pos, lam)
    o = rms_norm(o, subln_g) * (1.0 - lam_init)
    return o.astype(h.dtype).reshape(B, S, D_MODEL) @ w_out, k, v


def short_conv_mix(h, prev, w_in, conv_w, w_out):
    b_gate, c_gate, xin = jnp.split(h @ w_in, 3, axis=-1)
    y, new_prev = causal_dwconv(c_gate * xin, prev, conv_w)
    return (b_gate * y) @ w_out, new_prev


def sb_mix(h, k_cache, v_cache, pos, w_qkv, w_out):
    B, S, _ = h.shape
    q, k, v = [t.reshape(B, S, SB_HEADS, SB_HD) for t in jnp.split(h @ w_qkv, 3, axis=-1)]
    if k_cache is None:
        keys, vals = k, v
        o = sweep_blocks(lambda qb, pb: stick_breaking_block(qb, keys, vals, pb), q, pos)
    else:
        keys = jnp.concatenate([k_cache.astype(k.dtype), k], axis=1)
        vals = jnp.concatenate([v_cache.astype(v.dtype), v], axis=1)
        o = stick_breaking_block(q, keys, vals, pos)
    return o.astype(h.dtype).reshape(B, S, D_MODEL) @ w_out, k, v


def conv_ffn(h, prev, w_in, conv_w, conv_b, w_out):
    g, u = jnp.split(h @ w_in, 2, axis=-1)
    g, new_prev = causal_dwconv(g, prev, conv_w)
    return (jax.nn.silu(g + conv_b) * u) @ w_out, new_prev


def trunk(x, c, pos, diff_k_c, diff_v_c, sconv_c, sb_k_c, sb_v_c, ffn_c,
          ada_w, ada_b, norm_g, gm_w_in, gm_ln_g, gm_ln_b, gm_ws, gm_bs, gm_w_out,
          diff_w_qkv, diff_lambda, diff_subln_g, diff_w_out,
          sc_w_in, sc_conv_w, sc_w_out, sb_w_qkv, sb_w_out,
          ffn_w_in, ffn_conv_w, ffn_conv_b, ffn_w_out):
    streaming = diff_k_c is not None
    Bsz = x.shape[0]
    cs = jax.nn.silu(c)
    gm_v, dk, dv, scs, sbk, sbv, ffs = [], [], [], [], [], [], []
    for i in range(DEPTH):
        kind, j = i % N_MIXERS, i // N_MIXERS
        mod = (cs @ ada_w[i] + ada_b[i]).reshape(Bsz, 6, 1, D_MODEL)
        h = rms_norm(x, norm_g[i, 0]) * (1 + mod[:, 1]) + mod[:, 0]
        if kind == 0:
            m, v_rows = gmlp_mix(h, gm_w_in[j], gm_ln_g[j], gm_ln_b[j], gm_ws[j], gm_bs[j], gm_w_out[j])
            gm_v.append(v_rows)
        elif kind == 1:
            m, k_new, v_new = diff_mix(h, diff_k_c[j] if streaming else None,
                                       diff_v_c[j] if streaming else None, pos,
                                       diff_w_qkv[j], diff_lambda[j], diff_subln_g[j], diff_w_out[j],
                                       0.8 - 0.6 * math.exp(-0.3 * i))
            dk.append(k_new)
            dv.append(v_new)
        elif kind == 2:
            prev = sconv_c[j] if streaming else jnp.zeros((Bsz, SC_W - 1, D_SC), x.dtype)
            m, st = short_conv_mix(h, prev, sc_w_in[j], sc_conv_w[j], sc_w_out[j])
            scs.append(st)
        else:
            m, k_new, v_new = sb_mix(h, sb_k_c[j] if streaming else None,
                                     sb_v_c[j] if streaming else None, pos, sb_w_qkv[j], sb_w_out[j])
            sbk.append(k_new)
            sbv.append(v_new)
        x = x + mod[:, 2] * rms_norm(m, norm_g[i, 1])
        h = rms_norm(x, norm_g[i, 2]) * (1 + mod[:, 4]) + mod[:, 3]
        prev = ffn_c[i] if streaming else jnp.zeros((Bsz, FFN_W - 1, D_FF), x.dtype)
        f, st = conv_ffn(h, prev, ffn_w_in[i], ffn_conv_w[i], ffn_conv_b[i], ffn_w_out[i])
        ffs.append(st)
        x = x + mod[:, 5] * rms_norm(f, norm_g[i, 3])
    return (x, jnp.stack(gm_v), jnp.stack(dk), jnp.stack(dv), jnp.stack(scs),
            jnp.stack(sbk), jnp.stack(sbv), jnp.stack(ffs))


def setup_inputs(seed: int = 0) -> dict:
    key = jax.random.key(seed)
    ks = iter(jax.random.split(key, 40))
    D = D_MODEL

    def nrm(shape, scale):
        return scale * jax.random.normal(next(ks), shape, jnp.float32)

    return {
        'x_prompt': nrm((BATCH, SEQ, D), 1.0),
        'x_sample': nrm((DEC_BATCH, DEC_SEQ, D), 1.0),
        'cache_diff_k': nrm((N_B, DEC_BATCH, PAST_LEN, DIFF_HEADS, 2, DIFF_HD), 1.0),
        'cache_diff_v': nrm((N_B, DEC_BATCH, PAST_LEN, DIFF_HEADS, 2 * DIFF_HD), 1.0),
        'state_sconv': nrm((N_C, DEC_BATCH, SC_W - 1, D_SC), 1.0),
        'cache_sb_k': nrm((N_D, DEC_BATCH, PAST_LEN, SB_HEADS, SB_HD), 1.0),
        'cache_sb_v': nrm((N_D, DEC_BATCH, PAST_LEN, SB_HEADS, SB_HD), 1.0),
        'state_ffn_conv': nrm((DEPTH, DEC_BATCH, FFN_W - 1, D_FF), 1.0),
        'c_prompt': nrm((BATCH, D), 1.0),
        'c_sample': nrm((DEC_BATCH, D), 1.0),
        'ada_w': nrm((DEPTH, D, 6 * D), 0.5 * D ** -0.5),
        'ada_b': nrm((DEPTH, 6 * D), 0.02),
        'norm_g': 1.0 + nrm((DEPTH, 4, D), 0.05),
        'gm_w_in': nrm((N_A, D, 2 * D_GM), D ** -0.5),
        'gm_ln_g': 1.0 + nrm((N_A, D_GM), 0.05),
        'gm_ln_b': nrm((N_A, D_GM), 0.02),
        'gm_ws': nrm((N_A, GM_GROUPS, GM_CHUNK, GM_CHUNK), GM_CHUNK ** -0.5),
        'gm_bs': nrm((N_A, GM_GROUPS, GM_CHUNK), 0.02),
        'gm_w_out': nrm((N_A, D_GM, D), D_GM ** -0.5),
        'diff_w_qkv': nrm((N_B, D, 3 * D), D ** -0.5),
        'diff_lambda': nrm((N_B, 4, DIFF_HD), 0.1),
        'diff_subln_g': 1.0 + nrm((N_B, 2 * DIFF_HD), 0.05),
        'diff_w_out': nrm((N_B, D, D), D ** -0.5),
        'sc_w_in': nrm((N_C, D, 3 * D_SC), D ** -0.5),
        'sc_conv_w': nrm((N_C, SC_W, D_SC), SC_W ** -0.5),
        'sc_w_out': nrm((N_C, D_SC, D), D_SC ** -0.5),
        'sb_w_qkv': nrm((N_D, D, 3 * D), D ** -0.5),
        'sb_w_out': nrm((N_D, D, D), D ** -0.5),
        'ffn_w_in': nrm((DEPTH, D, 2 * D_FF), D ** -0.5),
        'ffn_conv_w': nrm((DEPTH, FFN_W, D_FF), FFN_W ** -0.5),
        'ffn_conv_b': nrm((DEPTH, D_FF), 0.02),
        'ffn_w_out': nrm((DEPTH, D_FF, D), D_FF ** -0.5),
    }


def reference(x_prompt, x_sample, cache_diff_k, cache_diff_v, state_sconv, cache_sb_k, cache_sb_v,
              state_ffn_conv, c_prompt, c_sample, ada_w, ada_b, norm_g,
              gm_w_in, gm_ln_g, gm_ln_b, gm_ws, gm_bs, gm_w_out,
              diff_w_qkv, diff_lambda, diff_subln_g, diff_w_out,
              sc_w_in, sc_conv_w, sc_w_out, sb_w_qkv, sb_w_out,
              ffn_w_in, ffn_conv_w, ffn_conv_b, ffn_w_out):
    weights = (ada_w, ada_b, norm_g, gm_w_in, gm_ln_g, gm_ln_b, gm_ws, gm_bs, gm_w_out,
               diff_w_qkv, diff_lambda, diff_subln_g, diff_w_out,
               sc_w_in, sc_conv_w, sc_w_out, sb_w_qkv, sb_w_out,
               ffn_w_in, ffn_conv_w, ffn_conv_b, ffn_w_out)
    pos_p = jnp.arange(x_prompt.shape[1])
    pos_s = cache_diff_k.shape[2] + jnp.arange(x_sample.shape[1])
    (y_prompt, _gm_v_prompt, diff_k_prompt, diff_v_prompt, sconv_prompt,
     sb_k_prompt, sb_v_prompt, ffn_conv_prompt) = trunk(
        x_prompt, c_prompt, pos_p, None, None, None, None, None, None, *weights)
    (y_sample, gm_v_sample, diff_k_sample, diff_v_sample, sconv_sample,
     sb_k_sample, sb_v_sample, ffn_conv_sample) = trunk(
        x_sample, c_sample, pos_s, cache_diff_k, cache_diff_v, state_sconv,
        cache_sb_k, cache_sb_v, state_ffn_conv, *weights)
    return (y_prompt, y_sample, gm_v_sample,
            diff_k_prompt, diff_v_prompt, diff_k_sample, diff_v_sample,
            sconv_prompt, sconv_sample,
            sb_k_prompt, sb_v_prompt, sb_k_sample, sb_v_sample,
            ffn_conv_prompt, ffn_conv_sample)
```

```python
import math
import contextlib
import numpy as np
import concourse.bass as bass
import concourse.mybir as mybir
from concourse.bass_utils import run_bass_kernel_spmd

F32 = mybir.dt.float32
BF16 = mybir.dt.bfloat16
ALU = mybir.AluOpType
AF = mybir.ActivationFunctionType
AX = mybir.AxisListType

NCORES = 8
D = 1024
KC = 8
DFF = 2816
FC = 22
SEQ = 2048
DEC = 16
EPS = 1e-6
LAM_INIT = 0.8 - 0.6 * math.exp(-0.3 * 1)


class Tile:
    __slots__ = ("name", "w", "r")

    def __init__(self, name):
        self.name = name
        self.w = None
        self.r = {}


class DTile(Tile):
    __slots__ = ("semkey", "count")

    def __init__(self, name, semkey):
        super().__init__(name)
        self.semkey = semkey
        self.count = 0


class _Eng:
    def __init__(self, name, obj, semkey):
        self.name = name
        self.obj = obj
        self.semkey = semkey
        self.count = 0
        self.known = {}
        self.ninst = 0


class Sched:
    def __init__(self, nc, stack):
        self.nc = nc
        self.stack = stack
        self.sems = []
        self.eng = {}
        for name, obj in (("pe", nc.tensor), ("act", nc.scalar), ("dve", nc.vector),
                          ("pool", nc.gpsimd), ("sp", nc.sync)):
            k = self.new_sem("e_" + name)
            self.eng[name] = _Eng(name, obj, k)
        self.dtiles = []

    def new_sem(self, name):
        h = self.stack.enter_context(self.nc.semaphore(name))
        self.sems.append(h)
        return len(self.sems) - 1

    def tile(self, name):
        return Tile(name)

    def dtile(self, name):
        t = DTile(name, self.new_sem("d_" + name))
        self.dtiles.append(t)
        return t

    def retire(self, srcs, dsts):
        for d in dsts:
            for s in srcs:
                if s is d:
                    continue
                if s.w is not None:
                    if d.r.get(s.w[0], 0) < s.w[1]:
                        d.r[s.w[0]] = s.w[1]
                for k, v in s.r.items():
                    if d.r.get(k, 0) < v:
                        d.r[k] = v

    def _deps(self, E, reads, writes):
        deps = {}

        def add(k, v, raw):
            if k == E.semkey and not raw:
                if E.name == "pe" or v > E.count:
                    return
            if E.known.get(k, 0) >= v:
                return
            if deps.get(k, 0) < v:
                deps[k] = v

        for t in reads:
            if t.w is not None:
                add(t.w[0], t.w[1], True)
        for t in writes:
            if t.w is not None:
                add(t.w[0], t.w[1], False)
            for k, v in t.r.items():
                add(k, v, False)
        return deps

    def _emit(self, E, deps, fn):
        items = list(deps.items())
        for k, v in items:
            if k == E.semkey:
                assert v <= E.count, f"self-wait on future signal {E.name} {v} {E.count}"
        for k, v in items[:-1]:
            E.obj.wait_ge(self.sems[k], v)
            E.known[k] = v
            E.ninst += 1
        inst = fn(E.obj)
        E.ninst += 1
        if items:
            k, v = items[-1]
            inst._wait_ge(self.sems[k], v)
            E.known[k] = v
        return inst

    def op(self, eng, fn, reads=(), writes=(), signal=True):
        E = self.eng[eng]
        deps = self._deps(E, reads, writes)
        inst = self._emit(E, deps, fn)
        if signal:
            inst.then_inc(self.sems[E.semkey], 1)
            E.count += 1
            val = E.count
        else:
            val = E.count + 1
        for t in reads:
            if t.r.get(E.semkey, 0) < val:
                t.r[E.semkey] = val
        for t in writes:
            t.w = (E.semkey, val)
            t.r = {}
        return inst

    def dma(self, q, out_ap, in_ap, dt, load, extra_reads=(), extra_writes=(), **kw):
        E = self.eng[q]
        if load:
            deps = self._deps(E, extra_reads, [dt] + list(extra_writes))
        else:
            deps = self._deps(E, [dt] + list(extra_reads), extra_writes)
        inst = self._emit(E, deps, lambda e: e.dma_start(out=out_ap, in_=in_ap, **kw))
        inst.then_inc(self.sems[dt.semkey], 16)
        dt.count += 16
        if load:
            dt.w = (dt.semkey, dt.count)
            dt.r = {}
            for t in extra_writes:
                t.w = (dt.semkey, dt.count)
                t.r = {}
        else:
            dt.r[dt.semkey] = dt.count
            for t in extra_reads:
                if t.r.get(dt.semkey, 0) < dt.count:
                    t.r[dt.semkey] = dt.count
        return inst

    def finish(self):
        E = self.eng["sp"]
        for t in self.dtiles:
            if t.count > 0 and E.known.get(t.semkey, 0) < t.count:
                E.obj.wait_ge(self.sems[t.semkey], t.count)
        for name in ("pe", "act", "dve", "pool"):
            X = self.eng[name]
            if X.count > 0:
                E.obj.wait_ge(self.sems[X.semkey], X.count)


class Buf:
    def __init__(self, t, rowlen, dtype):
        self.t = t
        self.rowlen = rowlen
        self.dtype = dtype

    def v(self, off, dims, p0=0, np_=128):
        return bass.AP(self.t, p0 * self.rowlen + off, [[self.rowlen, np_]] + [[s, n] for s, n in dims])


class SubBuf:
    def __init__(self, parent, base):
        self.parent = parent
        self.base = base
        self.rowlen = parent.rowlen
        self.dtype = parent.dtype

    def v(self, off, dims, p0=0, np_=128):
        return self.parent.v(self.base + off, dims, p0=p0, np_=np_)


class SeqCfg:
    def __init__(self, name, T, b, sample, idx):
        self.name = name
        self.T = T
        self.b = b
        self.sample = sample
        self.idx = idx
        self.nh = 1 if sample else 2
        self.TH = T // self.nh
        self.BW = min(512, self.TH)
        self.nb = self.TH // self.BW
        self.TW = min(128, self.TH)
        self.nt = self.TH // self.TW


class StopBuild(Exception):
    pass


class KB:
    def __init__(self, dbg=None, nlayers=4, seqs=("p0", "p1", "s"), cut=None, warm=1):
        self.cut = cut
        self.warm = warm
        self.dbg = dbg or {}
        self.nlayers = nlayers
        self.seq_names = seqs
        self.nc = bass.Bass("TRN2", target_bir_lowering=False)
        self.stack = contextlib.ExitStack()

    def din(self, name, shape):
        return self.nc.dram_tensor(name, list(shape), F32, kind="ExternalInput").ap()

    def dout(self, name, shape):
        return self.nc.dram_tensor(name, list(shape), F32, kind="ExternalOutput").ap()

    def sb(self, name, rowlen, dtype, np_=128):
        t = self.stack.enter_context(self.nc.sbuf_tensor(name, [np_, rowlen], dtype))
        return Buf(t, rowlen, dtype)

    def declare(self):
        S = self.S = Sched(self.nc, self.stack)
        self.i = {}
        I = self.i
        I["xp"] = self.din("xp", [2, SEQ, D])
        I["xs"] = self.din("xs", [DEC, D])
        I["cdk"] = self.din("cdk", [SEQ, D])
        I["cdv"] = self.din("cdv", [SEQ, D])
        I["csk"] = self.din("csk", [SEQ, D])
        I["csv"] = self.din("csv", [SEQ, D])
        I["r1"] = self.din("r1", [48, D])
        I["r2"] = self.din("r2", [24, DFF])
        I["cst"] = self.din("cst", [128, 512])
        I["rope"] = self.din("rope", [17 * 128, 16])
        I["ada_w"] = self.din("ada_w", [4, D, 6 * D])
        I["gm_w_in"] = self.din("gm_w_in", [D, 2 * D])
        I["gm_ln_g"] = self.din("gm_ln_g", [1, D])
        I["gm_ln_b"] = self.din("gm_ln_b", [1, D])
        I["gm_ws"] = self.din("gm_ws", [8, 128, 128])
        I["gm_bs"] = self.din("gm_bs", [1, D])
        I["gm_w_out"] = self.din("gm_w_out", [D, D])
        I["diff_w_qkv"] = self.din("diff_w_qkv", [8, 128, 3 * D])
        I["diff_lambda"] = self.din("diff_lambda", [1, 256])
        I["diff_subln_g"] = self.din("diff_subln_g", [1, 128])
        I["diff_w_out"] = self.din("diff_w_out", [D, D])
        I["sc_w_in"] = self.din("sc_w_in", [D, 3 * D])
        I["sc_w_out"] = self.din("sc_w_out", [D, D])
        I["sb_w_qkv"] = self.din("sb_w_qkv", [8, 128, 3 * D])
        I["sb_w_out"] = self.din("sb_w_out", [D, D])
        I["ffn_w_in"] = self.din("ffn_w_in", [4, D, 2 * DFF])
        I["ffn_w_out"] = self.din("ffn_w_out", [4, 8, 128, DFF])
        self.o = {}
        O = self.o
        O["y_p"] = self.dout("y_p", [2, SEQ, D])
        O["y_s"] = self.dout("y_s", [DEC, D])
        O["gmv_s"] = self.dout("gmv_s", [DEC, D])
        O["dk_p"] = self.dout("dk_p", [2, SEQ, D])
        O["dv_p"] = self.dout("dv_p", [2, SEQ, D])
        O["dk_s"] = self.dout("dk_s", [DEC, D])
        O["dv_s"] = self.dout("dv_s", [DEC, D])
        O["sc_p"] = self.dout("sc_p", [2, 2, D])
        O["sc_s"] = self.dout("sc_s", [2, D])
        O["sbk_p"] = self.dout("sbk_p", [2, SEQ, D])
        O["sbv_p"] = self.dout("sbv_p", [2, SEQ, D])
        O["sbk_s"] = self.dout("sbk_s", [DEC, D])
        O["sbv_s"] = self.dout("sbv_s", [DEC, D])
        O["ffc_p"] = self.dout("ffc_p", [4, 2, 2, DFF])
        O["ffc_s"] = self.dout("ffc_s", [4, 2, DFF])
        for name, shape in self.dbg.items():
            O["dbg_" + name] = self.dout("dbg_" + name, shape)

        self.X = self.sb("X", 8 * SEQ, F32)
        self.Xt = [S.tile("X0"), S.tile("X1")]
        self.HB = self.sb("HB", 8 * SEQ, BF16)
        self.HBt = [S.tile("HB0"), S.tile("HB1")]
        self.BIG = self.sb("BIG", FC * 1024, BF16)
        self.BIGa = S.tile("BIGa")
        self.BIGb = S.tile("BIGb")
        self.WS = [self.sb(f"WS{k}", 3072, BF16) for k in range(3)]
        self.WSt = [S.dtile(f"WS{k}") for k in range(3)]
        self.wsi = 0
        self.ATT = self.sb("ATT", 8704, BF16)
        self.ATTq = S.dtile("ATTq")
        self.ATTk = S.dtile("ATTk")
        self.ATTv = S.dtile("ATTv")
        self.WS += [SubBuf(self.ATT, 0), SubBuf(self.ATT, 3072)]
        self.WSt += [S.dtile("WS3"), S.dtile("WS4")]
        self.wide = False
        self.WF = [self.sb(f"WF{k}", 512, F32) for k in range(4)]
        self.WFt = [S.tile(f"WF{k}") for k in range(4)]
        self.WB = [self.sb(f"WB{k}", 512, BF16) for k in range(4)]
        self.WBt = [S.tile(f"WB{k}") for k in range(4)]
        self.GS = self.sb("GS", 1026, F32)
        self.GSt = S.tile("GS")
        self.LAMB = self.sb("LAMB", 512, BF16)
        self.LAMBt = S.dtile("LAMB")
        self.QKVs = [self.sb(f"QKV{k}", 384, F32) for k in range(2)]
        self.QKVts = [S.tile(f"QKV{k}") for k in range(2)]
        self.QKB = [SubBuf(self.LAMB, 0), SubBuf(self.LAMB, 256)]
        self.QKBt = [S.tile("QKB0"), S.tile("QKB1")]
        self.MODL = self.sb("MODL", 144, F32)
        self.MODLt = S.tile("MODL")
        self.V1T = self.sb("V1T", 8 * 48, F32)
        self.V1Tt = S.tile("V1T")
        self.V2T = self.sb("V2T", FC * 24, F32)
        self.V2Tt = S.tile("V2T")
        self.SC = self.sb("SC", 576, F32)
        self.SCt = S.tile("SC")
        self.CF = self.sb("CF", 512, F32)
        self.CFt = S.dtile("CF")
        self.CB = self.sb("CB", 7 * 128, BF16)
        self.CBt = S.tile("CB")
        self.ROPE = self.sb("ROPE", 17 * 16, F32)
        self.ROPEt = S.dtile("ROPE")
        self.GSUB = self.sb("GSUB", 128, F32)
        self.GSUBt = S.dtile("GSUB")
        self.SM = self.sb("SM", 64, F32)
        self.SMt = S.tile("SM")
        self.CSB = self.sb("CSB", 24, BF16)
        self.CSBt = S.tile("CSB")
        self.GCAR = self.sb("GCAR", FC * 2, F32)
        self.GCARt = S.tile("GCAR")
        self.PCAR = self.sb("PCAR", 16, F32)
        self.PCARt = S.tile("PCAR")
        self.HAL = self.sb("HAL", 6, F32)
        self.HALt = [S.tile("HAL0"), S.tile("HAL1"), S.tile("HALZ")]
        self.STG = [S.dtile("STG0"), S.dtile("STG1")]
        self.KSTt = S.dtile("KST")
        self.VSTt = S.dtile("VST")
        self.OST = S.dtile("OST")
        self.PS = []
        self.PSt = []
        for k in range(8):
            t = self.stack.enter_context(self.nc.psum_tensor(f"PS{k}", [128, 512], F32))
            self.PS.append(Buf(t, 512, F32))
            self.PSt.append(S.tile(f"PS{k}"))

    def Xv(self, kc, c0, n, np_=128):
        return self.X.v(kc * SEQ + c0, [(1, n)], np_=np_)

    def Hv(self, kc, c0, n):
        return self.HB.v(kc * SEQ + c0, [(1, n)])

    def Fv(self, mo, c0, n):
        return self.HB.v(0, [(1, 8 * SEQ)]).bitcast(F32)[:, mo * 1024 + c0: mo * 1024 + c0 + n]

    def Av(self, j, c0, n):
        return self.BIG.v(j * 1024 + c0, [(1, n)])

    def Ov(self, kc, c0, n):
        return self.BIG.v(kc * SEQ + c0, [(1, n)])

    def bigf(self, e0, n):
        return self.BIG.v(e0, [(1, 2 * n)]).bitcast(F32)

    def attf(self, e0, n):
        return self.ATT.v(e0, [(1, 2 * n)]).bitcast(F32)

    def gsf(self):
        return self.GS

    def sc(self, seq, i, s, kc):
        off = ((i * 3 + seq.b) * 6 + s) * 8 + kc
        return self.SC.v(off, [(1, 1)])

    def cb(self, which, k=128, m=128, p0=0):
        idx = {"ones_s": 0, "ones": 1, "tri": 2, "zero": 3, "msu": 4, "ident": 5, "trile": 6}[which]
        return self.CB.v(idx * 128, [(1, m)], p0=p0, np_=k)

    def idf(self, k):
        return self.CF.v(0, [(1, k)], np_=k)

    def load_panel(self, pieces, nk):
        S = self.S
        k = self.wsi % (5 if self.wide else 3)
        self.wsi += 1
        slot, st = self.WS[k], self.WSt[k]
        tot = sum(n for _, n in pieces)
        assert nk * tot <= 3072
        c = 0
        for ap, n in pieces:
            dst = slot.v(c, [(1, n)]) if nk == 1 else slot.v(c, [(tot, nk), (1, n)])
            S.dma("pool", dst, ap, st, True)
            c += n
        return slot, st, tot

    @staticmethod
    def wview(w2d, nk):
        return w2d.rearrange("(kc p) n -> p kc n", p=128)

    def cp(self, name):
        if self.cut == name:
            raise StopBuild()

    def set_wide(self, on):
        att = [self.ATTq, self.ATTk, self.ATTv]
        ext = [self.WSt[3], self.WSt[4]]
        if on:
            self.S.retire(att, ext)
        else:
            self.S.retire(ext, att)
        self.wide = on

    def dump(self, name, src_ap, tiles):
        if name not in self.dbg:
            return
        S = self.S
        dt = S.dtile("dbg_" + name)
        S.dma("sp", self.o["dbg_" + name], src_ap, dt, False, extra_reads=tiles)

    def setup(self):
        S, I = self.S, self.i
        S.dma("sp", self.CF.v(0, [(1, 512)]), I["cst"][:, :], self.CFt, True)
        S.op("pool", lambda e: e.memset(self.CB.v(0, [(1, 128)]), 1.0 / 1024.0), writes=[self.CBt])
        S.op("pool", lambda e: e.memset(self.CB.v(128, [(1, 128)]), 1.0), writes=[self.CBt])
        S.op("pool", lambda e: e.memset(self.CB.v(384, [(1, 128)]), 0.0), writes=[self.CBt])
        S.op("dve", lambda e: e.tensor_copy(self.CB.v(256, [(1, 128)]), self.CF.v(128, [(1, 128)])), reads=[self.CFt], writes=[self.CBt])
        S.op("dve", lambda e: e.tensor_copy(self.CB.v(512, [(1, 128)]), self.CF.v(256, [(1, 128)])), reads=[self.CFt], writes=[self.CBt])
        S.op("dve", lambda e: e.tensor_copy(self.CB.v(640, [(1, 128)]), self.CF.v(0, [(1, 128)])), reads=[self.CFt], writes=[self.CBt])
        S.op("dve", lambda e: e.tensor_scalar(self.CB.v(768, [(1, 128)]), self.CF.v(128, [(1, 128)]), -1.0, 1.0, ALU.mult, ALU.add),
             reads=[self.CFt], writes=[self.CBt])
        S.op("pool", lambda e: e.memset(self.GCAR.v(0, [(1, FC * 2)]), 0.0), writes=[self.GCARt])
        S.op("pool", lambda e: e.memset(self.PCAR.v(0, [(1, 16)]), 0.0), writes=[self.PCARt])
        S.op("pool", lambda e: e.memset(self.HAL.v(0, [(1, 6)]), 0.0), writes=self.HALt)
        S.dma("sp", self.ROPE.v(0, [(16, 17), (1, 16)]), I["rope"].rearrange("(t p) c -> p t c", p=128), self.ROPEt, True)
        st0 = self.bigf(0, 1024)
        S.dma("sp", st0[0:48, :], I["r1"][:, :], self.STG[0], True)
        ps = self.PS[0]
        for kc in range(8):
            S.op("pe", lambda e, kc=kc: e.transpose(ps.v(kc * 48, [(1, 48)]), st0[0:48, kc * 128:(kc + 1) * 128], self.idf(48)),
                 reads=[self.STG[0], self.CFt], writes=[self.PSt[0]])
        S.op("dve", lambda e: e.tensor_copy(self.V1T.v(0, [(1, 384)]), ps.v(0, [(1, 384)])), reads=[self.PSt[0]], writes=[self.V1Tt])
        st2 = self.bigf(2048, DFF)
        S.dma("sp", st2[0:24, :], I["r2"][:, :], self.STG[1], True)
        for grp in range(2):
            psg = self.PS[1 + grp]
            for jj in range(11):
                j = grp * 11 + jj
                S.op("pe", lambda e, j=j, jj=jj, psg=psg: e.transpose(psg.v(jj * 24, [(1, 24)]), st2[0:24, j * 128:(j + 1) * 128], self.idf(24)),
                     reads=[self.STG[1], self.CFt], writes=[self.PSt[1 + grp]])
            S.op("dve", lambda e, grp=grp, psg=psg: e.tensor_copy(self.V2T.v(grp * 264, [(1, 264)]), psg.v(0, [(1, 264)])),
                 reads=[self.PSt[1 + grp]], writes=[self.V2Tt])
        S.retire(self.STG, [self.BIGa])
        S.op("act", lambda e: e.activation(self.CSB.v(0, [(3, 8), (1, 3)]), self.V1T.v(16, [(48, 8), (1, 3)]), AF.Silu),
             reads=[self.V1Tt], writes=[self.CSBt])
        self.ada_layer(0)
        self.ada_next = 1
        LAMf = self.LAMB.v(0, [(1, 512)]).bitcast(F32)
        S.dma("sp", LAMf, I["diff_lambda"].partition_broadcast(128), self.LAMBt, True)
        S.dma("sp", self.GSUB.v(0, [(1, 128)]), I["diff_subln_g"].partition_broadcast(128), self.GSUBt, True)
        wf = self.WF[0]
        S.op("dve", lambda e: e.tensor_tensor(wf.v(0, [(1, 64)]), LAMf[:, 0:64], LAMf[:, 64:128], ALU.mult),
             reads=[self.LAMBt], writes=[self.WFt[0]])
        S.op("dve", lambda e: e.tensor_tensor(wf.v(64, [(1, 64)]), LAMf[:, 128:192], LAMf[:, 192:256], ALU.mult),
             reads=[self.LAMBt], writes=[self.WFt[0]])
        S.op("dve", lambda e: e.reduce_sum(self.SM.v(1, [(1, 1)]), wf.v(0, [(1, 64)]), axis=AX.X), reads=[self.WFt[0]], writes=[self.SMt])
        S.op("dve", lambda e: e.reduce_sum(self.SM.v(2, [(1, 1)]), wf.v(64, [(1, 64)]), axis=AX.X), reads=[self.WFt[0]], writes=[self.SMt])
        S.op("act", lambda e: e.activation(self.SM.v(3, [(1, 2)]), self.SM.v(1, [(1, 2)]), AF.Exp), reads=[self.SMt], writes=[self.SMt])
        S.op("dve", lambda e: e.scalar_tensor_tensor(self.SM.v(0, [(1, 1)]), self.SM.v(4, [(1, 1)]), -LAM_INIT, self.SM.v(3, [(1, 1)]),
                                                     ALU.add, ALU.subtract), reads=[self.SMt], writes=[self.SMt])
        S.op("dve", lambda e: e.tensor_scalar(self.GSUB.v(0, [(1, 128)]), self.GSUB.v(0, [(1, 128)]), 1.0 - LAM_INIT, None, ALU.mult),
             reads=[self.GSUBt], writes=[self.GSUBt])
        S.retire([self.LAMBt], self.QKBt)

    def ada_layer(self, i):
        S, I = self.S, self.i
        MOD, modt = self.MODL, self.MODLt
        psa = self.PS[3]
        psat = self.PSt[3]
        wv = self.wview(I["ada_w"][i], 8)
        for p in range(16):
            slot, st, tot = self.load_panel([(wv[:, :, p * 384:(p + 1) * 384], 384)], 8)
            for ml in range(3):
                m = p * 3 + ml
                for kc in range(8):
                    S.op("pe", lambda e, kc=kc, ml=ml, m=m, slot=slot: e.matmul(
                        psa.v(m * 3, [(1, 3)]), slot.v(kc * 384 + ml * 128, [(1, 128)]), self.CSB.v(kc * 3, [(1, 3)]),
                        start=(kc == 0), stop=(kc == 7)),
                        reads=[st, self.CSBt], writes=[psat], signal=(kc == 7))
        for kind in range(6):
            S.op("dve", lambda e, kind=kind: e.tensor_tensor(
                MOD.v(kind * 24, [(3, 8), (1, 3)]),
                psa.v(kind * 24, [(3, 8), (1, 3)]),
                self.V1T.v(22 + i * 6 + kind, [(48, 8), (0, 3)]), ALU.add),
                reads=[psat, self.V1Tt], writes=[modt])
        for b in range(3):
            def modv(kind):
                return MOD.v(kind * 24 + b, [(3, 8)])

            def scv(s_):
                return self.SC.v(((i * 3 + b) * 6 + s_) * 8, [(1, 8)])

            def ng(s_):
                return self.V1T.v(i * 4 + s_, [(48, 8)])
            S.op("dve", lambda e, modv=modv, scv=scv, ng=ng: e.scalar_tensor_tensor(scv(0), modv(1), 1.0, ng(0), ALU.add, ALU.mult),
                 reads=[modt, self.V1Tt], writes=[self.SCt])
            S.op("dve", lambda e, modv=modv, scv=scv: e.tensor_copy(scv(1), modv(0)), reads=[modt], writes=[self.SCt])
            S.op("dve", lambda e, modv=modv, scv=scv, ng=ng: e.tensor_tensor(scv(2), modv(2), ng(1), ALU.mult),
                 reads=[modt, self.V1Tt], writes=[self.SCt])
            S.op("dve", lambda e, modv=modv, scv=scv, ng=ng: e.scalar_tensor_tensor(scv(3), modv(4), 1.0, ng(2), ALU.add, ALU.mult),
                 reads=[modt, self.V1Tt], writes=[self.SCt])
            S.op("dve", lambda e, modv=modv, scv=scv: e.tensor_copy(scv(4), modv(3)), reads=[modt], writes=[self.SCt])
            S.op("dve", lambda e, modv=modv, scv=scv, ng=ng: e.tensor_tensor(scv(5), modv(5), ng(3), ALU.mult),
                 reads=[modt, self.V1Tt], writes=[self.SCt])

    def load_x(self, seq):
        S, I = self.S, self.i
        S.retire([self.BIGa], self.STG)
        src = I["xs"] if seq.sample else I["xp"][seq.idx]
        ntile = seq.T // seq.TW
        for t in range(ntile):
            k = t % 2
            stg = self.bigf(k * 2048, 1024)
            S.dma("sp", stg[0:seq.TW, :], src[t * seq.TW:(t + 1) * seq.TW, :], self.STG[k], True)
            for g in range(2):
                ps, pst = self.PS[(t % 2) * 2 + g], self.PSt[(t % 2) * 2 + g]
                for q in range(4):
                    kc = g * 4 + q
                    S.op("pe", lambda e, kc=kc, q=q, ps=ps, stg=stg: e.transpose(
                        ps.v(q * 128, [(1, seq.TW)]), stg[0:seq.TW, kc * 128:(kc + 1) * 128], self.idf(seq.TW)),
                        reads=[self.STG[k], self.CFt], writes=[pst], signal=(q == 3))
                h = (t * seq.TW) // seq.TH
                eng = "act" if g == 0 else "dve"
                dst = self.X.v(g * 4 * SEQ + t * seq.TW, [(SEQ, 4), (1, seq.TW)])
                srcp = ps.v(0, [(128, 4), (1, seq.TW)])
                if eng == "act":
                    S.op("act", lambda e, dst=dst, srcp=srcp: e.copy(dst, srcp), reads=[pst], writes=[self.Xt[h]])
                else:
                    S.op("dve", lambda e, dst=dst, srcp=srcp: e.tensor_copy(dst, srcp), reads=[pst], writes=[self.Xt[h]])
        S.retire(self.STG, [self.BIGa])

    def store_y(self, seq):
        S, O = self.S, self.o
        S.retire([self.BIGa], self.STG)
        dst = O["y_s"] if seq.sample else O["y_p"][seq.idx]
        ntile = seq.T // seq.TW
        for t in range(ntile):
            k = t % 2
            h = (t * seq.TW) // seq.TH
            stg = self.bigf(k * 2048, 1024)
            for g in range(2):
                ps, pst = self.PS[(t % 2) * 2 + g], self.PSt[(t % 2) * 2 + g]
                for q in range(4):
                    kc = g * 4 + q
                    S.op("pe", lambda e, kc=kc, q=q, ps=ps: e.transpose(
                        ps.v(q * 128, [(1, 128)], np_=seq.TW), self.Xv(kc, t * seq.TW, seq.TW), self.idf(128)),
                        reads=[self.Xt[h], self.CFt], writes=[pst], signal=(q == 3))
                if g == 0:
                    S.op("act", lambda e, ps=ps, stg=stg: e.copy(stg[0:seq.TW, 0:512], ps.v(0, [(1, 512)], np_=seq.TW)),
                         reads=[pst], writes=[self.STG[k]])
                else:
                    S.op("dve", lambda e, ps=ps, stg=stg: e.tensor_copy(stg[0:seq.TW, 512:1024], ps.v(0, [(1, 512)], np_=seq.TW)),
                         reads=[pst], writes=[self.STG[k]])
            S.dma("sp", dst[t * seq.TW:(t + 1) * seq.TW, :], stg[0:seq.TW, :], self.STG[k], False)
        S.retire(self.STG, [self.BIGa])

    def norm_mod(self, seq, i, s_gs, s_sh, c0, ncols, xh):
        S = self.S
        BW = min(512, ncols)
        xts = [self.Xt[h] for h in xh]
        hts = [self.HBt[h] for h in xh]
        for b0 in range(c0, c0 + ncols, BW):
            pss, psst = self.PS[4], self.PSt[4]
            for kc in range(8):
                sq, sqt = self.WB[kc % 2], self.WBt[kc % 2]
                S.op("act", lambda e, kc=kc, sq=sq: e.activation(sq.v(0, [(1, BW)]), self.Xv(kc, b0, BW), AF.Square),
                     reads=xts, writes=[sqt])
                S.op("pe", lambda e, kc=kc, sq=sq: e.matmul(pss.v(0, [(1, BW)]), self.cb("ones_s"), sq.v(0, [(1, BW)]),
                                                            start=(kc == 0), stop=(kc == 7)),
                     reads=[sqt, self.CBt], writes=[psst], signal=True)
            rs, rst = self.WF[0], self.WFt[0]
            S.op("act", lambda e: e.activation(rs.v(0, [(1, BW)]), pss.v(0, [(1, BW)]), AF.Ln, bias=self.eps_ap()), reads=[psst, self.SMt], writes=[rst])
            S.op("act", lambda e: e.activation(rs.v(0, [(1, BW)]), rs.v(0, [(1, BW)]), AF.Exp, scale=-0.5), reads=[rst], writes=[rst])
            for kc in range(8):
                tmp, tmpt = self.WF[1 + kc % 2], self.WFt[1 + kc % 2]
                S.op("dve", lambda e, kc=kc, tmp=tmp: e.scalar_tensor_tensor(
                    tmp.v(0, [(1, BW)]), self.Xv(kc, b0, BW), self.sc(seq, i, s_gs, kc), rs.v(0, [(1, BW)]), ALU.mult, ALU.mult),
                    reads=xts + [rst, self.SCt], writes=[tmpt])
                S.op("act", lambda e, kc=kc, tmp=tmp: e.activation(
                    self.Hv(kc, b0, BW), tmp.v(0, [(1, BW)]), AF.Identity, bias=self.sc(seq, i, s_sh, kc), scale=1.0),
                    reads=[tmpt, self.SCt], writes=hts)

    def eps_ap(self):
        return self.SM.v(8, [(1, 1)])

    def proj_res(self, seq, i, s_gg, w2d, nk, src, src_tiles, half):
        S = self.S
        BW, nb = seq.BW, seq.nb
        packed = (nk != 8)
        wv = None if packed else self.wview(w2d, nk)
        npm = 3 if nk == 8 else 1
        fts = self.HBt
        ss = [self.PS[4 + b] for b in range(nb)]
        sst = [self.PSt[4 + b] for b in range(nb)]
        mo = 0
        cnt = 0
        while mo < 8:
            n_here = min(npm, 8 - mo)
            if packed:
                slot, st, _ = self.load_panel([(w2d[mo], nk * 128)], 1)
                tot = 128
            else:
                slot, st, tot = self.load_panel([(wv[:, :, mo * 128:(mo + n_here) * 128], n_here * 128)], nk)
            for ml in range(n_here):
                for b in range(nb):
                    ps, pst = self.PS[cnt % 2], self.PSt[cnt % 2]
                    cnt += 1
                    for kc in range(nk):
                        S.op("pe", lambda e, kc=kc, ml=ml, b=b, ps=ps, slot=slot, tot=tot: e.matmul(
                            ps.v(0, [(1, BW)]), slot.v(kc * tot + ml * 128, [(1, 128)]), src(kc, b * BW, BW),
                            start=(kc == 0), stop=(kc == nk - 1)),
                            reads=[st] + src_tiles, writes=[pst], signal=(kc == nk - 1))
                    m = mo + ml
                    S.op("act", lambda e, m=m, b=b, ps=ps: e.copy(self.Fv(m, b * BW, BW), ps.v(0, [(1, BW)])),
                         reads=[pst], writes=fts)
                    sq, sqt = self.WB[cnt % 2], self.WBt[cnt % 2]
                    S.op("act", lambda e, ps=ps, sq=sq: e.activation(sq.v(0, [(1, BW)]), ps.v(0, [(1, BW)]), AF.Square),
                         reads=[pst], writes=[sqt])
                    S.op("pe", lambda e, b=b, sq=sq, m=m: e.matmul(ss[b].v(0, [(1, BW)]), self.cb("ones_s"), sq.v(0, [(1, BW)]),
                                                                  start=(m == 0), stop=(m == 7)),
                         reads=[sqt, self.CBt], writes=[sst[b]], signal=True)
            mo += n_here
        xt = [self.Xt[half]]
        for b in range(nb):
            rs, rst = self.WF[0], self.WFt[0]
            S.op("act", lambda e, b=b: e.activation(rs.v(0, [(1, BW)]), ss[b].v(0, [(1, BW)]), AF.Ln, bias=self.eps_ap()),
                 reads=[sst[b], self.SMt], writes=[rst])
            S.op("act", lambda e: e.activation(rs.v(0, [(1, BW)]), rs.v(0, [(1, BW)]), AF.Exp, scale=-0.5), reads=[rst], writes=[rst])
            for m in range(8):
                tmp, tmpt = self.WF[1 + m % 3], self.WFt[1 + m % 3]
                S.op("dve", lambda e, m=m, b=b, tmp=tmp: e.scalar_tensor_tensor(
                    tmp.v(0, [(1, BW)]), self.Fv(m, b * BW, BW), self.sc(seq, i, s_gg, m), rs.v(0, [(1, BW)]), ALU.mult, ALU.mult),
                    reads=fts + [rst, self.SCt], writes=[tmpt])
                xc = half * seq.TH + b * BW
                S.op("pool" if m in (1, 3, 5) else "dve",
                     lambda e, m=m, xc=xc, tmp=tmp: e.tensor_tensor(self.Xv(m, xc, BW), self.Xv(m, xc, BW), tmp.v(0, [(1, BW)]), ALU.add),
                     reads=[tmpt] + xt, writes=xt)

    def ffn_half(self, seq, i, half):
        S, I, O = self.S, self.i, self.o
        BW, nb, TH = seq.BW, seq.nb, seq.TH
        c0 = half * TH
        self.set_wide(True)
        self.norm_mod(seq, i, 3, 4, c0, TH, [half])
        wv = self.wview(I["ffn_w_in"][i], 8)
        big = [self.BIGa, self.BIGb]
        hts = [self.HBt[half]]
        gs, gst = self.GS, self.GSt
        cw = lambda k, j: self.V2T.v(j * 24 + i * 3 + k, [(1, 1)])
        cbias = lambda j: self.V2T.v(j * 24 + 12 + i, [(1, 1)])
        j = 0
        cnt = 0
        while j < FC:
            nj = min(3, FC - j)
            gslot, gstile, gtot = self.load_panel([(wv[:, :, j * 128:(j + nj) * 128], nj * 128)], 8)
            uslot, ustile, utot = self.load_panel([(wv[:, :, DFF + j * 128:DFF + (j + nj) * 128], nj * 128)], 8)
            for jl in range(nj):
                jj = j + jl
                for b in range(nb):
                    par = cnt % 2
                    gp, gpt = self.PS[par * 2], self.PSt[par * 2]
                    up, upt = self.PS[par * 2 + 1], self.PSt[par * 2 + 1]
                    cnt += 1
                    for kc in range(8):
                        S.op("pe", lambda e, kc=kc, jl=jl, b=b, gp=gp: e.matmul(
                            gp.v(0, [(1, BW)]), gslot.v(kc * gtot + jl * 128, [(1, 128)]), self.Hv(kc, c0 + b * BW, BW),
                            start=(kc == 0), stop=(kc == 7)), reads=[gstile] + hts, writes=[gpt], signal=(kc == 7))
                    for kc in range(8):
                        S.op("pe", lambda e, kc=kc, jl=jl, b=b, up=up: e.matmul(
                            up.v(0, [(1, BW)]), uslot.v(kc * utot + jl * 128, [(1, 128)]), self.Hv(kc, c0 + b * BW, BW),
                            start=(kc == 0), stop=(kc == 7)), reads=[ustile] + hts, writes=[upt], signal=(kc == 7))
                    acc, acct = self.WF[par * 2], self.WFt[par * 2]
                    sg, sgt = self.WF[par * 2 + 1], self.WFt[par * 2 + 1]
                    if b == 0:
                        if half == 0:
                            if seq.sample:
                                hal, halt = self.V2T.v(jj * 24 + 16 + i * 2, [(1, 2)]), self.V2Tt
                            else:
                                hal, halt = self.HAL.v(4, [(1, 2)]), self.HALt[2]
                        else:
                            hal, halt = self.GCAR.v(jj * 2, [(1, 2)]), self.GCARt
                    else:
                        hal, halt = self.HAL.v(((cnt - 2) % 2) * 2, [(1, 2)]), self.HALt[(cnt - 2) % 2]
                    S.op("act", lambda e, gp=gp, jj=jj, acc=acc: e.activation(acc.v(0, [(1, BW)]), gp.v(0, [(1, BW)]), AF.Identity,
                                                                               bias=cbias(jj), scale=cw(2, jj)),
                         reads=[gpt, self.V2Tt], writes=[acct])
                    if b == nb - 1:
                        S.op("act", lambda e, gp=gp, jj=jj: e.copy(self.GCAR.v(jj * 2, [(1, 2)]), gp.v(BW - 2, [(1, 2)])),
                             reads=[gpt], writes=[self.GCARt])
                    else:
                        S.op("act", lambda e, gp=gp, par=par: e.copy(self.HAL.v(par * 2, [(1, 2)]), gp.v(BW - 2, [(1, 2)])),
                             reads=[gpt], writes=[self.HALt[par]])
                    S.op("dve", lambda e, gp=gp, jj=jj, acc=acc: e.scalar_tensor_tensor(
                        acc.v(1, [(1, BW - 1)]), gp.v(0, [(1, BW - 1)]), cw(1, jj), acc.v(1, [(1, BW - 1)]), ALU.mult, ALU.add),
                        reads=[gpt, acct, self.V2Tt], writes=[acct])
                    S.op("dve", lambda e, gp=gp, jj=jj, acc=acc: e.scalar_tensor_tensor(
                        acc.v(2, [(1, BW - 2)]), gp.v(0, [(1, BW - 2)]), cw(0, jj), acc.v(2, [(1, BW - 2)]), ALU.mult, ALU.add),
                        reads=[gpt, acct, self.V2Tt], writes=[acct])
                    S.op("dve", lambda e, hal=hal, jj=jj, acc=acc: e.scalar_tensor_tensor(
                        acc.v(0, [(1, 2)]), hal, cw(0, jj), acc.v(0, [(1, 2)]), ALU.mult, ALU.add),
                        reads=[halt, acct, self.V2Tt], writes=[acct])
                    S.op("dve", lambda e, hal=hal, jj=jj, acc=acc: e.scalar_tensor_tensor(
                        acc.v(0, [(1, 1)]), hal[:, 1:2], cw(1, jj), acc.v(0, [(1, 1)]), ALU.mult, ALU.add),
                        reads=[halt, acct, self.V2Tt], writes=[acct])
                    S.op("act", lambda e, acc=acc, sg=sg: e.activation(sg.v(0, [(1, BW)]), acc.v(0, [(1, BW)]), AF.Silu), reads=[acct], writes=[sgt])
                    S.op("dve", lambda e, b=b, jj=jj, up=up, sg=sg: e.tensor_tensor(self.Av(jj, b * BW, BW), sg.v(0, [(1, BW)]), up.v(0, [(1, BW)]), ALU.mult),
                         reads=[sgt, upt], writes=big)
            j += nj
        if half == seq.nh - 1:
            self.emit_state_rows(self.GCAR, self.GCARt, FC,
                                 (O["ffc_s"][i] if seq.sample else O["ffc_p"][i, seq.idx]))
        self.proj_res(seq, i, 5, I["ffn_w_out"][i], FC, lambda kc, cc, n: self.Av(kc, cc, n), big, half)
        self.set_wide(False)

    def emit_state_rows(self, car, cart, nchunk, dst):
        S = self.S
        for g0 in range(0, nchunk, 4):
            ng = min(4, nchunk - g0)
            ps, pst = self.PS[6], self.PSt[6]
            for q in range(ng):
                S.op("pe", lambda e, q=q, g0=g0: e.transpose(ps.v(q * 128, [(1, 128)], np_=2), car.v((g0 + q) * 2, [(1, 2)]), self.idf(128)),
                     reads=[cart, self.CFt], writes=[pst], signal=(q == ng - 1))
            ost = self.WF[3]
            S.op("act", lambda e, ng=ng: e.copy(ost.v(0, [(1, ng * 128)], np_=2), ps.v(0, [(1, ng * 128)], np_=2)),
                 reads=[pst], writes=[self.WFt[3], self.OST])
            S.dma("sp", dst[:, g0 * 128:(g0 + ng) * 128], ost.v(0, [(1, ng * 128)], np_=2), self.OST, False, extra_reads=[self.WFt[3]])

    def gelu(self, dst, ps_ap, pst, n, np_, dst_tiles, wa, wb):
        S = self.S
        a, at = self.WF[wa], self.WFt[wa]
        b, bt = self.WF[wb], self.WFt[wb]
        S.op("act", lambda e: e.activation(a.v(0, [(1, n)], np_=np_), ps_ap, AF.Square), reads=[pst], writes=[at])
        S.op("dve", lambda e: e.tensor_scalar(a.v(0, [(1, n)], np_=np_), a.v(0, [(1, n)], np_=np_), 0.044715, 1.0, ALU.mult, ALU.add),
             reads=[at], writes=[at])
        S.op("dve", lambda e: e.tensor_tensor(a.v(0, [(1, n)], np_=np_), a.v(0, [(1, n)], np_=np_), ps_ap, ALU.mult), reads=[at, pst], writes=[at])
        S.op("act", lambda e: e.activation(b.v(0, [(1, n)], np_=np_), a.v(0, [(1, n)], np_=np_), AF.Sigmoid, scale=1.5957691216057308),
             reads=[at], writes=[bt])
        S.op("dve", lambda e: e.tensor_tensor(dst, b.v(0, [(1, n)], np_=np_), ps_ap, ALU.mult), reads=[bt, pst], writes=dst_tiles)

    def gm_consts(self):
        S, I = self.S, self.i
        LNG = self.attf(0, 1024)
        LNB = self.attf(2048, 1024)
        BSB = self.attf(4224, 1024)
        S.dma("sp", LNG, I["gm_ln_g"].partition_broadcast(128), self.ATTq, True)
        S.dma("sp", LNB, I["gm_ln_b"].partition_broadcast(128), self.ATTk, True)
        S.dma("sp", BSB, I["gm_bs"].partition_broadcast(128), self.ATTv, True)
        S.retire([self.BIGa], [self.STG[0]])
        stg = self.bigf(0, 1024)
        S.dma("sp", stg.rearrange("p (g s) -> p g s", g=8), I["gm_ws"].rearrange("g t s -> t g s"), self.STG[0], True)
        ps, pst = self.PS[0], self.PSt[0]
        ps2, pst2 = self.PS[1], self.PSt[1]
        for g in range(8):
            pp, ppt = (ps, pst) if g < 4 else (ps2, pst2)
            S.op("pe", lambda e, g=g, pp=pp: e.transpose(pp.v((g % 4) * 128, [(1, 128)]), stg[:, g * 128:(g + 1) * 128], self.idf(128)),
                 reads=[self.STG[0], self.CFt], writes=[ppt], signal=(g % 4 == 3))
        m, mt = self.WF[0], self.WFt[0]
        S.op("dve", lambda e: e.tensor_scalar(m.v(0, [(1, 128)]), self.CF.v(128, [(1, 128)]), -1.0, 1.0, ALU.mult, ALU.add),
             reads=[self.CFt], writes=[mt])
        for hgrp in range(2):
            pp, ppt = (ps, pst) if hgrp == 0 else (ps2, pst2)
            S.op("dve", lambda e, hgrp=hgrp, pp=pp: e.tensor_tensor(
                self.ATT.v(6272 + hgrp * 512, [(128, 4), (1, 128)]), pp.v(0, [(128, 4), (1, 128)]), m.v(0, [(0, 4), (1, 128)]), ALU.mult),
                reads=[ppt, mt], writes=[self.ATTv])
        S.retire([self.STG[0]], [self.BIGa])

    def gmlp_half(self, seq, i, half):
        S, I, O = self.S, self.i, self.o
        TH, BW, nb, TW, nt = seq.TH, seq.BW, seq.nb, seq.TW, seq.nt
        c0 = half * TH
        self.norm_mod(seq, i, 0, 1, c0, TH, [half])
        self.cp("gm_norm")
        hts = [self.HBt[half]]
        wv = self.wview(I["gm_w_in"], 8)
        LNG = self.attf(0, 1024)
        LNB = self.attf(2048, 1024)
        BSB = self.attf(4224, 1024)
        big = [self.BIGa]
        panels = []
        for (cs, n) in ((1024, 384), (1408, 384), (1792, 256)):
            panels.append(self.load_panel([(wv[:, :, cs:cs + n], n)], 8) + (cs - 1024,))
        vf, vft = self.GS, self.GSt
        for t in range(nt):
            for pi, (slot, st, tot, co) in enumerate(panels):
                ps, pst = self.PS[pi], self.PSt[pi]
                for kc in range(8):
                    S.op("pe", lambda e, kc=kc, t=t, ps=ps, slot=slot, tot=tot: e.matmul(
                        ps.v(0, [(1, tot)], np_=TW), self.Hv(kc, c0 + t * TW, TW), slot.v(kc * tot, [(1, tot)]),
                        start=(kc == 0), stop=(kc == 7)), reads=[st] + hts, writes=[pst], signal=(kc == 7))
                self.gelu(vf.v(co, [(1, tot)], np_=TW), ps.v(0, [(1, tot)], np_=TW), pst, tot, TW, [vft], (pi % 2) * 2, (pi % 2) * 2 + 1)
            sm = self.SM
            S.op("dve", lambda e: e.reduce_sum(sm.v(16, [(1, 1)], np_=TW), vf.v(0, [(1, 1024)], np_=TW), axis=AX.X), reads=[vft], writes=[self.SMt])
            S.op("dve", lambda e: e.tensor_scalar(sm.v(17, [(1, 1)], np_=TW), sm.v(16, [(1, 1)], np_=TW), 1.0 / 1024.0, None, ALU.mult),
                 reads=[self.SMt], writes=[self.SMt])
            S.op("dve", lambda e: e.tensor_scalar(vf.v(0, [(1, 1024)], np_=TW), vf.v(0, [(1, 1024)], np_=TW), sm.v(17, [(1, 1)], np_=TW), None, ALU.subtract),
                 reads=[vft, self.SMt], writes=[vft])
            junk, junkt = self.WB[0], self.WBt[0]
            for hh in range(2):
                S.op("act", lambda e, hh=hh: e.activation(junk.v(0, [(1, 512)], np_=TW), vf.v(hh * 512, [(1, 512)], np_=TW), AF.Square,
                                                            accum_out=sm.v(18 + hh, [(1, 1)], np_=TW)),
                     reads=[vft], writes=[junkt, self.SMt])
            S.op("dve", lambda e: e.tensor_tensor(sm.v(20, [(1, 1)], np_=TW), sm.v(18, [(1, 1)], np_=TW), sm.v(19, [(1, 1)], np_=TW), ALU.add),
                 reads=[self.SMt], writes=[self.SMt])
            S.op("act", lambda e: e.activation(sm.v(21, [(1, 1)], np_=TW), sm.v(20, [(1, 1)], np_=TW), AF.Ln, bias=self.eps_ap()[0:TW, :], scale=1.0 / 1024.0),
                 reads=[self.SMt], writes=[self.SMt])
            S.op("act", lambda e: e.activation(sm.v(22, [(1, 1)], np_=TW), sm.v(21, [(1, 1)], np_=TW), AF.Exp, scale=-0.5),
                 reads=[self.SMt], writes=[self.SMt])
            S.op("dve", lambda e: e.scalar_tensor_tensor(vf.v(0, [(1, 1024)], np_=TW), vf.v(0, [(1, 1024)], np_=TW), sm.v(22, [(1, 1)], np_=TW),
                                                         LNG[0:TW, :], ALU.mult, ALU.mult),
                 reads=[vft, self.SMt, self.ATTq], writes=[vft])
            if seq.sample:
                S.op("dve", lambda e: e.tensor_tensor(vf.v(0, [(1, 1024)], np_=TW), vf.v(0, [(1, 1024)], np_=TW), LNB[0:TW, :], ALU.add),
                     reads=[vft, self.ATTk], writes=[vft])
                od = S.dtile("gmv_out")
                S.dma("sp", O["gmv_s"][:, :], vf.v(0, [(1, 1024)], np_=TW), od, False, extra_reads=[vft])
                S.op("act", lambda e, t=t: e.copy(self.BIG.v(8192 + t * 1024, [(1, 1024)], np_=TW), vf.v(0, [(1, 1024)], np_=TW)),
                     reads=[vft], writes=big)
            else:
                S.op("dve", lambda e, t=t: e.tensor_tensor(self.BIG.v(8192 + t * 1024, [(1, 1024)], np_=TW), vf.v(0, [(1, 1024)], np_=TW), LNB[0:TW, :], ALU.add),
                     reads=[vft, self.ATTk], writes=big)
        self.cp("gm_v")
        upan = {}
        cnt = 0
        for g in range(8):
            pidx = g // 3
            if pidx not in upan:
                n = min(384, 1024 - pidx * 384)
                upan[pidx] = self.load_panel([(wv[:, :, pidx * 384:pidx * 384 + n], n)], 8)
            slot, st, tot = upan[pidx]
            gl = g % 3
            for b in range(nb):
                ups, upst = self.PS[(cnt % 2) * 2], self.PSt[(cnt % 2) * 2]
                mps, mpst = self.PS[(cnt % 2) * 2 + 1], self.PSt[(cnt % 2) * 2 + 1]
                cnt += 1
                for kc in range(8):
                    S.op("pe", lambda e, kc=kc, gl=gl, b=b, ups=ups, slot=slot, tot=tot: e.matmul(
                        ups.v(0, [(1, BW)]), slot.v(kc * tot + gl * 128, [(1, 128)]), self.Hv(kc, c0 + b * BW, BW),
                        start=(kc == 0), stop=(kc == 7)), reads=[st] + hts, writes=[upst], signal=(kc == 7))
                ntb = BW // TW
                for tt in range(ntb):
                    t = b * ntb + tt
                    S.op("pe", lambda e, tt=tt, t=t, g=g, mps=mps: e.matmul(
                        mps.v(tt * TW, [(1, TW)]), self.BIG.v(8192 + t * 1024 + g * 128, [(1, 128)], np_=TW),
                        self.ATT.v(6272 + g * 128, [(1, TW)], np_=TW), start=True, stop=True),
                        reads=big + [self.ATTv], writes=[mpst], signal=(tt == ntb - 1))
                if cnt % 2 == 0:
                    ug, ugt = self.WF[2], self.WFt[2]
                    mx, mxt = self.WF[3], self.WFt[3]
                else:
                    ug, ugt = SubBuf(self.GS, 0), self.GSt
                    mx, mxt = SubBuf(self.GS, 512), self.GSt
                self.gelu(ug.v(0, [(1, BW)]), ups.v(0, [(1, BW)]), upst, BW, 128, [ugt], 0, 1)
                S.op("dve", lambda e, g=g, mps=mps, ntb=ntb: e.tensor_tensor(
                    mx.v(0, [(TW, ntb), (1, TW)]), mps.v(0, [(TW, ntb), (1, TW)]), self._bsb(g, ntb, TW), ALU.add),
                    reads=[mpst, self.ATTv], writes=[mxt])
                S.op("dve", lambda e, g=g, b=b: e.tensor_tensor(self.BIG.v(g * 1024 + b * BW, [(1, BW)]), mx.v(0, [(1, BW)]), ug.v(0, [(1, BW)]), ALU.mult),
                     reads=[mxt, ugt], writes=big)
        self.cp("gm_u")
        self.proj_res(seq, i, 2, I["gm_w_out"], 8, lambda kc, cc, n: self.BIG.v(kc * 1024 + cc, [(1, n)]), big, half)
        self.cp("gm_proj")

    def _bsb(self, g, ntb, TW):
        return self.ATT.v(4224 + 2 * g * 128, [(0, ntb), (1, 2 * TW)]).bitcast(F32)

    def sconv_half(self, seq, i, half):
        S, I, O = self.S, self.i, self.o
        TH, BW, nb = seq.TH, seq.BW, seq.nb
        c0 = half * TH
        self.set_wide(True)
        self.norm_mod(seq, i, 0, 1, c0, TH, [half])
        hts = [self.HBt[half]]
        wv = self.wview(I["sc_w_in"], 8)
        big = [self.BIGa]
        gs, gst = self.GS, self.GSt
        cw = lambda k, m: self.V1T.v(m * 48 + 19 + k, [(1, 1)])
        cnt = 0
        grp = {}
        for m in range(8):
            if m % 3 == 0:
                ng = min(3, 8 - m)
                grp = [self.load_panel([(wv[:, :, q * D + m * 128:q * D + (m + ng) * 128], ng * 128)], 8) for q in range(3)]
            ml = m % 3
            if half == 0:
                if seq.sample:
                    S.op("pool", lambda e, m=m: e.tensor_copy(gs.v(0, [(1, 2)]), self.V1T.v(m * 48 + 46, [(1, 2)])), reads=[self.V1Tt], writes=[gst])
                else:
                    S.op("pool", lambda e: e.memset(gs.v(0, [(1, 2)]), 0.0), writes=[gst])
            else:
                S.op("pool", lambda e, m=m: e.tensor_copy(gs.v(0, [(1, 2)]), self.PCAR.v(m * 2, [(1, 2)])), reads=[self.PCARt], writes=[gst])
            for b in range(nb):
                base = (cnt % 2) * 3
                cnt += 1
                pp = [self.PS[base + q] for q in range(3)]
                ppt = [self.PSt[base + q] for q in range(3)]
                for q in range(3):
                    slot, st, tot = grp[q]
                    for kc in range(8):
                        S.op("pe", lambda e, kc=kc, q=q, b=b, pp=pp, slot=slot, tot=tot, ml=ml: e.matmul(
                            pp[q].v(0, [(1, BW)]), slot.v(kc * tot + ml * 128, [(1, 128)]), self.Hv(kc, c0 + b * BW, BW),
                            start=(kc == 0), stop=(kc == 7)), reads=[st] + hts, writes=[ppt[q]], signal=(kc == 7))
                xs_, xst = self.WF[0], self.WFt[0]
                S.op("act", lambda e, pp=pp: e.copy(xs_.v(0, [(1, BW)]), pp[2].v(0, [(1, BW)])), reads=[ppt[2]], writes=[xst])
                S.op("dve", lambda e, b=b, pp=pp: e.tensor_tensor(gs.v(2 + b * BW, [(1, BW)]), pp[1].v(0, [(1, BW)]), xs_.v(0, [(1, BW)]), ALU.mult),
                     reads=[ppt[1], xst], writes=[gst])
                yy, yyt = self.WF[1], self.WFt[1]
                S.op("act", lambda e, b=b, m=m: e.activation(yy.v(0, [(1, BW)]), gs.v(2 + b * BW, [(1, BW)]), AF.Identity, scale=cw(2, m)),
                     reads=[gst, self.V1Tt], writes=[yyt])
                S.op("dve", lambda e, b=b, m=m: e.scalar_tensor_tensor(yy.v(0, [(1, BW)]), gs.v(1 + b * BW, [(1, BW)]), cw(1, m), yy.v(0, [(1, BW)]), ALU.mult, ALU.add),
                     reads=[gst, yyt, self.V1Tt], writes=[yyt])
                S.op("dve", lambda e, b=b, m=m: e.scalar_tensor_tensor(yy.v(0, [(1, BW)]), gs.v(b * BW, [(1, BW)]), cw(0, m), yy.v(0, [(1, BW)]), ALU.mult, ALU.add),
                     reads=[gst, yyt, self.V1Tt], writes=[yyt])
                S.op("dve", lambda e, b=b, m=m, pp=pp: e.tensor_tensor(self.BIG.v(m * 1024 + b * BW, [(1, BW)]), yy.v(0, [(1, BW)]), pp[0].v(0, [(1, BW)]), ALU.mult),
                     reads=[yyt, ppt[0]], writes=big)
            S.op("pool", lambda e, m=m: e.tensor_copy(self.PCAR.v(m * 2, [(1, 2)]), gs.v(TH, [(1, 2)])), reads=[gst], writes=[self.PCARt])
        if half == seq.nh - 1:
            self.emit_state_rows(self.PCAR, self.PCARt, 8, (O["sc_s"] if seq.sample else O["sc_p"][seq.idx]))
        self.proj_res(seq, i, 2, I["sc_w_out"], 8, lambda kc, cc, n: self.BIG.v(kc * 1024 + cc, [(1, n)]), big, half)
        self.set_wide(False)

    def attn_layer(self, seq, i, kind):
        S, I, O = self.S, self.i, self.o
        T, TW = seq.T, seq.TW
        self.norm_mod(seq, i, 0, 1, 0, T, list(range(seq.nh)))
        hts = self.HBt[:seq.nh]
        wqkv = I["diff_w_qkv"] if kind == "diff" else I["sb_w_qkv"]
        if kind == "diff":
            ok = O["dk_s"] if seq.sample else O["dk_p"][seq.idx]
            ov = O["dv_s"] if seq.sample else O["dv_p"][seq.idx]
            ck, cv = I["cdk"], I["cdv"]
        else:
            ok = O["sbk_s"] if seq.sample else O["sbk_p"][seq.idx]
            ov = O["sbv_s"] if seq.sample else O["sbv_p"][seq.idx]
            ck, cv = I["csk"], I["csv"]
        ntile = T // TW
        nkt_cache = 16 if seq.sample else 0
        S.retire([self.BIGa, self.BIGb], [self.KSTt, self.VSTt])
        QT0, KT0, VB0 = 0, 2048, 4224
        vw = 129 if kind == "diff" else 256
        big = [self.BIGa]
        if kind == "sb":
            S.op("pool", lambda e: e.memset(self.ATT.v(VB0, [(1, 17 * 256)]), 0.0), writes=[self.ATTv])
        else:
            S.op("pool", lambda e: e.memset(self.ATT.v(VB0, [(1, 17 * 129)]), 1.0), writes=[self.ATTv])
        def qkv_panel(hh):
            slot_, st_, _ = self.load_panel([(wqkv[hh], 3 * D)], 1)
            return slot_, st_, 384
        nxt = qkv_panel(0)
        for h in range(8):
            slot, st, tot = nxt
            if seq.sample:
                kst = self.bigf(16384, 2048).rearrange("p (t c) -> p t c", t=16)
                S.dma("sp", kst, ck.rearrange("(t p) c -> p t c", p=128)[:, :, h * 128:(h + 1) * 128], self.KSTt, True, extra_writes=[self.VSTt])
                for t in range(16):
                    ps, pst = self.PS[t % 2], self.PSt[t % 2]
                    S.op("pe", lambda e, t=t, ps=ps: e.transpose(ps.v(0, [(1, 128)]), kst[:, t, :], self.idf(128)),
                         reads=[self.KSTt, self.CFt], writes=[pst])
                    S.op("act" if t % 2 == 0 else "dve",
                         (lambda e, t=t, ps=ps: e.copy(self.ATT.v(KT0 + t * 128, [(1, 128)]), ps.v(0, [(1, 128)]))) if t % 2 == 0 else
                         (lambda e, t=t, ps=ps: e.tensor_copy(self.ATT.v(KT0 + t * 128, [(1, 128)]), ps.v(0, [(1, 128)]))),
                         reads=[pst], writes=[self.ATTk])
                cvv = cv.rearrange("(t p) c -> p t c", p=128)
                if kind == "diff":
                    S.dma("pool", self.ATT.v(VB0, [(129, 16), (1, 128)]), cvv[:, :, h * 128:(h + 1) * 128], self.ATTv, True)
                else:
                    S.dma("pool", self.ATT.v(VB0, [(256, 16), (1, 64)]), cvv[:, :, h * 128:h * 128 + 64], self.ATTv, True)
                    S.dma("pool", self.ATT.v(VB0 + 128 + 64, [(256, 16), (1, 64)]), cvv[:, :, h * 128 + 64:(h + 1) * 128], self.ATTv, True)
            def qkv_mm(t):
                ps, pst = self.PS[t % 2], self.PSt[t % 2]
                for kc in range(8):
                    S.op("pe", lambda e, kc=kc, t=t, ps=ps, slot=slot, tot=tot: e.matmul(
                        ps.v(0, [(1, 384)], np_=TW), self.Hv(kc, t * TW, TW), slot.v(kc * tot, [(1, 384)]),
                        start=(kc == 0), stop=(kc == 7)), reads=[st] + hts, writes=[pst], signal=(kc == 7))

            def qkv_post(t):
                ps, pst = self.PS[t % 2], self.PSt[t % 2]
                qk, qkt = self.QKVs[t % 2], self.QKVts[t % 2]
                S.op("act", lambda e, ps=ps: e.copy(qk.v(0, [(1, 384)], np_=TW), ps.v(0, [(1, 384)], np_=TW)), reads=[pst], writes=[qkt])
                if kind == "diff":
                    rt = nkt_cache + t if seq.sample else t
                    cosv = self.ROPE.v(rt * 16, [(0, 4), (1, 8)], np_=TW)
                    sinv = self.ROPE.v(rt * 16 + 8, [(0, 4), (1, 8)], np_=TW)
                    x1 = qk.v(0, [(64, 4), (1, 8)], np_=TW)
                    x2 = qk.v(8, [(64, 4), (1, 8)], np_=TW)
                    tm, tmt = self.WF[t % 2], self.WFt[t % 2]
                    t1 = tm.v(0, [(8, 4), (1, 8)], np_=TW)
                    t2 = tm.v(32, [(8, 4), (1, 8)], np_=TW)
                    t3 = tm.v(64, [(8, 4), (1, 8)], np_=TW)
                    t4 = tm.v(96, [(8, 4), (1, 8)], np_=TW)
                    S.op("dve", lambda e: e.tensor_tensor(t1, x1, cosv, ALU.mult), reads=[qkt, self.ROPEt], writes=[tmt])
                    S.op("dve", lambda e: e.tensor_tensor(t2, x2, sinv, ALU.mult), reads=[qkt, self.ROPEt], writes=[tmt])
                    S.op("dve", lambda e: e.tensor_tensor(t3, x2, cosv, ALU.mult), reads=[qkt, self.ROPEt], writes=[tmt])
                    S.op("dve", lambda e: e.tensor_tensor(t4, x1, sinv, ALU.mult), reads=[qkt, self.ROPEt], writes=[tmt])
                    S.op("dve", lambda e: e.tensor_tensor(x1, t1, t2, ALU.subtract), reads=[tmt], writes=[qkt])
                    S.op("dve", lambda e: e.tensor_tensor(x2, t3, t4, ALU.add), reads=[tmt], writes=[qkt])
                if seq.sample:
                    osb, osbt = self.WF[3], self.WFt[3]
                    S.op("act", lambda e: e.copy(osb.v(0, [(1, 256)], np_=TW), qk.v(128, [(1, 256)], np_=TW)), reads=[qkt], writes=[osbt, self.OST])
                    S.dma("sp", ok[:, h * 128:(h + 1) * 128], osb.v(0, [(1, 128)], np_=TW), self.OST, False, extra_reads=[osbt])
                    S.dma("sp", ov[:, h * 128:(h + 1) * 128], osb.v(128, [(1, 128)], np_=TW), self.OST, False, extra_reads=[osbt])
                else:
                    tl = t % 8
                    kstv = self.bigf(16384, 1024)
                    vstv = self.bigf(18432, 1024)
                    S.op("act", lambda e, tl=tl: e.copy(kstv[:, tl * 128:(tl + 1) * 128], qk.v(128, [(1, 128)])), reads=[qkt], writes=[self.KSTt])
                    S.op("pool", lambda e, tl=tl: e.tensor_copy(vstv[:, tl * 128:(tl + 1) * 128], qk.v(256, [(1, 128)])), reads=[qkt], writes=[self.VSTt])
                    if tl == 7:
                        r0 = (t - 7) * 128
                        S.dma("sp", ok[r0:r0 + 1024, :].rearrange("(t p) c -> p t c", p=128)[:, :, h * 128:(h + 1) * 128],
                              kstv.rearrange("p (t c) -> p t c", t=8), self.KSTt, False)
                        S.dma("sp", ov[r0:r0 + 1024, :].rearrange("(t p) c -> p t c", p=128)[:, :, h * 128:(h + 1) * 128],
                              vstv.rearrange("p (t c) -> p t c", t=8), self.VSTt, False)
                kt_idx = nkt_cache + t
                qb16, qb16t = self.QKB[t % 2], self.QKBt[t % 2]
                S.op("dve", lambda e: e.tensor_copy(qb16.v(0, [(1, 256)], np_=TW), qk.v(0, [(1, 256)], np_=TW)), reads=[qkt], writes=[qb16t])
                for which, off, dst0, dtile in ((0, 0, QT0 + t * TW, self.ATTq), (1, 128, KT0 + kt_idx * 128, self.ATTk)):
                    ps2, pst2 = self.PS[2 + which + 2 * (t % 2)], self.PSt[2 + which + 2 * (t % 2)]
                    pv16 = ps2.v(0, [(1, (TW + 1) // 2)]).bitcast(BF16)[:, 0:TW]
                    S.op("pe", lambda e, off=off, pv16=pv16: e.transpose(pv16, qb16.v(off, [(1, 128)], np_=TW), self.cb("ident", k=TW, m=TW)),
                         reads=[qb16t, self.CBt], writes=[pst2])
                    if which == 0:
                        S.op("act", lambda e, dst0=dst0, pv16=pv16: e.copy(self.ATT.v(dst0, [(1, TW)]), pv16), reads=[pst2], writes=[dtile])
                    else:
                        S.op("dve", lambda e, dst0=dst0, pv16=pv16: e.tensor_copy(self.ATT.v(dst0, [(1, TW)]), pv16), reads=[pst2], writes=[dtile])
                if kind == "diff":
                    S.op("pool", lambda e, kt_idx=kt_idx: e.tensor_copy(self.ATT.v(VB0 + kt_idx * 129, [(1, 128)], np_=TW), qk.v(256, [(1, 128)], np_=TW)),
                         reads=[qkt], writes=[self.ATTv])
                else:
                    S.op("pool", lambda e, kt_idx=kt_idx: e.tensor_copy(self.ATT.v(VB0 + kt_idx * 256, [(1, 64)], np_=TW), qk.v(256, [(1, 64)], np_=TW)),
                         reads=[qkt], writes=[self.ATTv])
                    S.op("pool", lambda e, kt_idx=kt_idx: e.tensor_copy(self.ATT.v(VB0 + kt_idx * 256 + 192, [(1, 64)], np_=TW), qk.v(320, [(1, 64)], np_=TW)),
                         reads=[qkt], writes=[self.ATTv])

            qkv_mm(0)
            for t in range(ntile):
                if t + 1 < ntile:
                    qkv_mm(t + 1)
                qkv_post(t)
            if h + 1 < 8:
                nxt = qkv_panel(h + 1)
            if kind == "diff":
                self.diff_attend(seq, h, QT0, KT0, VB0, nkt_cache)
            else:
                self.sb_attend(seq, h, QT0, KT0, VB0, nkt_cache)
        S.retire([self.KSTt, self.VSTt], [self.BIGb])
        wout = I["diff_w_out"] if kind == "diff" else I["sb_w_out"]
        for half in range(seq.nh):
            hc = half * seq.TH
            self.proj_res(seq, i, 2, wout, 8, lambda kc, cc, n, hc=hc: self.Ov(kc, hc + cc, n), big, half)

    def diff_attend(self, seq, h, QT0, KT0, VB0, nkc):
        S = self.S
        T, TW = seq.T, seq.TW
        QB = min(512, T)
        nqb = T // QB
        nqt = QB // TW
        big = [self.BIGa]
        scale = 0.125
        ops_ = [self.PS[5], self.PS[6], self.PS[7]]
        opst = [self.PSt[5], self.PSt[6], self.PSt[7]]

        def oacc(c, qt):
            a = c * nqt + qt
            return ops_[a // 3].v((a % 3) * 160, [(1, 129)], np_=TW), opst[a // 3]

        pending = []

        def flush_pending():
            while pending:
                pending.pop(0)()

        for qb in range(nqb):
            if seq.sample:
                ktiles = [(kt, 128) for kt in range(16)] + [(16, 16)]
            else:
                ktiles = [(kt, 128) for kt in range(qb * 4 + 4)]
            nbank = (2 * nqt + 2) // 3
            for rep in range(self.warm if not seq.sample else 1):
                for k in range(nbank):
                    S.op("pe", lambda e, k=k: e.matmul(ops_[k].v(0, [(1, 512)]), self.cb("zero"), self.CB.v(0, [(1, 512)]), start=True, stop=False, skip_group_check=True),
                         reads=[self.CBt], writes=[opst[k]], signal=False)
            units = []
            for ki, (kt, kr) in enumerate(ktiles):
                for c in range(2):
                    units.append((ki, kt, kr, c))

            def geom(kt):
                j = kt - qb * 4 if not seq.sample else -1
                cstart = j * 128 if j >= 0 else 0
                return j, cstart, QB - cstart

            def s_mm(u):
                ki, kt, kr, c = u
                j, cstart, ncol = geom(kt)
                sp, spt = self.PS[c * 2 + (ki % 2)], self.PSt[c * 2 + (ki % 2)]
                S.op("pe", lambda e: e.matmul(
                    sp.v(cstart, [(1, ncol)], np_=kr), self.ATT.v(KT0 + kt * 128, [(1, kr)], p0=c * 64, np_=64),
                    self.ATT.v(QT0 + qb * QB + cstart, [(1, ncol)], p0=c * 64, np_=64), start=True, stop=True),
                    reads=[self.ATTq, self.ATTk], writes=[spt])

            def ex(u):
                ki, kt, kr, c = u
                j, cstart, ncol = geom(kt)
                sp, spt = self.PS[c * 2 + (ki % 2)], self.PSt[c * 2 + (ki % 2)]
                pb, pbt = self.WB[c * 2 + (ki % 2)], self.WBt[c * 2 + (ki % 2)]
                S.op("act", lambda e: e.activation(
                    pb.v(cstart, [(1, ncol)], np_=kr), sp.v(cstart, [(1, ncol)], np_=kr), AF.Exp, scale=scale),
                    reads=[spt], writes=[pbt])
                if j >= 0:
                    S.op("pool", lambda e: e.memset(pb.v(cstart, [(1, 64)], p0=64, np_=64), 0.0), writes=[pbt])

            def pv(u):
                ki, kt, kr, c = u
                j, cstart, ncol = geom(kt)
                pb, pbt = self.WB[c * 2 + (ki % 2)], self.WBt[c * 2 + (ki % 2)]
                for qt in range(nqt):
                    if j >= 0 and qt < j:
                        continue
                    gqt = qb * nqt + qt
                    last_kt = (16 if seq.sample else gqt)
                    oap, oat = oacc(c, qt)
                    S.op("pe", lambda e, oap=oap, qt=qt, last_kt=last_kt: e.matmul(
                        oap, pb.v(qt * TW, [(1, TW)], np_=kr), self.ATT.v(VB0 + kt * 129, [(1, 129)], np_=kr),
                        start=False, stop=(kt == last_kt), skip_group_check=True),
                        reads=[pbt, self.ATTv], writes=[oat], signal=(kt == last_kt))

            nu_ = len(units)
            s_mm(units[0])
            s_mm(units[1])
            for idx in range(nu_ + 1):
                if idx % 2 == 1 and idx + 1 < nu_:
                    s_mm(units[idx + 1])
                    s_mm(units[idx + 2])
                if idx >= 1:
                    pv(units[idx - 1])
                if idx < nu_:
                    ex(units[idx])
                if idx == 6:
                    flush_pending()
            flush_pending()
            nbank = (2 * nqt + 2) // 3
            for k in range(nbank):
                S.op("act", lambda e, k=k: e.copy(self.WF[1 + k].v(0, [(1, 480)], np_=TW), ops_[k].v(0, [(1, 480)], np_=TW)),
                     reads=[opst[k]], writes=[self.WFt[1 + k]])

            def sacc(c, qt):
                a_ = c * nqt + qt
                return self.WF[1 + a_ // 3].v((a_ % 3) * 160, [(1, 129)], np_=TW), self.WFt[1 + a_ // 3]
            for qt in range(nqt):
                o0, o0t = sacc(0, qt)
                o1, o1t = sacc(1, qt)
                sm = self.SM
                av = o0[:, 0:128]
                at = o0t
                S.op("dve", lambda e, o0=o0: e.reciprocal(sm.v(24, [(1, 1)], np_=TW), o0[:, 128:129]), reads=[o0t], writes=[self.SMt])
                S.op("dve", lambda e, o1=o1: e.reciprocal(sm.v(25, [(1, 1)], np_=TW), o1[:, 128:129]), reads=[o1t], writes=[self.SMt])
                S.op("dve", lambda e: e.tensor_tensor(sm.v(26, [(1, 1)], np_=TW), sm.v(25, [(1, 1)], np_=TW), sm.v(0, [(1, 1)], np_=TW), ALU.mult),
                     reads=[self.SMt], writes=[self.SMt])
                S.op("dve", lambda e, av=av: e.tensor_scalar(av, av, sm.v(24, [(1, 1)], np_=TW), None, ALU.mult),
                     reads=[o0t, self.SMt], writes=[at])
                S.op("dve", lambda e, o1=o1, av=av: e.scalar_tensor_tensor(av, o1[:, 0:128], sm.v(26, [(1, 1)], np_=TW), av, ALU.mult, ALU.add),
                     reads=[o1t, self.SMt, at], writes=[at])
                jk, jkt = self.WF[0], self.WFt[0]
                S.op("act", lambda e, av=av: e.activation(jk.v(0, [(1, 128)], np_=TW), av, AF.Square,
                                                          accum_out=sm.v(27, [(1, 1)], np_=TW)), reads=[at], writes=[jkt, self.SMt])
                S.op("act", lambda e: e.activation(sm.v(28, [(1, 1)], np_=TW), sm.v(27, [(1, 1)], np_=TW), AF.Ln, bias=self.eps_ap()[0:TW, :], scale=1.0 / 128.0),
                     reads=[self.SMt], writes=[self.SMt])
                S.op("act", lambda e: e.activation(sm.v(29, [(1, 1)], np_=TW), sm.v(28, [(1, 1)], np_=TW), AF.Exp, scale=-0.5),
                     reads=[self.SMt], writes=[self.SMt])
                S.op("dve", lambda e, av=av: e.scalar_tensor_tensor(av, av, sm.v(29, [(1, 1)], np_=TW),
                                                                    self.GSUB.v(0, [(1, 128)], np_=TW), ALU.mult, ALU.mult),
                     reads=[at, self.SMt, self.GSUBt], writes=[at])
                col = qb * QB + qt * TW

                def tr(av=av, col=col, at=at):
                    tp, tpt = self.PS[4], self.PSt[4]
                    S.op("pe", lambda e: e.transpose(tp.v(0, [(1, TW)]), av, self.idf(TW)), reads=[at, self.CFt], writes=[tpt])
                    S.op("act", lambda e: e.copy(self.Ov(h, col, TW), tp.v(0, [(1, TW)])), reads=[tpt], writes=big)
                pending.append(tr)
        flush_pending()

    def sb_attend(self, seq, h, QT0, KT0, VB0, nkc):
        S = self.S
        T, TW = seq.T, seq.TW
        QB = min(512, T)
        nqb = T // QB
        big = [self.BIGa]
        ET = [self.WF[0], self.WF[1]]
        ETt = [self.WFt[0], self.WFt[1]]
        NL = [self.WF[2], self.WF[3]]
        NLt = [self.WFt[2], self.WFt[3]]
        NB = [self.WB[0], self.WB[1]]
        NBt = [self.WBt[0], self.WBt[1]]
        WT = [self.WB[2], self.WB[3]]
        WTt = [self.WBt[2], self.WBt[3]]
        for qb in range(nqb):
            if seq.sample:
                ktiles = [(16, 16, 0)] + [(kt, 128, -1) for kt in range(15, -1, -1)]
            else:
                ktiles = [(kt, 128, kt - qb * 4) for kt in range(qb * 4 + 3, -1, -1)]
            op_, opt = self.PS[6 + (qb % 2)], self.PSt[6 + (qb % 2)]
            for rep in range(self.warm if not seq.sample else 1):
                S.op("pe", lambda e: e.matmul(op_.v(0, [(1, QB)]), self.cb("zero"), self.CB.v(0, [(1, QB)]), start=True, stop=False, skip_group_check=True),
                     reads=[self.CBt], writes=[opt], signal=False)
                for c in range(2):
                    S.op("pe", lambda e, c=c: e.matmul(self.PS[4 + c].v(0, [(1, QB)]), self.cb("zero"), self.CB.v(0, [(1, QB)]), start=True, stop=False, skip_group_check=True),
                         reads=[self.CBt], writes=[self.PSt[4 + c]], signal=False)
            units = []
            for ki, (kt, kr, j) in enumerate(ktiles):
                for c in range(2):
                    units.append((ki, kt, kr, j, c))
            nu = len(units)

            def geom(u):
                ki, kt, kr, j, c = u
                cstart = j * 128 if (j >= 0 and not seq.sample) else 0
                return cstart, QB - cstart

            def zmm(u):
                ki, kt, kr, j, c = u
                cstart, ncol = geom(u)
                zp, zpt = self.PS[c * 2 + (ki % 2)], self.PSt[c * 2 + (ki % 2)]
                S.op("pe", lambda e: e.matmul(
                    zp.v(cstart, [(1, ncol)], np_=kr), self.ATT.v(KT0 + kt * 128, [(1, kr)], p0=c * 64, np_=64),
                    self.ATT.v(QT0 + qb * QB + cstart, [(1, ncol)], p0=c * 64, np_=64), start=True, stop=True),
                    reads=[self.ATTq, self.ATTk], writes=[zpt])

            def a_act(u):
                ki, kt, kr, j, c = u
                cstart, ncol = geom(u)
                zp, zpt = self.PS[c * 2 + (ki % 2)], self.PSt[c * 2 + (ki % 2)]
                sl = lambda B: B.v(cstart, [(1, ncol)], np_=kr)
                S.op("act", lambda e: e.activation(sl(ET[c]), sl(zp), AF.Exp, scale=0.125), reads=[zpt], writes=[ETt[c]])
                S.op("act", lambda e: e.activation(sl(NL[c]), sl(ET[c]), AF.Ln, bias=self.SM.v(9, [(1, 1)], np_=kr)),
                     reads=[ETt[c], self.SMt], writes=[NLt[c]])

            def a_dve(u):
                ki, kt, kr, j, c = u
                cstart, ncol = geom(u)
                zp, zpt = self.PS[c * 2 + (ki % 2)], self.PSt[c * 2 + (ki % 2)]
                sl = lambda B: B.v(cstart, [(1, ncol)], np_=kr)
                if j >= 0:
                    msk = self.CF.v(256, [(1, TW)], np_=kr)
                    S.op("dve", lambda e: e.tensor_tensor(
                        NL[c].v(cstart, [(1, TW)], np_=kr), NL[c].v(cstart, [(1, TW)], np_=kr), msk, ALU.mult),
                        reads=[NLt[c], self.CFt], writes=[NLt[c]])
                S.op("dve", lambda e: e.tensor_copy(sl(NB[c]), sl(NL[c])), reads=[NLt[c]], writes=[NBt[c]])
                S.op("dve", lambda e: e.scalar_tensor_tensor(sl(ET[c]), sl(zp), 0.125, sl(NL[c]), ALU.mult, ALU.subtract),
                     reads=[zpt, NLt[c]], writes=[ETt[c]])

            def stage_b1(u):
                ki, kt, kr, j, c = u
                cstart, ncol = geom(u)
                sp_, spt_ = self.PS[4 + c], self.PSt[4 + c]
                S.op("pe", lambda e: e.matmul(
                    sp_.v(cstart, [(1, ncol)], np_=kr), self.cb("tri", k=kr, m=kr), NB[c].v(cstart, [(1, ncol)], np_=kr), start=False, stop=False, skip_group_check=True),
                    reads=[NBt[c], self.CBt], writes=[spt_])
                sl = lambda B: B.v(cstart, [(1, ncol)], np_=kr)
                S.op("dve", lambda e: e.tensor_tensor(sl(ET[c]), sl(ET[c]), sl(sp_), ALU.subtract), reads=[ETt[c], spt_], writes=[ETt[c]])

            def stage_b2(u, idx):
                ki, kt, kr, j, c = u
                cstart, ncol = geom(u)
                sp_, spt_ = self.PS[4 + c], self.PSt[4 + c]
                sl = lambda B: B.v(cstart, [(1, ncol)], np_=kr)
                S.op("pe", lambda e: e.matmul(
                    sp_.v(cstart, [(1, ncol)]), self.cb("trile", k=kr, m=128), NB[c].v(cstart, [(1, ncol)], np_=kr), start=False, stop=(idx >= nu - 2), skip_group_check=True),
                    reads=[NBt[c], self.CBt], writes=[spt_])
                S.op("act", lambda e: e.activation(sl(WT[c]), sl(ET[c]), AF.Exp), reads=[ETt[c]], writes=[WTt[c]])
                if j >= 0:
                    mskb = self.cb("msu", k=kr, m=TW)
                    S.op("pool", lambda e: e.tensor_tensor(
                        WT[c].v(cstart, [(1, TW)], np_=kr), WT[c].v(cstart, [(1, TW)], np_=kr), mskb, ALU.mult), reads=[WTt[c], self.CBt], writes=[WTt[c]])

            def stage_c(u, idx):
                ki, kt, kr, j, c = u
                cstart, ncol = geom(u)
                last = (idx == nu - 1)
                S.op("pe", lambda e: e.matmul(
                    op_.v(cstart, [(1, ncol)]), self.ATT.v(VB0 + kt * 256 + c * 128, [(1, 128)], np_=kr), WT[c].v(cstart, [(1, ncol)], np_=kr),
                    start=False, stop=last, skip_group_check=True), reads=[WTt[c], self.ATTv], writes=[opt], signal=last)

            zmm(units[0])
            zmm(units[1])
            for s_ in range(nu + 2):
                if s_ % 2 == 1 and s_ + 1 < nu:
                    zmm(units[s_ + 1])
                    zmm(units[s_ + 2])
                if s_ < nu:
                    a_act(units[s_])
                if 0 <= s_ - 1 < nu:
                    stage_b1(units[s_ - 1])
                if s_ < nu:
                    a_dve(units[s_])
                if 0 <= s_ - 2 < nu:
                    stage_c(units[s_ - 2], s_ - 2)
                if 0 <= s_ - 1 < nu:
                    stage_b2(units[s_ - 1], s_ - 1)
            S.op("act", lambda e, qb=qb: e.copy(self.Ov(h, qb * QB, QB), op_.v(0, [(1, QB)])), reads=[opt], writes=big)

    def run_seq(self, seq):
        self.load_x(seq)
        self.dump(f"{seq.name}_x", self.X.v(0, [(1, 8 * SEQ)]), self.Xt)
        for i in range(self.nlayers):
            kind = i % 4
            if kind == 0:
                self.gm_consts()
                self.cp("gm_consts")
                for half in range(seq.nh):
                    self.gmlp_half(seq, i, half)
            elif kind == 1:
                self.attn_layer(seq, i, "diff")
            elif kind == 2:
                for half in range(seq.nh):
                    self.sconv_half(seq, i, half)
            else:
                self.attn_layer(seq, i, "sb")
            if self.ada_next == i + 1 and i + 1 < 4:
                self.ada_layer(i + 1)
                self.ada_next = i + 2
            self.dump(f"{seq.name}_xm{i}", self.X.v(0, [(1, 8 * SEQ)]), self.Xt)
            for half in range(seq.nh):
                self.ffn_half(seq, i, half)
            self.dump(f"{seq.name}_xf{i}", self.X.v(0, [(1, 8 * SEQ)]), self.Xt)
        self.store_y(seq)

    def build(self):
        self.declare()
        S = self.S
        S.op("pool", lambda e: e.memset(self.SM.v(8, [(1, 1)]), EPS), writes=[self.SMt])
        S.op("pool", lambda e: e.memset(self.SM.v(9, [(1, 1)]), 1.0), writes=[self.SMt])
        self.setup()
        cfgs = {"p0": SeqCfg("p0", SEQ, 0, False, 0), "p1": SeqCfg("p1", SEQ, 1, False, 1), "s": SeqCfg("s", DEC, 2, True, 0)}
        try:
            for n in self.seq_names:
                self.run_seq(cfgs[n])
        except StopBuild:
            pass
        for name, (fn) in getattr(self, "dumps", {}).items():
            pass
        S.finish()
        self.stack.close()
        return self.nc


def _consts():
    p = np.arange(128)
    ident = (p[:, None] == p[None, :]).astype(np.float32)
    tri = (p[:, None] > p[None, :]).astype(np.float32)
    msu = (p[:, None] < p[None, :]).astype(np.float32)
    cst = np.concatenate([ident, tri, msu, np.zeros((128, 128), np.float32)], axis=1)
    pos = np.arange(17 * 128, dtype=np.float32)
    inv = (500000.0 ** (-np.arange(0, 16, 2, dtype=np.float32) / 16.0)).astype(np.float32)
    ang = pos[:, None] * inv[None, :]
    rope = np.concatenate([np.cos(ang), np.sin(ang)], axis=1).astype(np.float32)
    return np.ascontiguousarray(cst), np.ascontiguousarray(rope)


def _pack_qkv(w):
    w = np.asarray(w, dtype=np.float32).reshape(8, 128, 3, 8, 128)
    return np.ascontiguousarray(w.transpose(3, 1, 0, 2, 4).reshape(8, 128, 3 * D))


def _pack_wout(w):
    w = np.asarray(w, dtype=np.float32).reshape(4, FC, 128, 8, 128)
    return np.ascontiguousarray(w.transpose(0, 3, 2, 1, 4).reshape(4, 8, 128, DFF))


def make_in_maps(inp):
    f = lambda a: np.ascontiguousarray(np.asarray(a, dtype=np.float32))
    cst, rope = _consts()
    shared = {
        "cst": cst, "rope": rope,
        "ada_w": f(inp["ada_w"]),
        "gm_w_in": f(inp["gm_w_in"][0]), "gm_ln_g": f(inp["gm_ln_g"][0:1]), "gm_ln_b": f(inp["gm_ln_b"][0:1]),
        "gm_ws": f(inp["gm_ws"][0]), "gm_bs": f(inp["gm_bs"][0].reshape(1, D)), "gm_w_out": f(inp["gm_w_out"][0]),
        "diff_w_qkv": _pack_qkv(inp["diff_w_qkv"][0]), "diff_lambda": f(inp["diff_lambda"][0].reshape(1, 256)),
        "diff_subln_g": f(inp["diff_subln_g"][0:1]), "diff_w_out": f(inp["diff_w_out"][0]),
        "sc_w_in": f(inp["sc_w_in"][0]), "sc_w_out": f(inp["sc_w_out"][0]),
        "sb_w_qkv": _pack_qkv(inp["sb_w_qkv"][0]), "sb_w_out": f(inp["sb_w_out"][0]),
        "ffn_w_in": f(inp["ffn_w_in"]), "ffn_w_out": _pack_wout(inp["ffn_w_out"]),
    }
    maps = []
    for c in range(NCORES):
        r1 = np.concatenate([
            inp["norm_g"].reshape(16, D),
            inp["c_prompt"][2 * c:2 * c + 2], inp["c_sample"][c:c + 1],
            inp["sc_conv_w"][0],
            inp["ada_b"].reshape(24, D),
            inp["state_sconv"][0, c],
        ], axis=0)
        r2 = np.concatenate([
            inp["ffn_conv_w"].reshape(12, DFF),
            inp["ffn_conv_b"],
            inp["state_ffn_conv"][:, c].reshape(8, DFF),
        ], axis=0)
        m = dict(shared)
        m.update({
            "xp": f(inp["x_prompt"][2 * c:2 * c + 2]),
            "xs": f(inp["x_sample"][c]),
            "cdk": f(inp["cache_diff_k"][0, c].reshape(SEQ, D)),
            "cdv": f(inp["cache_diff_v"][0, c].reshape(SEQ, D)),
            "csk": f(inp["cache_sb_k"][0, c].reshape(SEQ, D)),
            "csv": f(inp["cache_sb_v"][0, c].reshape(SEQ, D)),
            "r1": f(r1), "r2": f(r2),
        })
        maps.append(m)
    return maps


_NC_CACHE = {}


def get_program(**kw):
    key = repr(sorted(kw.items()))
    if key not in _NC_CACHE:
        _NC_CACHE[key] = KB(**kw).build()
    return _NC_CACHE[key]


def assemble(results):
    cat = lambda k: np.concatenate([r[k] for r in results], axis=0)
    stack = lambda k: np.stack([r[k] for r in results], axis=0)
    y_p = cat("y_p")
    y_s = stack("y_s")
    gmv_s = stack("gmv_s")[None]
    dk_p = cat("dk_p").reshape(1, 16, SEQ, 8, 2, 64)
    dv_p = cat("dv_p").reshape(1, 16, SEQ, 8, 128)
    dk_s = stack("dk_s").reshape(1, 8, DEC, 8, 2, 64)
    dv_s = stack("dv_s").reshape(1, 8, DEC, 8, 128)
    sc_p = cat("sc_p")[None]
    sc_s = stack("sc_s")[None]
    sbk_p = cat("sbk_p").reshape(1, 16, SEQ, 16, 64)
    sbv_p = cat("sbv_p").reshape(1, 16, SEQ, 16, 64)
    sbk_s = stack("sbk_s").reshape(1, 8, DEC, 16, 64)
    sbv_s = stack("sbv_s").reshape(1, 8, DEC, 16, 64)
    ffc_p = np.concatenate([r["ffc_p"] for r in results], axis=1)
    ffc_s = np.stack([r["ffc_s"] for r in results], axis=1)
    outs = (y_p, y_s, gmv_s, dk_p, dv_p, dk_s, dv_s, sc_p, sc_s, sbk_p, sbv_p, sbk_s, sbv_s, ffc_p, ffc_s)
    return tuple(np.ascontiguousarray(o, dtype=np.float32) for o in outs)


def kernel(**inputs):
    inp = {k: np.asarray(v) for k, v in inputs.items()}
    nc = get_program()
    maps = make_in_maps(inp)
    res = run_bass_kernel_spmd(nc, maps, core_ids=list(range(NCORES)))
    return assemble(res.results)
```

```python
import math
import contextlib
import numpy as np
import concourse.bass as bass
import concourse.mybir as mybir
from concourse.bass_utils import run_bass_kernel_spmd

F32 = mybir.dt.float32
BF16 = mybir.dt.bfloat16
ALU = mybir.AluOpType
AF = mybir.ActivationFunctionType
AX = mybir.AxisListType

NCORES = 8
D = 1024
KC = 8
DFF = 2816
FC = 22
SEQ = 2048
DEC = 16
EPS = 1e-6
LAM_INIT = 0.8 - 0.6 * math.exp(-0.3 * 1)


class Tile:
    __slots__ = ("name", "w", "r")

    def __init__(self, name):
        self.name = name
        self.w = None
        self.r = {}


class DTile(Tile):
    __slots__ = ("semkey", "count")

    def __init__(self, name, semkey):
        super().__init__(name)
        self.semkey = semkey
        self.count = 0


class _Eng:
    def __init__(self, name, obj, semkey):
        self.name = name
        self.obj = obj
        self.semkey = semkey
        self.count = 0
        self.known = {}
        self.ninst = 0


class Sched:
    def __init__(self, nc, stack):
        self.nc = nc
        self.stack = stack
        self.sems = []
        self.eng = {}
        for name, obj in (("pe", nc.tensor), ("act", nc.scalar), ("dve", nc.vector),
                          ("pool", nc.gpsimd), ("sp", nc.sync)):
            k = self.new_sem("e_" + name)
            self.eng[name] = _Eng(name, obj, k)
        self.dtiles = []

    def new_sem(self, name):
        h = self.stack.enter_context(self.nc.semaphore(name))
        self.sems.append(h)
        return len(self.sems) - 1

    def tile(self, name):
        return Tile(name)

    def dtile(self, name):
        t = DTile(name, self.new_sem("d_" + name))
        self.dtiles.append(t)
        return t

    def retire(self, srcs, dsts):
        for d in dsts:
            for s in srcs:
                if s is d:
                    continue
                if s.w is not None:
                    if d.r.get(s.w[0], 0) < s.w[1]:
                        d.r[s.w[0]] = s.w[1]
                for k, v in s.r.items():
                    if d.r.get(k, 0) < v:
                        d.r[k] = v

    def _deps(self, E, reads, writes):
        deps = {}

        def add(k, v, raw):
            if k == E.semkey and not raw:
                if E.name == "pe" or v > E.count:
                    return
            if E.known.get(k, 0) >= v:
                return
            if deps.get(k, 0) < v:
                deps[k] = v

        for t in reads:
            if t.w is not None:
                add(t.w[0], t.w[1], True)
        for t in writes:
            if t.w is not None:
                add(t.w[0], t.w[1], False)
            for k, v in t.r.items():
                add(k, v, False)
        return deps

    def _emit(self, E, deps, fn):
        items = list(deps.items())
        for k, v in items:
            if k == E.semkey:
                assert v <= E.count, f"self-wait on future signal {E.name} {v} {E.count}"
        for k, v in items[:-1]:
            E.obj.wait_ge(self.sems[k], v)
            E.known[k] = v
            E.ninst += 1
        inst = fn(E.obj)
        E.ninst += 1
        if items:
            k, v = items[-1]
            inst._wait_ge(self.sems[k], v)
            E.known[k] = v
        return inst

    def op(self, eng, fn, reads=(), writes=(), signal=True):
        E = self.eng[eng]
        deps = self._deps(E, reads, writes)
        inst = self._emit(E, deps, fn)
        if signal:
            inst.then_inc(self.sems[E.semkey], 1)
            E.count += 1
            val = E.count
        else:
            val = E.count + 1
        for t in reads:
            if t.r.get(E.semkey, 0) < val:
                t.r[E.semkey] = val
        for t in writes:
            t.w = (E.semkey, val)
            t.r = {}
        return inst

    def dma(self, q, out_ap, in_ap, dt, load, extra_reads=(), extra_writes=(), **kw):
        E = self.eng[q]
        if load:
            deps = self._deps(E, extra_reads, [dt] + list(extra_writes))
        else:
            deps = self._deps(E, [dt] + list(extra_reads), extra_writes)
        inst = self._emit(E, deps, lambda e: e.dma_start(out=out_ap, in_=in_ap, **kw))
        inst.then_inc(self.sems[dt.semkey], 16)
        dt.count += 16
        if load:
            dt.w = (dt.semkey, dt.count)
            dt.r = {}
            for t in extra_writes:
                t.w = (dt.semkey, dt.count)
                t.r = {}
        else:
            dt.r[dt.semkey] = dt.count
            for t in extra_reads:
                if t.r.get(dt.semkey, 0) < dt.count:
                    t.r[dt.semkey] = dt.count
        return inst

    def finish(self):
        E = self.eng["sp"]
        for t in self.dtiles:
            if t.count > 0 and E.known.get(t.semkey, 0) < t.count:
                E.obj.wait_ge(self.sems[t.semkey], t.count)
        for name in ("pe", "act", "dve", "pool"):
            X = self.eng[name]
            if X.count > 0:
                E.obj.wait_ge(self.sems[X.semkey], X.count)


class Buf:
    def __init__(self, t, rowlen, dtype):
        self.t = t
        self.rowlen = rowlen
        self.dtype = dtype

    def v(self, off, dims, p0=0, np_=128):
        return bass.AP(self.t, p0 * self.rowlen + off, [[self.rowlen, np_]] + [[s, n] for s, n in dims])


class SubBuf:
    def __init__(self, parent, base):
        self.parent = parent
        self.base = base
        self.rowlen = parent.rowlen
        self.dtype = parent.dtype

    def v(self, off, dims, p0=0, np_=128):
        return self.parent.v(self.base + off, dims, p0=p0, np_=np_)


class SeqCfg:
    def __init__(self, name, T, b, sample, idx):
        self.name = name
        self.T = T
        self.b = b
        self.sample = sample
        self.idx = idx
        self.nh = 1 if sample else 2
        self.TH = T // self.nh
        self.BW = min(512, self.TH)
        self.nb = self.TH // self.BW
        self.TW = min(128, self.TH)
        self.nt = self.TH // self.TW


class StopBuild(Exception):
    pass


class KB:
    def __init__(self, dbg=None, nlayers=4, seqs=("p0", "p1", "s"), cut=None, warm=1):
        self.cut = cut
        self.warm = warm
        self.dbg = dbg or {}
        self.nlayers = nlayers
        self.seq_names = seqs
        self.nc = bass.Bass("TRN2", target_bir_lowering=False)
        self.stack = contextlib.ExitStack()

    def din(self, name, shape):
        return self.nc.dram_tensor(name, list(shape), F32, kind="ExternalInput").ap()

    def dout(self, name, shape):
        return self.nc.dram_tensor(name, list(shape), F32, kind="ExternalOutput").ap()

    def sb(self, name, rowlen, dtype, np_=128):
        t = self.stack.enter_context(self.nc.sbuf_tensor(name, [np_, rowlen], dtype))
        return Buf(t, rowlen, dtype)

    def declare(self):
        S = self.S = Sched(self.nc, self.stack)
        self.i = {}
        I = self.i
        I["xp"] = self.din("xp", [2, SEQ, D])
        I["xs"] = self.din("xs", [DEC, D])
        I["cdk"] = self.din("cdk", [SEQ, D])
        I["cdv"] = self.din("cdv", [SEQ, D])
        I["csk"] = self.din("csk", [SEQ, D])
        I["csv"] = self.din("csv", [SEQ, D])
        I["r1"] = self.din("r1", [48, D])
        I["r2"] = self.din("r2", [24, DFF])
        I["cst"] = self.din("cst", [128, 512])
        I["rope"] = self.din("rope", [17 * 128, 16])
        I["ada_w"] = self.din("ada_w", [4, D, 6 * D])
        I["gm_w_in"] = self.din("gm_w_in", [D, 2 * D])
        I["gm_ln_g"] = self.din("gm_ln_g", [1, D])
        I["gm_ln_b"] = self.din("gm_ln_b", [1, D])
        I["gm_ws"] = self.din("gm_ws", [8, 128, 128])
        I["gm_bs"] = self.din("gm_bs", [1, D])
        I["gm_w_out"] = self.din("gm_w_out", [D, D])
        I["diff_w_qkv"] = self.din("diff_w_qkv", [8, 128, 3 * D])
        I["diff_lambda"] = self.din("diff_lambda", [1, 256])
        I["diff_subln_g"] = self.din("diff_subln_g", [1, 128])
        I["diff_w_out"] = self.din("diff_w_out", [D, D])
        I["sc_w_in"] = self.din("sc_w_in", [D, 3 * D])
        I["sc_w_out"] = self.din("sc_w_out", [D, D])
        I["sb_w_qkv"] = self.din("sb_w_qkv", [8, 128, 3 * D])
        I["sb_w_out"] = self.din("sb_w_out", [D, D])
        I["ffn_w_in"] = self.din("ffn_w_in", [4, D, 2 * DFF])
        I["ffn_w_out"] = self.din("ffn_w_out", [4, 8, 128, DFF])
        self.o = {}
        O = self.o
        O["y_p"] = self.dout("y_p", [2, SEQ, D])
        O["y_s"] = self.dout("y_s", [DEC, D])
        O["gmv_s"] = self.dout("gmv_s", [DEC, D])
        O["dk_p"] = self.dout("dk_p", [2, SEQ, D])
        O["dv_p"] = self.dout("dv_p", [2, SEQ, D])
        O["dk_s"] = self.dout("dk_s", [DEC, D])
        O["dv_s"] = self.dout("dv_s", [DEC, D])
        O["sc_p"] = self.dout("sc_p", [2, 2, D])
        O["sc_s"] = self.dout("sc_s", [2, D])
        O["sbk_p"] = self.dout("sbk_p", [2, SEQ, D])
        O["sbv_p"] = self.dout("sbv_p", [2, SEQ, D])
        O["sbk_s"] = self.dout("sbk_s", [DEC, D])
        O["sbv_s"] = self.dout("sbv_s", [DEC, D])
        O["ffc_p"] = self.dout("ffc_p", [4, 2, 2, DFF])
        O["ffc_s"] = self.dout("ffc_s", [4, 2, DFF])
        for name, shape in self.dbg.items():
            O["dbg_" + name] = self.dout("dbg_" + name, shape)

        self.X = self.sb("X", 8 * SEQ, F32)
        self.Xt = [S.tile("X0"), S.tile("X1")]
        self.HB = self.sb("HB", 8 * SEQ, BF16)
        self.HBt = [S.tile("HB0"), S.tile("HB1")]
        self.BIG = self.sb("BIG", FC * 1024, BF16)
        self.BIGa = S.tile("BIGa")
        self.BIGb = S.tile("BIGb")
        self.WS = [self.sb(f"WS{k}", 3072, BF16) for k in range(3)]
        self.WSt = [S.dtile(f"WS{k}") for k in range(3)]
        self.wsi = 0
        self.ATT = self.sb("ATT", 8704, BF16)
        self.ATTq = S.dtile("ATTq")
        self.ATTk = S.dtile("ATTk")
        self.ATTv = S.dtile("ATTv")
        self.WS += [SubBuf(self.ATT, 0), SubBuf(self.ATT, 3072)]
        self.WSt += [S.dtile("WS3"), S.dtile("WS4")]
        self.wide = False
        self.WF = [self.sb(f"WF{k}", 512, F32) for k in range(4)]
        self.WFt = [S.tile(f"WF{k}") for k in range(4)]
        self.WB = [self.sb(f"WB{k}", 512, BF16) for k in range(4)]
        self.WBt = [S.tile(f"WB{k}") for k in range(4)]
        self.GS = self.sb("GS", 1026, F32)
        self.GSt = S.tile("GS")
        self.LAMB = self.sb("LAMB", 512, BF16)
        self.LAMBt = S.dtile("LAMB")
        self.QKVs = [self.sb(f"QKV{k}", 384, F32) for k in range(2)]
        self.QKVts = [S.tile(f"QKV{k}") for k in range(2)]
        self.QKB = [SubBuf(self.LAMB, 0), SubBuf(self.LAMB, 256)]
        self.QKBt = [S.tile("QKB0"), S.tile("QKB1")]
        self.MODL = self.sb("MODL", 144, F32)
        self.MODLt = S.tile("MODL")
        self.V1T = self.sb("V1T", 8 * 48, F32)
        self.V1Tt = S.tile("V1T")
        self.V2T = self.sb("V2T", FC * 24, F32)
        self.V2Tt = S.tile("V2T")
        self.SC = self.sb("SC", 576, F32)
        self.SCt = S.tile("SC")
        self.CF = self.sb("CF", 512, F32)
        self.CFt = S.dtile("CF")
        self.CB = self.sb("CB", 7 * 128, BF16)
        self.CBt = S.tile("CB")
        self.ROPE = self.sb("ROPE", 17 * 16, F32)
        self.ROPEt = S.dtile("ROPE")
        self.GSUB = self.sb("GSUB", 128, F32)
        self.GSUBt = S.dtile("GSUB")
        self.SM = self.sb("SM", 64, F32)
        self.SMt = S.tile("SM")
        self.CSB = self.sb("CSB", 24, BF16)
        self.CSBt = S.tile("CSB")
        self.GCAR = self.sb("GCAR", FC * 2, F32)
        self.GCARt = S.tile("GCAR")
        self.PCAR = self.sb("PCAR", 16, F32)
        self.PCARt = S.tile("PCAR")
        self.HAL = self.sb("HAL", 6, F32)
        self.HALt = [S.tile("HAL0"), S.tile("HAL1"), S.tile("HALZ")]
        self.STG = [S.dtile("STG0"), S.dtile("STG1")]
        self.KSTt = S.dtile("KST")
        self.VSTt = S.dtile("VST")
        self.OST = S.dtile("OST")
        self.PS = []
        self.PSt = []
        for k in range(8):
            t = self.stack.enter_context(self.nc.psum_tensor(f"PS{k}", [128, 512], F32))
            self.PS.append(Buf(t, 512, F32))
            self.PSt.append(S.tile(f"PS{k}"))

    def Xv(self, kc, c0, n, np_=128):
        return self.X.v(kc * SEQ + c0, [(1, n)], np_=np_)

    def Hv(self, kc, c0, n):
        return self.HB.v(kc * SEQ + c0, [(1, n)])

    def Fv(self, mo, c0, n):
        return self.HB.v(0, [(1, 8 * SEQ)]).bitcast(F32)[:, mo * 1024 + c0: mo * 1024 + c0 + n]

    def Av(self, j, c0, n):
        return self.BIG.v(j * 1024 + c0, [(1, n)])

    def Ov(self, kc, c0, n):
        return self.BIG.v(kc * SEQ + c0, [(1, n)])

    def bigf(self, e0, n):
        return self.BIG.v(e0, [(1, 2 * n)]).bitcast(F32)

    def attf(self, e0, n):
        return self.ATT.v(e0, [(1, 2 * n)]).bitcast(F32)

    def gsf(self):
        return self.GS

    def sc(self, seq, i, s, kc):
        off = ((i * 3 + seq.b) * 6 + s) * 8 + kc
        return self.SC.v(off, [(1, 1)])

    def cb(self, which, k=128, m=128, p0=0):
        idx = {"ones_s": 0, "ones": 1, "tri": 2, "zero": 3, "msu": 4, "ident": 5, "trile": 6}[which]
        return self.CB.v(idx * 128, [(1, m)], p0=p0, np_=k)

    def idf(self, k):
        return self.CF.v(0, [(1, k)], np_=k)

    def load_panel(self, pieces, nk):
        S = self.S
        k = self.wsi % (5 if self.wide else 3)
        self.wsi += 1
        slot, st = self.WS[k], self.WSt[k]
        tot = sum(n for _, n in pieces)
        assert nk * tot <= 3072
        c = 0
        for ap, n in pieces:
            dst = slot.v(c, [(1, n)]) if nk == 1 else slot.v(c, [(tot, nk), (1, n)])
            S.dma("pool", dst, ap, st, True)
            c += n
        return slot, st, tot

    @staticmethod
    def wview(w2d, nk):
        return w2d.rearrange("(kc p) n -> p kc n", p=128)

    def cp(self, name):
        if self.cut == name:
            raise StopBuild()

    def set_wide(self, on):
        att = [self.ATTq, self.ATTk, self.ATTv]
        ext = [self.WSt[3], self.WSt[4]]
        if on:
            self.S.retire(att, ext)
        else:
            self.S.retire(ext, att)
        self.wide = on

    def dump(self, name, src_ap, tiles):
        if name not in self.dbg:
            return
        S = self.S
        dt = S.dtile("dbg_" + name)
        S.dma("sp", self.o["dbg_" + name], src_ap, dt, False, extra_reads=tiles)

    def setup(self):
        S, I = self.S, self.i
        S.dma("sp", self.CF.v(0, [(1, 512)]), I["cst"][:, :], self.CFt, True)
        S.op("pool", lambda e: e.memset(self.CB.v(0, [(1, 128)]), 1.0 / 1024.0), writes=[self.CBt])
        S.op("pool", lambda e: e.memset(self.CB.v(128, [(1, 128)]), 1.0), writes=[self.CBt])
        S.op("pool", lambda e: e.memset(self.CB.v(384, [(1, 128)]), 0.0), writes=[self.CBt])
        S.op("dve", lambda e: e.tensor_copy(self.CB.v(256, [(1, 128)]), self.CF.v(128, [(1, 128)])), reads=[self.CFt], writes=[self.CBt])
        S.op("dve", lambda e: e.tensor_copy(self.CB.v(512, [(1, 128)]), self.CF.v(256, [(1, 128)])), reads=[self.CFt], writes=[self.CBt])
        S.op("dve", lambda e: e.tensor_copy(self.CB.v(640, [(1, 128)]), self.CF.v(0, [(1, 128)])), reads=[self.CFt], writes=[self.CBt])
        S.op("dve", lambda e: e.tensor_scalar(self.CB.v(768, [(1, 128)]), self.CF.v(128, [(1, 128)]), -1.0, 1.0, ALU.mult, ALU.add),
             reads=[self.CFt], writes=[self.CBt])
        S.op("pool", lambda e: e.memset(self.GCAR.v(0, [(1, FC * 2)]), 0.0), writes=[self.GCARt])
        S.op("pool", lambda e: e.memset(self.PCAR.v(0, [(1, 16)]), 0.0), writes=[self.PCARt])
        S.op("pool", lambda e: e.memset(self.HAL.v(0, [(1, 6)]), 0.0), writes=self.HALt)
        S.dma("sp", self.ROPE.v(0, [(16, 17), (1, 16)]), I["rope"].rearrange("(t p) c -> p t c", p=128), self.ROPEt, True)
        st0 = self.bigf(0, 1024)
        S.dma("sp", st0[0:48, :], I["r1"][:, :], self.STG[0], True)
        ps = self.PS[0]
        for kc in range(8):
            S.op("pe", lambda e, kc=kc: e.transpose(ps.v(kc * 48, [(1, 48)]), st0[0:48, kc * 128:(kc + 1) * 128], self.idf(48)),
                 reads=[self.STG[0], self.CFt], writes=[self.PSt[0]])
        S.op("dve", lambda e: e.tensor_copy(self.V1T.v(0, [(1, 384)]), ps.v(0, [(1, 384)])), reads=[self.PSt[0]], writes=[self.V1Tt])
        st2 = self.bigf(2048, DFF)
        S.dma("sp", st2[0:24, :], I["r2"][:, :], self.STG[1], True)
        for grp in range(2):
            psg = self.PS[1 + grp]
            for jj in range(11):
                j = grp * 11 + jj
                S.op("pe", lambda e, j=j, jj=jj, psg=psg: e.transpose(psg.v(jj * 24, [(1, 24)]), st2[0:24, j * 128:(j + 1) * 128], self.idf(24)),
                     reads=[self.STG[1], self.CFt], writes=[self.PSt[1 + grp]])
            S.op("dve", lambda e, grp=grp, psg=psg: e.tensor_copy(self.V2T.v(grp * 264, [(1, 264)]), psg.v(0, [(1, 264)])),
                 reads=[self.PSt[1 + grp]], writes=[self.V2Tt])
        S.retire(self.STG, [self.BIGa])
        S.op("act", lambda e: e.activation(self.CSB.v(0, [(3, 8), (1, 3)]), self.V1T.v(16, [(48, 8), (1, 3)]), AF.Silu),
             reads=[self.V1Tt], writes=[self.CSBt])
        self.ada_layer(0)
        self.ada_next = 1
        LAMf = self.LAMB.v(0, [(1, 512)]).bitcast(F32)
        S.dma("sp", LAMf, I["diff_lambda"].partition_broadcast(128), self.LAMBt, True)
        S.dma("sp", self.GSUB.v(0, [(1, 128)]), I["diff_subln_g"].partition_broadcast(128), self.GSUBt, True)
        wf = self.WF[0]
        S.op("dve", lambda e: e.tensor_tensor(wf.v(0, [(1, 64)]), LAMf[:, 0:64], LAMf[:, 64:128], ALU.mult),
             reads=[self.LAMBt], writes=[self.WFt[0]])
        S.op("dve", lambda e: e.tensor_tensor(wf.v(64, [(1, 64)]), LAMf[:, 128:192], LAMf[:, 192:256], ALU.mult),
             reads=[self.LAMBt], writes=[self.WFt[0]])
        S.op("dve", lambda e: e.reduce_sum(self.SM.v(1, [(1, 1)]), wf.v(0, [(1, 64)]), axis=AX.X), reads=[self.WFt[0]], writes=[self.SMt])
        S.op("dve", lambda e: e.reduce_sum(self.SM.v(2, [(1, 1)]), wf.v(64, [(1, 64)]), axis=AX.X), reads=[self.WFt[0]], writes=[self.SMt])
        S.op("act", lambda e: e.activation(self.SM.v(3, [(1, 2)]), self.SM.v(1, [(1, 2)]), AF.Exp), reads=[self.SMt], writes=[self.SMt])
        S.op("dve", lambda e: e.scalar_tensor_tensor(self.SM.v(0, [(1, 1)]), self.SM.v(4, [(1, 1)]), -LAM_INIT, self.SM.v(3, [(1, 1)]),
                                                     ALU.add, ALU.subtract), reads=[self.SMt], writes=[self.SMt])
        S.op("dve", lambda e: e.tensor_scalar(self.GSUB.v(0, [(1, 128)]), self.GSUB.v(0, [(1, 128)]), 1.0 - LAM_INIT, None, ALU.mult),
             reads=[self.GSUBt], writes=[self.GSUBt])
        S.retire([self.LAMBt], self.QKBt)

    def ada_layer(self, i):
        S, I = self.S, self.i
        MOD, modt = self.MODL, self.MODLt
        psa = self.PS[3]
        psat = self.PSt[3]
        wv = self.wview(I["ada_w"][i], 8)
        for p in range(16):
            slot, st, tot = self.load_panel([(wv[:, :, p * 384:(p + 1) * 384], 384)], 8)
            for ml in range(3):
                m = p * 3 + ml
                for kc in range(8):
                    S.op("pe", lambda e, kc=kc, ml=ml, m=m, slot=slot: e.matmul(
                        psa.v(m * 3, [(1, 3)]), slot.v(kc * 384 + ml * 128, [(1, 128)]), self.CSB.v(kc * 3, [(1, 3)]),
                        start=(kc == 0), stop=(kc == 7)),
                        reads=[st, self.CSBt], writes=[psat], signal=(kc == 7))
        for kind in range(6):
            S.op("dve", lambda e, kind=kind: e.tensor_tensor(
                MOD.v(kind * 24, [(3, 8), (1, 3)]),
                psa.v(kind * 24, [(3, 8), (1, 3)]),
                self.V1T.v(22 + i * 6 + kind, [(48, 8), (0, 3)]), ALU.add),
                reads=[psat, self.V1Tt], writes=[modt])
        for b in range(3):
            def modv(kind):
                return MOD.v(kind * 24 + b, [(3, 8)])

            def scv(s_):
                return self.SC.v(((i * 3 + b) * 6 + s_) * 8, [(1, 8)])

            def ng(s_):
                return self.V1T.v(i * 4 + s_, [(48, 8)])
            S.op("dve", lambda e, modv=modv, scv=scv, ng=ng: e.scalar_tensor_tensor(scv(0), modv(1), 1.0, ng(0), ALU.add, ALU.mult),
                 reads=[modt, self.V1Tt], writes=[self.SCt])
            S.op("dve", lambda e, modv=modv, scv=scv: e.tensor_copy(scv(1), modv(0)), reads=[modt], writes=[self.SCt])
            S.op("dve", lambda e, modv=modv, scv=scv, ng=ng: e.tensor_tensor(scv(2), modv(2), ng(1), ALU.mult),
                 reads=[modt, self.V1Tt], writes=[self.SCt])
            S.op("dve", lambda e, modv=modv, scv=scv, ng=ng: e.scalar_tensor_tensor(scv(3), modv(4), 1.0, ng(2), ALU.add, ALU.mult),
                 reads=[modt, self.V1Tt], writes=[self.SCt])
            S.op("dve", lambda e, modv=modv, scv=scv: e.tensor_copy(scv(4), modv(3)), reads=[modt], writes=[self.SCt])
            S.op("dve", lambda e, modv=modv, scv=scv, ng=ng: e.tensor_tensor(scv(5), modv(5), ng(3), ALU.mult),
                 reads=[modt, self.V1Tt], writes=[self.SCt])

    def load_x(self, seq):
        S, I = self.S, self.i
        S.retire([self.BIGa], self.STG)
        src = I["xs"] if seq.sample else I["xp"][seq.idx]
        ntile = seq.T // seq.TW
        for t in range(ntile):
            k = t % 2
            stg = self.bigf(k * 2048, 1024)
            S.dma("sp", stg[0:seq.TW, :], src[t * seq.TW:(t + 1) * seq.TW, :], self.STG[k], True)
            for g in range(2):
                ps, pst = self.PS[(t % 2) * 2 + g], self.PSt[(t % 2) * 2 + g]
                for q in range(4):
                    kc = g * 4 + q
                    S.op("pe", lambda e, kc=kc, q=q, ps=ps, stg=stg: e.transpose(
                        ps.v(q * 128, [(1, seq.TW)]), stg[0:seq.TW, kc * 128:(kc + 1) * 128], self.idf(seq.TW)),
                        reads=[self.STG[k], self.CFt], writes=[pst], signal=(q == 3))
                h = (t * seq.TW) // seq.TH
                eng = "act" if g == 0 else "dve"
                dst = self.X.v(g * 4 * SEQ + t * seq.TW, [(SEQ, 4), (1, seq.TW)])
                srcp = ps.v(0, [(128, 4), (1, seq.TW)])
                if eng == "act":
                    S.op("act", lambda e, dst=dst, srcp=srcp: e.copy(dst, srcp), reads=[pst], writes=[self.Xt[h]])
                else:
                    S.op("dve", lambda e, dst=dst, srcp=srcp: e.tensor_copy(dst, srcp), reads=[pst], writes=[self.Xt[h]])
        S.retire(self.STG, [self.BIGa])

    def store_y(self, seq):
        S, O = self.S, self.o
        S.retire([self.BIGa], self.STG)
        dst = O["y_s"] if seq.sample else O["y_p"][seq.idx]
        ntile = seq.T // seq.TW
        for t in range(ntile):
            k = t % 2
            h = (t * seq.TW) // seq.TH
            stg = self.bigf(k * 2048, 1024)
            for g in range(2):
                ps, pst = self.PS[(t % 2) * 2 + g], self.PSt[(t % 2) * 2 + g]
                for q in range(4):
                    kc = g * 4 + q
                    S.op("pe", lambda e, kc=kc, q=q, ps=ps: e.transpose(
                        ps.v(q * 128, [(1, 128)], np_=seq.TW), self.Xv(kc, t * seq.TW, seq.TW), self.idf(128)),
                        reads=[self.Xt[h], self.CFt], writes=[pst], signal=(q == 3))
                if g == 0:
                    S.op("act", lambda e, ps=ps, stg=stg: e.copy(stg[0:seq.TW, 0:512], ps.v(0, [(1, 512)], np_=seq.TW)),
                         reads=[pst], writes=[self.STG[k]])
                else:
                    S.op("dve", lambda e, ps=ps, stg=stg: e.tensor_copy(stg[0:seq.TW, 512:1024], ps.v(0, [(1, 512)], np_=seq.TW)),
                         reads=[pst], writes=[self.STG[k]])
            S.dma("sp", dst[t * seq.TW:(t + 1) * seq.TW, :], stg[0:seq.TW, :], self.STG[k], False)
        S.retire(self.STG, [self.BIGa])

    def norm_mod(self, seq, i, s_gs, s_sh, c0, ncols, xh):
        S = self.S
        BW = min(512, ncols)
        xts = [self.Xt[h] for h in xh]
        hts = [self.HBt[h] for h in xh]
        for b0 in range(c0, c0 + ncols, BW):
            pss, psst = self.PS[4], self.PSt[4]
            for kc in range(8):
                sq, sqt = self.WB[kc % 2], self.WBt[kc % 2]
                if kc in (1, 4, 6):
                    S.op("dve", lambda e, kc=kc, sq=sq: e.tensor_tensor(sq.v(0, [(1, BW)]), self.Xv(kc, b0, BW), self.Xv(kc, b0, BW), ALU.mult),
                         reads=xts, writes=[sqt])
                else:
                    S.op("act", lambda e, kc=kc, sq=sq: e.activation(sq.v(0, [(1, BW)]), self.Xv(kc, b0, BW), AF.Square),
                         reads=xts, writes=[sqt])
                S.op("pe", lambda e, kc=kc, sq=sq: e.matmul(pss.v(0, [(1, BW)]), self.cb("ones_s"), sq.v(0, [(1, BW)]),
                                                            start=(kc == 0), stop=(kc == 7)),
                     reads=[sqt, self.CBt], writes=[psst], signal=True)
            rs, rst = self.WF[0], self.WFt[0]
            S.op("act", lambda e: e.activation(rs.v(0, [(1, BW)]), pss.v(0, [(1, BW)]), AF.Ln, bias=self.eps_ap()), reads=[psst, self.SMt], writes=[rst])
            S.op("act", lambda e: e.activation(rs.v(0, [(1, BW)]), rs.v(0, [(1, BW)]), AF.Exp, scale=-0.5), reads=[rst], writes=[rst])
            for kc in range(8):
                tmp, tmpt = self.WF[1 + kc % 2], self.WFt[1 + kc % 2]
                S.op("dve", lambda e, kc=kc, tmp=tmp: e.scalar_tensor_tensor(
                    tmp.v(0, [(1, BW)]), self.Xv(kc, b0, BW), self.sc(seq, i, s_gs, kc), rs.v(0, [(1, BW)]), ALU.mult, ALU.mult),
                    reads=xts + [rst, self.SCt], writes=[tmpt])
                S.op("act", lambda e, kc=kc, tmp=tmp: e.activation(
                    self.Hv(kc, b0, BW), tmp.v(0, [(1, BW)]), AF.Identity, bias=self.sc(seq, i, s_sh, kc), scale=1.0),
                    reads=[tmpt, self.SCt], writes=hts)

    def eps_ap(self):
        return self.SM.v(8, [(1, 1)])

    def proj_res(self, seq, i, s_gg, w2d, nk, src, src_tiles, half):
        S = self.S
        BW, nb = seq.BW, seq.nb
        packed = (nk != 8)
        wv = None if packed else self.wview(w2d, nk)
        npm = 3 if nk == 8 else 1
        fts = self.HBt
        ss = [self.PS[4 + b] for b in range(nb)]
        sst = [self.PSt[4 + b] for b in range(nb)]
        mo = 0
        cnt = 0
        while mo < 8:
            n_here = min(npm, 8 - mo)
            if packed:
                slot, st, _ = self.load_panel([(w2d[mo], nk * 128)], 1)
                tot = 128
            else:
                slot, st, tot = self.load_panel([(wv[:, :, mo * 128:(mo + n_here) * 128], n_here * 128)], nk)
            for ml in range(n_here):
                for b in range(nb):
                    ps, pst = self.PS[cnt % 2], self.PSt[cnt % 2]
                    cnt += 1
                    for kc in range(nk):
                        S.op("pe", lambda e, kc=kc, ml=ml, b=b, ps=ps, slot=slot, tot=tot: e.matmul(
                            ps.v(0, [(1, BW)]), slot.v(kc * tot + ml * 128, [(1, 128)]), src(kc, b * BW, BW),
                            start=(kc == 0), stop=(kc == nk - 1)),
                            reads=[st] + src_tiles, writes=[pst], signal=(kc == nk - 1))
                    m = mo + ml
                    S.op("act", lambda e, m=m, b=b, ps=ps: e.copy(self.Fv(m, b * BW, BW), ps.v(0, [(1, BW)])),
                         reads=[pst], writes=fts)
                    sq, sqt = self.WB[cnt % 2], self.WBt[cnt % 2]
                    S.op("act", lambda e, ps=ps, sq=sq: e.activation(sq.v(0, [(1, BW)]), ps.v(0, [(1, BW)]), AF.Square),
                         reads=[pst], writes=[sqt])
                    S.op("pe", lambda e, b=b, sq=sq, m=m: e.matmul(ss[b].v(0, [(1, BW)]), self.cb("ones_s"), sq.v(0, [(1, BW)]),
                                                                  start=(m == 0), stop=(m == 7)),
                         reads=[sqt, self.CBt], writes=[sst[b]], signal=True)
            mo += n_here
        xt = [self.Xt[half]]
        for b in range(nb):
            rs, rst = self.WF[0], self.WFt[0]
            S.op("act", lambda e, b=b: e.activation(rs.v(0, [(1, BW)]), ss[b].v(0, [(1, BW)]), AF.Ln, bias=self.eps_ap()),
                 reads=[sst[b], self.SMt], writes=[rst])
            S.op("act", lambda e: e.activation(rs.v(0, [(1, BW)]), rs.v(0, [(1, BW)]), AF.Exp, scale=-0.5), reads=[rst], writes=[rst])
            for m in range(8):
                tmp, tmpt = self.WF[1 + m % 3], self.WFt[1 + m % 3]
                S.op("dve", lambda e, m=m, b=b, tmp=tmp: e.scalar_tensor_tensor(
                    tmp.v(0, [(1, BW)]), self.Fv(m, b * BW, BW), self.sc(seq, i, s_gg, m), rs.v(0, [(1, BW)]), ALU.mult, ALU.mult),
                    reads=fts + [rst, self.SCt], writes=[tmpt])
                xc = half * seq.TH + b * BW
                S.op("pool" if m in (2, 5) else "dve",
                     lambda e, m=m, xc=xc, tmp=tmp: e.tensor_tensor(self.Xv(m, xc, BW), self.Xv(m, xc, BW), tmp.v(0, [(1, BW)]), ALU.add),
                     reads=[tmpt] + xt, writes=xt)

    def ffn_half(self, seq, i, half):
        S, I, O = self.S, self.i, self.o
        BW, nb, TH = seq.BW, seq.nb, seq.TH
        c0 = half * TH
        self.set_wide(True)
        self.norm_mod(seq, i, 3, 4, c0, TH, [half])
        wv = self.wview(I["ffn_w_in"][i], 8)
        big = [self.BIGa, self.BIGb]
        hts = [self.HBt[half]]
        gs, gst = self.GS, self.GSt
        cw = lambda k, j: self.V2T.v(j * 24 + i * 3 + k, [(1, 1)])
        cbias = lambda j: self.V2T.v(j * 24 + 12 + i, [(1, 1)])
        j = 0
        cnt = 0
        while j < FC:
            nj = min(3, FC - j)
            gslot, gstile, gtot = self.load_panel([(wv[:, :, j * 128:(j + nj) * 128], nj * 128)], 8)
            uslot, ustile, utot = self.load_panel([(wv[:, :, DFF + j * 128:DFF + (j + nj) * 128], nj * 128)], 8)
            for jl in range(nj):
                jj = j + jl
                for b in range(nb):
                    par = cnt % 2
                    gp, gpt = self.PS[par * 2], self.PSt[par * 2]
                    up, upt = self.PS[par * 2 + 1], self.PSt[par * 2 + 1]
                    cnt += 1
                    for kc in range(8):
                        S.op("pe", lambda e, kc=kc, jl=jl, b=b, gp=gp: e.matmul(
                            gp.v(0, [(1, BW)]), gslot.v(kc * gtot + jl * 128, [(1, 128)]), self.Hv(kc, c0 + b * BW, BW),
                            start=(kc == 0), stop=(kc == 7)), reads=[gstile] + hts, writes=[gpt], signal=(kc == 7))
                    for kc in range(8):
                        S.op("pe", lambda e, kc=kc, jl=jl, b=b, up=up: e.matmul(
                            up.v(0, [(1, BW)]), uslot.v(kc * utot + jl * 128, [(1, 128)]), self.Hv(kc, c0 + b * BW, BW),
                            start=(kc == 0), stop=(kc == 7)), reads=[ustile] + hts, writes=[upt], signal=(kc == 7))
                    acc, acct = self.WF[par * 2], self.WFt[par * 2]
                    sg, sgt = self.WF[par * 2 + 1], self.WFt[par * 2 + 1]
                    if b == 0:
                        if half == 0:
                            if seq.sample:
                                hal, halt = self.V2T.v(jj * 24 + 16 + i * 2, [(1, 2)]), self.V2Tt
                            else:
                                hal, halt = self.HAL.v(4, [(1, 2)]), self.HALt[2]
                        else:
                            hal, halt = self.GCAR.v(jj * 2, [(1, 2)]), self.GCARt
                    else:
                        hal, halt = self.HAL.v(((cnt - 2) % 2) * 2, [(1, 2)]), self.HALt[(cnt - 2) % 2]
                    S.op("act", lambda e, gp=gp, jj=jj, acc=acc: e.activation(acc.v(0, [(1, BW)]), gp.v(0, [(1, BW)]), AF.Identity,
                                                                               bias=cbias(jj), scale=cw(2, jj)),
                         reads=[gpt, self.V2Tt], writes=[acct])
                    if b == nb - 1:
                        S.op("act", lambda e, gp=gp, jj=jj: e.copy(self.GCAR.v(jj * 2, [(1, 2)]), gp.v(BW - 2, [(1, 2)])),
                             reads=[gpt], writes=[self.GCARt])
                    else:
                        S.op("act", lambda e, gp=gp, par=par: e.copy(self.HAL.v(par * 2, [(1, 2)]), gp.v(BW - 2, [(1, 2)])),
                             reads=[gpt], writes=[self.HALt[par]])
                    S.op("dve", lambda e, gp=gp, jj=jj, acc=acc: e.scalar_tensor_tensor(
                        acc.v(1, [(1, BW - 1)]), gp.v(0, [(1, BW - 1)]), cw(1, jj), acc.v(1, [(1, BW - 1)]), ALU.mult, ALU.add),
                        reads=[gpt, acct, self.V2Tt], writes=[acct])
                    S.op("dve", lambda e, gp=gp, jj=jj, acc=acc: e.scalar_tensor_tensor(
                        acc.v(2, [(1, BW - 2)]), gp.v(0, [(1, BW - 2)]), cw(0, jj), acc.v(2, [(1, BW - 2)]), ALU.mult, ALU.add),
                        reads=[gpt, acct, self.V2Tt], writes=[acct])
                    S.op("dve", lambda e, hal=hal, jj=jj, acc=acc: e.scalar_tensor_tensor(
                        acc.v(0, [(1, 2)]), hal, cw(0, jj), acc.v(0, [(1, 2)]), ALU.mult, ALU.add),
                        reads=[halt, acct, self.V2Tt], writes=[acct])
                    S.op("dve", lambda e, hal=hal, jj=jj, acc=acc: e.scalar_tensor_tensor(
                        acc.v(0, [(1, 1)]), hal[:, 1:2], cw(1, jj), acc.v(0, [(1, 1)]), ALU.mult, ALU.add),
                        reads=[halt, acct, self.V2Tt], writes=[acct])
                    S.op("act", lambda e, acc=acc, sg=sg: e.activation(sg.v(0, [(1, BW)]), acc.v(0, [(1, BW)]), AF.Silu), reads=[acct], writes=[sgt])
                    S.op("dve", lambda e, b=b, jj=jj, up=up, sg=sg: e.tensor_tensor(self.Av(jj, b * BW, BW), sg.v(0, [(1, BW)]), up.v(0, [(1, BW)]), ALU.mult),
                         reads=[sgt, upt], writes=big)
            j += nj
        if half == seq.nh - 1:
            self.emit_state_rows(self.GCAR, self.GCARt, FC,
                                 (O["ffc_s"][i] if seq.sample else O["ffc_p"][i, seq.idx]))
        self.proj_res(seq, i, 5, I["ffn_w_out"][i], FC, lambda kc, cc, n: self.Av(kc, cc, n), big, half)
        self.set_wide(False)

    def emit_state_rows(self, car, cart, nchunk, dst):
        S = self.S
        for g0 in range(0, nchunk, 4):
            ng = min(4, nchunk - g0)
            ps, pst = self.PS[6], self.PSt[6]
            for q in range(ng):
                S.op("pe", lambda e, q=q, g0=g0: e.transpose(ps.v(q * 128, [(1, 128)], np_=2), car.v((g0 + q) * 2, [(1, 2)]), self.idf(128)),
                     reads=[cart, self.CFt], writes=[pst], signal=(q == ng - 1))
            ost = self.WF[3]
            S.op("act", lambda e, ng=ng: e.copy(ost.v(0, [(1, ng * 128)], np_=2), ps.v(0, [(1, ng * 128)], np_=2)),
                 reads=[pst], writes=[self.WFt[3], self.OST])
            S.dma("sp", dst[:, g0 * 128:(g0 + ng) * 128], ost.v(0, [(1, ng * 128)], np_=2), self.OST, False, extra_reads=[self.WFt[3]])

    def gelu(self, dst, ps_ap, pst, n, np_, dst_tiles, wa, wb):
        S = self.S
        a, at = self.WF[wa], self.WFt[wa]
        b, bt = self.WF[wb], self.WFt[wb]
        S.op("act", lambda e: e.activation(a.v(0, [(1, n)], np_=np_), ps_ap, AF.Square), reads=[pst], writes=[at])
        S.op("dve", lambda e: e.tensor_scalar(a.v(0, [(1, n)], np_=np_), a.v(0, [(1, n)], np_=np_), 0.044715, 1.0, ALU.mult, ALU.add),
             reads=[at], writes=[at])
        S.op("dve", lambda e: e.tensor_tensor(a.v(0, [(1, n)], np_=np_), a.v(0, [(1, n)], np_=np_), ps_ap, ALU.mult), reads=[at, pst], writes=[at])
        S.op("act", lambda e: e.activation(b.v(0, [(1, n)], np_=np_), a.v(0, [(1, n)], np_=np_), AF.Sigmoid, scale=1.5957691216057308),
             reads=[at], writes=[bt])
        S.op("dve", lambda e: e.tensor_tensor(dst, b.v(0, [(1, n)], np_=np_), ps_ap, ALU.mult), reads=[bt, pst], writes=dst_tiles)

    def gm_consts(self):
        S, I = self.S, self.i
        LNG = self.attf(0, 1024)
        LNB = self.attf(2048, 1024)
        BSB = self.attf(4224, 1024)
        S.dma("sp", LNG, I["gm_ln_g"].partition_broadcast(128), self.ATTq, True)
        S.dma("sp", LNB, I["gm_ln_b"].partition_broadcast(128), self.ATTk, True)
        S.dma("sp", BSB, I["gm_bs"].partition_broadcast(128), self.ATTv, True)
        S.retire([self.BIGa], [self.STG[0]])
        stg = self.bigf(0, 1024)
        S.dma("sp", stg.rearrange("p (g s) -> p g s", g=8), I["gm_ws"].rearrange("g t s -> t g s"), self.STG[0], True)
        ps, pst = self.PS[0], self.PSt[0]
        ps2, pst2 = self.PS[1], self.PSt[1]
        for g in range(8):
            pp, ppt = (ps, pst) if g < 4 else (ps2, pst2)
            S.op("pe", lambda e, g=g, pp=pp: e.transpose(pp.v((g % 4) * 128, [(1, 128)]), stg[:, g * 128:(g + 1) * 128], self.idf(128)),
                 reads=[self.STG[0], self.CFt], writes=[ppt], signal=(g % 4 == 3))
        m, mt = self.WF[0], self.WFt[0]
        S.op("dve", lambda e: e.tensor_scalar(m.v(0, [(1, 128)]), self.CF.v(128, [(1, 128)]), -1.0, 1.0, ALU.mult, ALU.add),
             reads=[self.CFt], writes=[mt])
        for hgrp in range(2):
            pp, ppt = (ps, pst) if hgrp == 0 else (ps2, pst2)
            S.op("dve", lambda e, hgrp=hgrp, pp=pp: e.tensor_tensor(
                self.ATT.v(6272 + hgrp * 512, [(128, 4), (1, 128)]), pp.v(0, [(128, 4), (1, 128)]), m.v(0, [(0, 4), (1, 128)]), ALU.mult),
                reads=[ppt, mt], writes=[self.ATTv])
        S.retire([self.STG[0]], [self.BIGa])

    def gmlp_half(self, seq, i, half):
        S, I, O = self.S, self.i, self.o
        TH, BW, nb, TW, nt = seq.TH, seq.BW, seq.nb, seq.TW, seq.nt
        c0 = half * TH
        self.norm_mod(seq, i, 0, 1, c0, TH, [half])
        self.cp("gm_norm")
        hts = [self.HBt[half]]
        wv = self.wview(I["gm_w_in"], 8)
        LNG = self.attf(0, 1024)
        LNB = self.attf(2048, 1024)
        BSB = self.attf(4224, 1024)
        big = [self.BIGa]
        panels = []
        for (cs, n) in ((1024, 384), (1408, 384), (1792, 256)):
            panels.append(self.load_panel([(wv[:, :, cs:cs + n], n)], 8) + (cs - 1024,))
        vf, vft = self.GS, self.GSt
        for t in range(nt):
            for pi, (slot, st, tot, co) in enumerate(panels):
                ps, pst = self.PS[pi], self.PSt[pi]
                for kc in range(8):
                    S.op("pe", lambda e, kc=kc, t=t, ps=ps, slot=slot, tot=tot: e.matmul(
                        ps.v(0, [(1, tot)], np_=TW), self.Hv(kc, c0 + t * TW, TW), slot.v(kc * tot, [(1, tot)]),
                        start=(kc == 0), stop=(kc == 7)), reads=[st] + hts, writes=[pst], signal=(kc == 7))
                self.gelu(vf.v(co, [(1, tot)], np_=TW), ps.v(0, [(1, tot)], np_=TW), pst, tot, TW, [vft], (pi % 2) * 2, (pi % 2) * 2 + 1)
            sm = self.SM
            S.op("dve", lambda e: e.reduce_sum(sm.v(16, [(1, 1)], np_=TW), vf.v(0, [(1, 1024)], np_=TW), axis=AX.X), reads=[vft], writes=[self.SMt])
            S.op("dve", lambda e: e.tensor_scalar(sm.v(17, [(1, 1)], np_=TW), sm.v(16, [(1, 1)], np_=TW), 1.0 / 1024.0, None, ALU.mult),
                 reads=[self.SMt], writes=[self.SMt])
            S.op("dve", lambda e: e.tensor_scalar(vf.v(0, [(1, 1024)], np_=TW), vf.v(0, [(1, 1024)], np_=TW), sm.v(17, [(1, 1)], np_=TW), None, ALU.subtract),
                 reads=[vft, self.SMt], writes=[vft])
            junk, junkt = self.WB[0], self.WBt[0]
            for hh in range(2):
                S.op("act", lambda e, hh=hh: e.activation(junk.v(0, [(1, 512)], np_=TW), vf.v(hh * 512, [(1, 512)], np_=TW), AF.Square,
                                                            accum_out=sm.v(18 + hh, [(1, 1)], np_=TW)),
                     reads=[vft], writes=[junkt, self.SMt])
            S.op("dve", lambda e: e.tensor_tensor(sm.v(20, [(1, 1)], np_=TW), sm.v(18, [(1, 1)], np_=TW), sm.v(19, [(1, 1)], np_=TW), ALU.add),
                 reads=[self.SMt], writes=[self.SMt])
            S.op("act", lambda e: e.activation(sm.v(21, [(1, 1)], np_=TW), sm.v(20, [(1, 1)], np_=TW), AF.Ln, bias=self.eps_ap()[0:TW, :], scale=1.0 / 1024.0),
                 reads=[self.SMt], writes=[self.SMt])
            S.op("act", lambda e: e.activation(sm.v(22, [(1, 1)], np_=TW), sm.v(21, [(1, 1)], np_=TW), AF.Exp, scale=-0.5),
                 reads=[self.SMt], writes=[self.SMt])
            S.op("dve", lambda e: e.scalar_tensor_tensor(vf.v(0, [(1, 1024)], np_=TW), vf.v(0, [(1, 1024)], np_=TW), sm.v(22, [(1, 1)], np_=TW),
                                                         LNG[0:TW, :], ALU.mult, ALU.mult),
                 reads=[vft, self.SMt, self.ATTq], writes=[vft])
            if seq.sample:
                S.op("dve", lambda e: e.tensor_tensor(vf.v(0, [(1, 1024)], np_=TW), vf.v(0, [(1, 1024)], np_=TW), LNB[0:TW, :], ALU.add),
                     reads=[vft, self.ATTk], writes=[vft])
                od = S.dtile("gmv_out")
                S.dma("sp", O["gmv_s"][:, :], vf.v(0, [(1, 1024)], np_=TW), od, False, extra_reads=[vft])
                S.op("act", lambda e, t=t: e.copy(self.BIG.v(8192 + t * 1024, [(1, 1024)], np_=TW), vf.v(0, [(1, 1024)], np_=TW)),
                     reads=[vft], writes=big)
            else:
                S.op("dve", lambda e, t=t: e.tensor_tensor(self.BIG.v(8192 + t * 1024, [(1, 1024)], np_=TW), vf.v(0, [(1, 1024)], np_=TW), LNB[0:TW, :], ALU.add),
                     reads=[vft, self.ATTk], writes=big)
        self.cp("gm_v")
        upan = {}
        cnt = 0
        for g in range(8):
            pidx = g // 3
            if pidx not in upan:
                n = min(384, 1024 - pidx * 384)
                upan[pidx] = self.load_panel([(wv[:, :, pidx * 384:pidx * 384 + n], n)], 8)
            slot, st, tot = upan[pidx]
            gl = g % 3
            for b in range(nb):
                ups, upst = self.PS[(cnt % 2) * 2], self.PSt[(cnt % 2) * 2]
                mps, mpst = self.PS[(cnt % 2) * 2 + 1], self.PSt[(cnt % 2) * 2 + 1]
                cnt += 1
                for kc in range(8):
                    S.op("pe", lambda e, kc=kc, gl=gl, b=b, ups=ups, slot=slot, tot=tot: e.matmul(
                        ups.v(0, [(1, BW)]), slot.v(kc * tot + gl * 128, [(1, 128)]), self.Hv(kc, c0 + b * BW, BW),
                        start=(kc == 0), stop=(kc == 7)), reads=[st] + hts, writes=[upst], signal=(kc == 7))
                ntb = BW // TW
                for tt in range(ntb):
                    t = b * ntb + tt
                    S.op("pe", lambda e, tt=tt, t=t, g=g, mps=mps: e.matmul(
                        mps.v(tt * TW, [(1, TW)]), self.BIG.v(8192 + t * 1024 + g * 128, [(1, 128)], np_=TW),
                        self.ATT.v(6272 + g * 128, [(1, TW)], np_=TW), start=True, stop=True),
                        reads=big + [self.ATTv], writes=[mpst], signal=(tt == ntb - 1))
                if cnt % 2 == 0:
                    ug, ugt = self.WF[2], self.WFt[2]
                    mx, mxt = self.WF[3], self.WFt[3]
                else:
                    ug, ugt = SubBuf(self.GS, 0), self.GSt
                    mx, mxt = SubBuf(self.GS, 512), self.GSt
                self.gelu(ug.v(0, [(1, BW)]), ups.v(0, [(1, BW)]), upst, BW, 128, [ugt], 0, 1)
                S.op("dve", lambda e, g=g, mps=mps, ntb=ntb: e.tensor_tensor(
                    mx.v(0, [(TW, ntb), (1, TW)]), mps.v(0, [(TW, ntb), (1, TW)]), self._bsb(g, ntb, TW), ALU.add),
                    reads=[mpst, self.ATTv], writes=[mxt])
                S.op("dve", lambda e, g=g, b=b: e.tensor_tensor(self.BIG.v(g * 1024 + b * BW, [(1, BW)]), mx.v(0, [(1, BW)]), ug.v(0, [(1, BW)]), ALU.mult),
                     reads=[mxt, ugt], writes=big)
        self.cp("gm_u")
        self.proj_res(seq, i, 2, I["gm_w_out"], 8, lambda kc, cc, n: self.BIG.v(kc * 1024 + cc, [(1, n)]), big, half)
        self.cp("gm_proj")

    def _bsb(self, g, ntb, TW):
        return self.ATT.v(4224 + 2 * g * 128, [(0, ntb), (1, 2 * TW)]).bitcast(F32)

    def sconv_half(self, seq, i, half):
        S, I, O = self.S, self.i, self.o
        TH, BW, nb = seq.TH, seq.BW, seq.nb
        c0 = half * TH
        self.set_wide(True)
        self.norm_mod(seq, i, 0, 1, c0, TH, [half])
        hts = [self.HBt[half]]
        wv = self.wview(I["sc_w_in"], 8)
        big = [self.BIGa]
        gs, gst = self.GS, self.GSt
        cw = lambda k, m: self.V1T.v(m * 48 + 19 + k, [(1, 1)])
        cnt = 0
        grp = {}
        for m in range(8):
            if m % 3 == 0:
                ng = min(3, 8 - m)
                grp = [self.load_panel([(wv[:, :, q * D + m * 128:q * D + (m + ng) * 128], ng * 128)], 8) for q in range(3)]
            ml = m % 3
            if half == 0:
                if seq.sample:
                    S.op("pool", lambda e, m=m: e.tensor_copy(gs.v(0, [(1, 2)]), self.V1T.v(m * 48 + 46, [(1, 2)])), reads=[self.V1Tt], writes=[gst])
                else:
                    S.op("pool", lambda e: e.memset(gs.v(0, [(1, 2)]), 0.0), writes=[gst])
            else:
                S.op("pool", lambda e, m=m: e.tensor_copy(gs.v(0, [(1, 2)]), self.PCAR.v(m * 2, [(1, 2)])), reads=[self.PCARt], writes=[gst])
            for b in range(nb):
                base = (cnt % 2) * 3
                cnt += 1
                pp = [self.PS[base + q] for q in range(3)]
                ppt = [self.PSt[base + q] for q in range(3)]
                for q in range(3):
                    slot, st, tot = grp[q]
                    for kc in range(8):
                        S.op("pe", lambda e, kc=kc, q=q, b=b, pp=pp, slot=slot, tot=tot, ml=ml: e.matmul(
                            pp[q].v(0, [(1, BW)]), slot.v(kc * tot + ml * 128, [(1, 128)]), self.Hv(kc, c0 + b * BW, BW),
                            start=(kc == 0), stop=(kc == 7)), reads=[st] + hts, writes=[ppt[q]], signal=(kc == 7))
                xs_, xst = self.WF[0], self.WFt[0]
                S.op("act", lambda e, pp=pp: e.copy(xs_.v(0, [(1, BW)]), pp[2].v(0, [(1, BW)])), reads=[ppt[2]], writes=[xst])
                S.op("dve", lambda e, b=b, pp=pp: e.tensor_tensor(gs.v(2 + b * BW, [(1, BW)]), pp[1].v(0, [(1, BW)]), xs_.v(0, [(1, BW)]), ALU.mult),
                     reads=[ppt[1], xst], writes=[gst])
                yy, yyt = self.WF[1], self.WFt[1]
                S.op("act", lambda e, b=b, m=m: e.activation(yy.v(0, [(1, BW)]), gs.v(2 + b * BW, [(1, BW)]), AF.Identity, scale=cw(2, m)),
                     reads=[gst, self.V1Tt], writes=[yyt])
                S.op("dve", lambda e, b=b, m=m: e.scalar_tensor_tensor(yy.v(0, [(1, BW)]), gs.v(1 + b * BW, [(1, BW)]), cw(1, m), yy.v(0, [(1, BW)]), ALU.mult, ALU.add),
                     reads=[gst, yyt, self.V1Tt], writes=[yyt])
                S.op("dve", lambda e, b=b, m=m: e.scalar_tensor_tensor(yy.v(0, [(1, BW)]), gs.v(b * BW, [(1, BW)]), cw(0, m), yy.v(0, [(1, BW)]), ALU.mult, ALU.add),
                     reads=[gst, yyt, self.V1Tt], writes=[yyt])
                S.op("dve", lambda e, b=b, m=m, pp=pp: e.tensor_tensor(self.BIG.v(m * 1024 + b * BW, [(1, BW)]), yy.v(0, [(1, BW)]), pp[0].v(0, [(1, BW)]), ALU.mult),
                     reads=[yyt, ppt[0]], writes=big)
            S.op("pool", lambda e, m=m: e.tensor_copy(self.PCAR.v(m * 2, [(1, 2)]), gs.v(TH, [(1, 2)])), reads=[gst], writes=[self.PCARt])
        if half == seq.nh - 1:
            self.emit_state_rows(self.PCAR, self.PCARt, 8, (O["sc_s"] if seq.sample else O["sc_p"][seq.idx]))
        self.proj_res(seq, i, 2, I["sc_w_out"], 8, lambda kc, cc, n: self.BIG.v(kc * 1024 + cc, [(1, n)]), big, half)
        self.set_wide(False)

    def attn_layer(self, seq, i, kind):
        S, I, O = self.S, self.i, self.o
        T, TW = seq.T, seq.TW
        self.norm_mod(seq, i, 0, 1, 0, T, list(range(seq.nh)))
        hts = self.HBt[:seq.nh]
        wqkv = I["diff_w_qkv"] if kind == "diff" else I["sb_w_qkv"]
        if kind == "diff":
            ok = O["dk_s"] if seq.sample else O["dk_p"][seq.idx]
            ov = O["dv_s"] if seq.sample else O["dv_p"][seq.idx]
            ck, cv = I["cdk"], I["cdv"]
        else:
            ok = O["sbk_s"] if seq.sample else O["sbk_p"][seq.idx]
            ov = O["sbv_s"] if seq.sample else O["sbv_p"][seq.idx]
            ck, cv = I["csk"], I["csv"]
        ntile = T // TW
        nkt_cache = 16 if seq.sample else 0
        S.retire([self.BIGa, self.BIGb], [self.KSTt, self.VSTt])
        QT0, KT0, VB0 = 0, 2048, 4224
        vw = 129 if kind == "diff" else 256
        big = [self.BIGa]
        if kind == "sb":
            S.op("pool", lambda e: e.memset(self.ATT.v(VB0, [(1, 17 * 256)]), 0.0), writes=[self.ATTv])
        else:
            S.op("pool", lambda e: e.memset(self.ATT.v(VB0, [(1, 17 * 129)]), 1.0), writes=[self.ATTv])
        def qkv_panel(hh):
            slot_, st_, _ = self.load_panel([(wqkv[hh], 3 * D)], 1)
            return slot_, st_, 384
        nxt = qkv_panel(0)
        for h in range(8):
            slot, st, tot = nxt
            if seq.sample:
                kst = self.bigf(16384, 2048).rearrange("p (t c) -> p t c", t=16)
                S.dma("sp", kst, ck.rearrange("(t p) c -> p t c", p=128)[:, :, h * 128:(h + 1) * 128], self.KSTt, True, extra_writes=[self.VSTt])
                for t in range(16):
                    ps, pst = self.PS[t % 2], self.PSt[t % 2]
                    S.op("pe", lambda e, t=t, ps=ps: e.transpose(ps.v(0, [(1, 128)]), kst[:, t, :], self.idf(128)),
                         reads=[self.KSTt, self.CFt], writes=[pst])
                    S.op("act" if t % 2 == 0 else "dve",
                         (lambda e, t=t, ps=ps: e.copy(self.ATT.v(KT0 + t * 128, [(1, 128)]), ps.v(0, [(1, 128)]))) if t % 2 == 0 else
                         (lambda e, t=t, ps=ps: e.tensor_copy(self.ATT.v(KT0 + t * 128, [(1, 128)]), ps.v(0, [(1, 128)]))),
                         reads=[pst], writes=[self.ATTk])
                cvv = cv.rearrange("(t p) c -> p t c", p=128)
                if kind == "diff":
                    S.dma("pool", self.ATT.v(VB0, [(129, 16), (1, 128)]), cvv[:, :, h * 128:(h + 1) * 128], self.ATTv, True)
                else:
                    S.dma("pool", self.ATT.v(VB0, [(256, 16), (1, 64)]), cvv[:, :, h * 128:h * 128 + 64], self.ATTv, True)
                    S.dma("pool", self.ATT.v(VB0 + 128 + 64, [(256, 16), (1, 64)]), cvv[:, :, h * 128 + 64:(h + 1) * 128], self.ATTv, True)
            def qkv_mm(t):
                ps, pst = self.PS[t % 2], self.PSt[t % 2]
                for kc in range(8):
                    S.op("pe", lambda e, kc=kc, t=t, ps=ps, slot=slot, tot=tot: e.matmul(
                        ps.v(0, [(1, 384)], np_=TW), self.Hv(kc, t * TW, TW), slot.v(kc * tot, [(1, 384)]),
                        start=(kc == 0), stop=(kc == 7)), reads=[st] + hts, writes=[pst], signal=(kc == 7))

            def qkv_post(t):
                ps, pst = self.PS[t % 2], self.PSt[t % 2]
                qk, qkt = self.QKVs[t % 2], self.QKVts[t % 2]
                S.op("act", lambda e, ps=ps: e.copy(qk.v(0, [(1, 384)], np_=TW), ps.v(0, [(1, 384)], np_=TW)), reads=[pst], writes=[qkt])
                if kind == "diff":
                    rt = nkt_cache + t if seq.sample else t
                    cosv = self.ROPE.v(rt * 16, [(0, 4), (1, 8)], np_=TW)
                    sinv = self.ROPE.v(rt * 16 + 8, [(0, 4), (1, 8)], np_=TW)
                    x1 = qk.v(0, [(64, 4), (1, 8)], np_=TW)
                    x2 = qk.v(8, [(64, 4), (1, 8)], np_=TW)
                    tm, tmt = self.WF[t % 2], self.WFt[t % 2]
                    t1 = tm.v(0, [(8, 4), (1, 8)], np_=TW)
                    t2 = tm.v(32, [(8, 4), (1, 8)], np_=TW)
                    t3 = tm.v(64, [(8, 4), (1, 8)], np_=TW)
                    t4 = tm.v(96, [(8, 4), (1, 8)], np_=TW)
                    S.op("dve", lambda e: e.tensor_tensor(t1, x1, cosv, ALU.mult), reads=[qkt, self.ROPEt], writes=[tmt])
                    S.op("dve", lambda e: e.tensor_tensor(t2, x2, sinv, ALU.mult), reads=[qkt, self.ROPEt], writes=[tmt])
                    S.op("dve", lambda e: e.tensor_tensor(t3, x2, cosv, ALU.mult), reads=[qkt, self.ROPEt], writes=[tmt])
                    S.op("dve", lambda e: e.tensor_tensor(t4, x1, sinv, ALU.mult), reads=[qkt, self.ROPEt], writes=[tmt])
                    S.op("dve", lambda e: e.tensor_tensor(x1, t1, t2, ALU.subtract), reads=[tmt], writes=[qkt])
                    S.op("dve", lambda e: e.tensor_tensor(x2, t3, t4, ALU.add), reads=[tmt], writes=[qkt])
                if seq.sample:
                    osb, osbt = self.WF[3], self.WFt[3]
                    S.op("act", lambda e: e.copy(osb.v(0, [(1, 256)], np_=TW), qk.v(128, [(1, 256)], np_=TW)), reads=[qkt], writes=[osbt, self.OST])
                    S.dma("sp", ok[:, h * 128:(h + 1) * 128], osb.v(0, [(1, 128)], np_=TW), self.OST, False, extra_reads=[osbt])
                    S.dma("sp", ov[:, h * 128:(h + 1) * 128], osb.v(128, [(1, 128)], np_=TW), self.OST, False, extra_reads=[osbt])
                else:
                    tl = t % 8
                    kstv = self.bigf(16384, 1024)
                    vstv = self.bigf(18432, 1024)
                    S.op("act", lambda e, tl=tl: e.copy(kstv[:, tl * 128:(tl + 1) * 128], qk.v(128, [(1, 128)])), reads=[qkt], writes=[self.KSTt])
                    S.op("pool", lambda e, tl=tl: e.tensor_copy(vstv[:, tl * 128:(tl + 1) * 128], qk.v(256, [(1, 128)])), reads=[qkt], writes=[self.VSTt])
                    if tl == 7:
                        r0 = (t - 7) * 128
                        S.dma("sp", ok[r0:r0 + 1024, :].rearrange("(t p) c -> p t c", p=128)[:, :, h * 128:(h + 1) * 128],
                              kstv.rearrange("p (t c) -> p t c", t=8), self.KSTt, False)
                        S.dma("sp", ov[r0:r0 + 1024, :].rearrange("(t p) c -> p t c", p=128)[:, :, h * 128:(h + 1) * 128],
                              vstv.rearrange("p (t c) -> p t c", t=8), self.VSTt, False)
                kt_idx = nkt_cache + t
                qb16, qb16t = self.QKB[t % 2], self.QKBt[t % 2]
                S.op("dve", lambda e: e.tensor_copy(qb16.v(0, [(1, 256)], np_=TW), qk.v(0, [(1, 256)], np_=TW)), reads=[qkt], writes=[qb16t])
                for which, off, dst0, dtile in ((0, 0, QT0 + t * TW, self.ATTq), (1, 128, KT0 + kt_idx * 128, self.ATTk)):
                    ps2, pst2 = self.PS[2 + which + 2 * (t % 2)], self.PSt[2 + which + 2 * (t % 2)]
                    pv16 = ps2.v(0, [(1, (TW + 1) // 2)]).bitcast(BF16)[:, 0:TW]
                    S.op("pe", lambda e, off=off, pv16=pv16: e.transpose(pv16, qb16.v(off, [(1, 128)], np_=TW), self.cb("ident", k=TW, m=TW)),
                         reads=[qb16t, self.CBt], writes=[pst2])
                    if which == 0:
                        S.op("act", lambda e, dst0=dst0, pv16=pv16: e.copy(self.ATT.v(dst0, [(1, TW)]), pv16), reads=[pst2], writes=[dtile])
                    else:
                        S.op("dve", lambda e, dst0=dst0, pv16=pv16: e.tensor_copy(self.ATT.v(dst0, [(1, TW)]), pv16), reads=[pst2], writes=[dtile])
                if kind == "diff":
                    S.op("pool", lambda e, kt_idx=kt_idx: e.tensor_copy(self.ATT.v(VB0 + kt_idx * 129, [(1, 128)], np_=TW), qk.v(256, [(1, 128)], np_=TW)),
                         reads=[qkt], writes=[self.ATTv])
                else:
                    S.op("pool", lambda e, kt_idx=kt_idx: e.tensor_copy(self.ATT.v(VB0 + kt_idx * 256, [(1, 64)], np_=TW), qk.v(256, [(1, 64)], np_=TW)),
                         reads=[qkt], writes=[self.ATTv])
                    S.op("pool", lambda e, kt_idx=kt_idx: e.tensor_copy(self.ATT.v(VB0 + kt_idx * 256 + 192, [(1, 64)], np_=TW), qk.v(320, [(1, 64)], np_=TW)),
                         reads=[qkt], writes=[self.ATTv])

            qkv_mm(0)
            for t in range(ntile):
                if t + 1 < ntile:
                    qkv_mm(t + 1)
                qkv_post(t)
            if h + 1 < 8:
                nxt = qkv_panel(h + 1)
            if kind == "diff":
                self.diff_attend(seq, h, QT0, KT0, VB0, nkt_cache)
            else:
                self.sb_attend(seq, h, QT0, KT0, VB0, nkt_cache)
        S.retire([self.KSTt, self.VSTt], [self.BIGb])
        wout = I["diff_w_out"] if kind == "diff" else I["sb_w_out"]
        for half in range(seq.nh):
            hc = half * seq.TH
            self.proj_res(seq, i, 2, wout, 8, lambda kc, cc, n, hc=hc: self.Ov(kc, hc + cc, n), big, half)

    def diff_attend(self, seq, h, QT0, KT0, VB0, nkc):
        S = self.S
        T, TW = seq.T, seq.TW
        QB = min(512, T)
        nqb = T // QB
        nqt = QB // TW
        big = [self.BIGa]
        scale = 0.125
        ops_ = [self.PS[5], self.PS[6], self.PS[7]]
        opst = [self.PSt[5], self.PSt[6], self.PSt[7]]

        def oacc(c, qt):
            a = c * nqt + qt
            return ops_[a // 3].v((a % 3) * 160, [(1, 129)], np_=TW), opst[a // 3]

        pending = []

        def flush_pending():
            while pending:
                pending.pop(0)()

        for qb in range(nqb):
            if seq.sample:
                ktiles = [(kt, 128) for kt in range(16)] + [(16, 16)]
            else:
                ktiles = [(kt, 128) for kt in range(qb * 4 + 4)]
            nbank = (2 * nqt + 2) // 3
            for rep in range(self.warm if not seq.sample else 1):
                for k in range(nbank):
                    S.op("pe", lambda e, k=k: e.matmul(ops_[k].v(0, [(1, 512)]), self.cb("zero"), self.CB.v(0, [(1, 512)]), start=True, stop=False, skip_group_check=True),
                         reads=[self.CBt], writes=[opst[k]], signal=False)
            units = []
            for ki, (kt, kr) in enumerate(ktiles):
                for c in range(2):
                    units.append((ki, kt, kr, c))

            def geom(kt):
                j = kt - qb * 4 if not seq.sample else -1
                cstart = j * 128 if j >= 0 else 0
                return j, cstart, QB - cstart

            def s_mm(u):
                ki, kt, kr, c = u
                j, cstart, ncol = geom(kt)
                sp, spt = self.PS[c * 2 + (ki % 2)], self.PSt[c * 2 + (ki % 2)]
                S.op("pe", lambda e: e.matmul(
                    sp.v(cstart, [(1, ncol)], np_=kr), self.ATT.v(KT0 + kt * 128, [(1, kr)], p0=c * 64, np_=64),
                    self.ATT.v(QT0 + qb * QB + cstart, [(1, ncol)], p0=c * 64, np_=64), start=True, stop=True),
                    reads=[self.ATTq, self.ATTk], writes=[spt])

            def ex(u):
                ki, kt, kr, c = u
                j, cstart, ncol = geom(kt)
                sp, spt = self.PS[c * 2 + (ki % 2)], self.PSt[c * 2 + (ki % 2)]
                pb, pbt = self.WB[c * 2 + (ki % 2)], self.WBt[c * 2 + (ki % 2)]
                S.op("act", lambda e: e.activation(
                    pb.v(cstart, [(1, ncol)], np_=kr), sp.v(cstart, [(1, ncol)], np_=kr), AF.Exp, scale=scale),
                    reads=[spt], writes=[pbt])
                if j >= 0:
                    S.op("pool", lambda e: e.memset(pb.v(cstart, [(1, 64)], p0=64, np_=64), 0.0), writes=[pbt])

            def pv(u):
                ki, kt, kr, c = u
                j, cstart, ncol = geom(kt)
                pb, pbt = self.WB[c * 2 + (ki % 2)], self.WBt[c * 2 + (ki % 2)]
                for qt in range(nqt):
                    if j >= 0 and qt < j:
                        continue
                    gqt = qb * nqt + qt
                    last_kt = (16 if seq.sample else gqt)
                    oap, oat = oacc(c, qt)
                    S.op("pe", lambda e, oap=oap, qt=qt, last_kt=last_kt: e.matmul(
                        oap, pb.v(qt * TW, [(1, TW)], np_=kr), self.ATT.v(VB0 + kt * 129, [(1, 129)], np_=kr),
                        start=False, stop=(kt == last_kt), skip_group_check=True),
                        reads=[pbt, self.ATTv], writes=[oat], signal=(kt == last_kt))

            nu_ = len(units)
            s_mm(units[0])
            s_mm(units[1])
            for idx in range(nu_ + 1):
                if idx % 2 == 1 and idx + 1 < nu_:
                    s_mm(units[idx + 1])
                    s_mm(units[idx + 2])
                if idx >= 1:
                    pv(units[idx - 1])
                if idx < nu_:
                    ex(units[idx])
                if idx == 6:
                    flush_pending()
            flush_pending()
            nbank = (2 * nqt + 2) // 3
            for k in range(nbank):
                S.op("act", lambda e, k=k: e.copy(self.WF[1 + k].v(0, [(1, 480)], np_=TW), ops_[k].v(0, [(1, 480)], np_=TW)),
                     reads=[opst[k]], writes=[self.WFt[1 + k]])

            def sacc(c, qt):
                a_ = c * nqt + qt
                return self.WF[1 + a_ // 3].v((a_ % 3) * 160, [(1, 129)], np_=TW), self.WFt[1 + a_ // 3]
            for qt in range(nqt):
                o0, o0t = sacc(0, qt)
                o1, o1t = sacc(1, qt)
                sm = self.SM
                av = o0[:, 0:128]
                at = o0t
                S.op("dve", lambda e, o0=o0: e.reciprocal(sm.v(24, [(1, 1)], np_=TW), o0[:, 128:129]), reads=[o0t], writes=[self.SMt])
                S.op("dve", lambda e, o1=o1: e.reciprocal(sm.v(25, [(1, 1)], np_=TW), o1[:, 128:129]), reads=[o1t], writes=[self.SMt])
                S.op("dve", lambda e: e.tensor_tensor(sm.v(26, [(1, 1)], np_=TW), sm.v(25, [(1, 1)], np_=TW), sm.v(0, [(1, 1)], np_=TW), ALU.mult),
                     reads=[self.SMt], writes=[self.SMt])
                S.op("dve", lambda e, av=av: e.tensor_scalar(av, av, sm.v(24, [(1, 1)], np_=TW), None, ALU.mult),
                     reads=[o0t, self.SMt], writes=[at])
                S.op("dve", lambda e, o1=o1, av=av: e.scalar_tensor_tensor(av, o1[:, 0:128], sm.v(26, [(1, 1)], np_=TW), av, ALU.mult, ALU.add),
                     reads=[o1t, self.SMt, at], writes=[at])
                jk, jkt = self.WF[0], self.WFt[0]
                S.op("act", lambda e, av=av: e.activation(jk.v(0, [(1, 128)], np_=TW), av, AF.Square,
                                                          accum_out=sm.v(27, [(1, 1)], np_=TW)), reads=[at], writes=[jkt, self.SMt])
                S.op("act", lambda e: e.activation(sm.v(28, [(1, 1)], np_=TW), sm.v(27, [(1, 1)], np_=TW), AF.Ln, bias=self.eps_ap()[0:TW, :], scale=1.0 / 128.0),
                     reads=[self.SMt], writes=[self.SMt])
                S.op("act", lambda e: e.activation(sm.v(29, [(1, 1)], np_=TW), sm.v(28, [(1, 1)], np_=TW), AF.Exp, scale=-0.5),
                     reads=[self.SMt], writes=[self.SMt])
                S.op("dve", lambda e, av=av: e.scalar_tensor_tensor(av, av, sm.v(29, [(1, 1)], np_=TW),
                                                                    self.GSUB.v(0, [(1, 128)], np_=TW), ALU.mult, ALU.mult),
                     reads=[at, self.SMt, self.GSUBt], writes=[at])
                col = qb * QB + qt * TW

                def tr(av=av, col=col, at=at):
                    tp, tpt = self.PS[4], self.PSt[4]
                    S.op("pe", lambda e: e.transpose(tp.v(0, [(1, TW)]), av, self.idf(TW)), reads=[at, self.CFt], writes=[tpt])
                    S.op("act", lambda e: e.copy(self.Ov(h, col, TW), tp.v(0, [(1, TW)])), reads=[tpt], writes=big)
                pending.append(tr)
        flush_pending()

    def sb_attend(self, seq, h, QT0, KT0, VB0, nkc):
        S = self.S
        T, TW = seq.T, seq.TW
        QB = min(512, T)
        nqb = T // QB
        big = [self.BIGa]
        ET = [self.WF[0], self.WF[1]]
        ETt = [self.WFt[0], self.WFt[1]]
        NL = [self.WF[2], self.WF[3]]
        NLt = [self.WFt[2], self.WFt[3]]
        NB = [self.WB[0], self.WB[1]]
        NBt = [self.WBt[0], self.WBt[1]]
        WT = [self.WB[2], self.WB[3]]
        WTt = [self.WBt[2], self.WBt[3]]
        for qb in range(nqb):
            if seq.sample:
                ktiles = [(16, 16, 0)] + [(kt, 128, -1) for kt in range(15, -1, -1)]
            else:
                ktiles = [(kt, 128, kt - qb * 4) for kt in range(qb * 4 + 3, -1, -1)]
            op_, opt = self.PS[6 + (qb % 2)], self.PSt[6 + (qb % 2)]
            for rep in range(self.warm if not seq.sample else 1):
                S.op("pe", lambda e: e.matmul(op_.v(0, [(1, QB)]), self.cb("zero"), self.CB.v(0, [(1, QB)]), start=True, stop=False, skip_group_check=True),
                     reads=[self.CBt], writes=[opt], signal=False)
                for c in range(2):
                    S.op("pe", lambda e, c=c: e.matmul(self.PS[4 + c].v(0, [(1, QB)]), self.cb("zero"), self.CB.v(0, [(1, QB)]), start=True, stop=False, skip_group_check=True),
                         reads=[self.CBt], writes=[self.PSt[4 + c]], signal=False)
            units = []
            for ki, (kt, kr, j) in enumerate(ktiles):
                for c in range(2):
                    units.append((ki, kt, kr, j, c))
            nu = len(units)

            def geom(u):
                ki, kt, kr, j, c = u
                cstart = j * 128 if (j >= 0 and not seq.sample) else 0
                return cstart, QB - cstart

            def zmm(u):
                ki, kt, kr, j, c = u
                cstart, ncol = geom(u)
                zp, zpt = self.PS[c * 2 + (ki % 2)], self.PSt[c * 2 + (ki % 2)]
                S.op("pe", lambda e: e.matmul(
                    zp.v(cstart, [(1, ncol)], np_=kr), self.ATT.v(KT0 + kt * 128, [(1, kr)], p0=c * 64, np_=64),
                    self.ATT.v(QT0 + qb * QB + cstart, [(1, ncol)], p0=c * 64, np_=64), start=True, stop=True),
                    reads=[self.ATTq, self.ATTk], writes=[zpt])

            def a_act(u):
                ki, kt, kr, j, c = u
                cstart, ncol = geom(u)
                zp, zpt = self.PS[c * 2 + (ki % 2)], self.PSt[c * 2 + (ki % 2)]
                sl = lambda B: B.v(cstart, [(1, ncol)], np_=kr)
                S.op("act", lambda e: e.activation(sl(ET[c]), sl(zp), AF.Exp, scale=0.125), reads=[zpt], writes=[ETt[c]])
                S.op("act", lambda e: e.activation(sl(NL[c]), sl(ET[c]), AF.Ln, bias=self.SM.v(9, [(1, 1)], np_=kr)),
                     reads=[ETt[c], self.SMt], writes=[NLt[c]])

            def a_dve(u):
                ki, kt, kr, j, c = u
                cstart, ncol = geom(u)
                zp, zpt = self.PS[c * 2 + (ki % 2)], self.PSt[c * 2 + (ki % 2)]
                sl = lambda B: B.v(cstart, [(1, ncol)], np_=kr)
                if j >= 0:
                    msk = self.CF.v(256, [(1, TW)], np_=kr)
                    S.op("dve", lambda e: e.tensor_tensor(
                        NL[c].v(cstart, [(1, TW)], np_=kr), NL[c].v(cstart, [(1, TW)], np_=kr), msk, ALU.mult),
                        reads=[NLt[c], self.CFt], writes=[NLt[c]])
                S.op("dve", lambda e: e.tensor_copy(sl(NB[c]), sl(NL[c])), reads=[NLt[c]], writes=[NBt[c]])
                S.op("dve", lambda e: e.scalar_tensor_tensor(sl(ET[c]), sl(zp), 0.125, sl(NL[c]), ALU.mult, ALU.subtract),
                     reads=[zpt, NLt[c]], writes=[ETt[c]])

            def stage_b1(u):
                ki, kt, kr, j, c = u
                cstart, ncol = geom(u)
                sp_, spt_ = self.PS[4 + c], self.PSt[4 + c]
                S.op("pe", lambda e: e.matmul(
                    sp_.v(cstart, [(1, ncol)], np_=kr), self.cb("tri", k=kr, m=kr), NB[c].v(cstart, [(1, ncol)], np_=kr), start=False, stop=False, skip_group_check=True),
                    reads=[NBt[c], self.CBt], writes=[spt_])
                sl = lambda B: B.v(cstart, [(1, ncol)], np_=kr)
                S.op("dve", lambda e: e.tensor_tensor(sl(ET[c]), sl(ET[c]), sl(sp_), ALU.subtract), reads=[ETt[c], spt_], writes=[ETt[c]])

            def stage_b2(u, idx):
                ki, kt, kr, j, c = u
                cstart, ncol = geom(u)
                sp_, spt_ = self.PS[4 + c], self.PSt[4 + c]
                sl = lambda B: B.v(cstart, [(1, ncol)], np_=kr)
                S.op("pe", lambda e: e.matmul(
                    sp_.v(cstart, [(1, ncol)]), self.cb("trile", k=kr, m=128), NB[c].v(cstart, [(1, ncol)], np_=kr), start=False, stop=(idx >= nu - 2), skip_group_check=True),
                    reads=[NBt[c], self.CBt], writes=[spt_])
                S.op("act", lambda e: e.activation(sl(WT[c]), sl(ET[c]), AF.Exp), reads=[ETt[c]], writes=[WTt[c]])
                if j >= 0:
                    mskb = self.cb("msu", k=kr, m=TW)
                    S.op("pool", lambda e: e.tensor_tensor(
                        WT[c].v(cstart, [(1, TW)], np_=kr), WT[c].v(cstart, [(1, TW)], np_=kr), mskb, ALU.mult), reads=[WTt[c], self.CBt], writes=[WTt[c]])

            def stage_c(u, idx):
                ki, kt, kr, j, c = u
                cstart, ncol = geom(u)
                last = (idx == nu - 1)
                S.op("pe", lambda e: e.matmul(
                    op_.v(cstart, [(1, ncol)]), self.ATT.v(VB0 + kt * 256 + c * 128, [(1, 128)], np_=kr), WT[c].v(cstart, [(1, ncol)], np_=kr),
                    start=False, stop=last, skip_group_check=True), reads=[WTt[c], self.ATTv], writes=[opt], signal=last)

            zmm(units[0])
            zmm(units[1])
            for s_ in range(nu + 2):
                if s_ % 2 == 1 and s_ + 1 < nu:
                    zmm(units[s_ + 1])
                    zmm(units[s_ + 2])
                if s_ < nu:
                    a_act(units[s_])
                if 0 <= s_ - 1 < nu:
                    stage_b1(units[s_ - 1])
                if s_ < nu:
                    a_dve(units[s_])
                if 0 <= s_ - 2 < nu:
                    stage_c(units[s_ - 2], s_ - 2)
                if 0 <= s_ - 1 < nu:
                    stage_b2(units[s_ - 1], s_ - 1)
            S.op("act", lambda e, qb=qb: e.copy(self.Ov(h, qb * QB, QB), op_.v(0, [(1, QB)])), reads=[opt], writes=big)

    def run_seq(self, seq):
        self.load_x(seq)
        self.dump(f"{seq.name}_x", self.X.v(0, [(1, 8 * SEQ)]), self.Xt)
        for i in range(self.nlayers):
            kind = i % 4
            if kind == 0:
                self.gm_consts()
                self.cp("gm_consts")
                for half in range(seq.nh):
                    self.gmlp_half(seq, i, half)
            elif kind == 1:
                self.attn_layer(seq, i, "diff")
            elif kind == 2:
                for half in range(seq.nh):
                    self.sconv_half(seq, i, half)
            else:
                self.attn_layer(seq, i, "sb")
            if self.ada_next == i + 1 and i + 1 < 4:
                self.ada_layer(i + 1)
                self.ada_next = i + 2
            self.dump(f"{seq.name}_xm{i}", self.X.v(0, [(1, 8 * SEQ)]), self.Xt)
            for half in range(seq.nh):
                self.ffn_half(seq, i, half)
            self.dump(f"{seq.name}_xf{i}", self.X.v(0, [(1, 8 * SEQ)]), self.Xt)
        self.store_y(seq)

    def build(self):
        self.declare()
        S = self.S
        S.op("pool", lambda e: e.memset(self.SM.v(8, [(1, 1)]), EPS), writes=[self.SMt])
        S.op("pool", lambda e: e.memset(self.SM.v(9, [(1, 1)]), 1.0), writes=[self.SMt])
        self.setup()
        cfgs = {"p0": SeqCfg("p0", SEQ, 0, False, 0), "p1": SeqCfg("p1", SEQ, 1, False, 1), "s": SeqCfg("s", DEC, 2, True, 0)}
        try:
            for n in self.seq_names:
                self.run_seq(cfgs[n])
        except StopBuild:
            pass
        for name, (fn) in getattr(self, "dumps", {}).items():
            pass
        S.finish()
        self.stack.close()
        return self.nc


def _consts():
    p = np.arange(128)
    ident = (p[:, None] == p[None, :]).astype(np.float32)
    tri = (p[:, None] > p[None, :]).astype(np.float32)
    msu = (p[:, None] < p[None, :]).astype(np.float32)
    cst = np.concatenate([ident, tri, msu, np.zeros((128, 128), np.float32)], axis=1)
    pos = np.arange(17 * 128, dtype=np.float32)
    inv = (500000.0 ** (-np.arange(0, 16, 2, dtype=np.float32) / 16.0)).astype(np.float32)
    ang = pos[:, None] * inv[None, :]
    rope = np.concatenate([np.cos(ang), np.sin(ang)], axis=1).astype(np.float32)
    return np.ascontiguousarray(cst), np.ascontiguousarray(rope)


def _pack_qkv(w):
    w = np.asarray(w, dtype=np.float32).reshape(8, 128, 3, 8, 128)
    return np.ascontiguousarray(w.transpose(3, 1, 0, 2, 4).reshape(8, 128, 3 * D))


def _pack_wout(w):
    w = np.asarray(w, dtype=np.float32).reshape(4, FC, 128, 8, 128)
    return np.ascontiguousarray(w.transpose(0, 3, 2, 1, 4).reshape(4, 8, 128, DFF))


def make_in_maps(inp):
    f = lambda a: np.ascontiguousarray(np.asarray(a, dtype=np.float32))
    cst, rope = _consts()
    shared = {
        "cst": cst, "rope": rope,
        "ada_w": f(inp["ada_w"]),
        "gm_w_in": f(inp["gm_w_in"][0]), "gm_ln_g": f(inp["gm_ln_g"][0:1]), "gm_ln_b": f(inp["gm_ln_b"][0:1]),
        "gm_ws": f(inp["gm_ws"][0]), "gm_bs": f(inp["gm_bs"][0].reshape(1, D)), "gm_w_out": f(inp["gm_w_out"][0]),
        "diff_w_qkv": _pack_qkv(inp["diff_w_qkv"][0]), "diff_lambda": f(inp["diff_lambda"][0].reshape(1, 256)),
        "diff_subln_g": f(inp["diff_subln_g"][0:1]), "diff_w_out": f(inp["diff_w_out"][0]),
        "sc_w_in": f(inp["sc_w_in"][0]), "sc_w_out": f(inp["sc_w_out"][0]),
        "sb_w_qkv": _pack_qkv(inp["sb_w_qkv"][0]), "sb_w_out": f(inp["sb_w_out"][0]),
        "ffn_w_in": f(inp["ffn_w_in"]), "ffn_w_out": _pack_wout(inp["ffn_w_out"]),
    }
    maps = []
    for c in range(NCORES):
        r1 = np.concatenate([
            inp["norm_g"].reshape(16, D),
            inp["c_prompt"][2 * c:2 * c + 2], inp["c_sample"][c:c + 1],
            inp["sc_conv_w"][0],
            inp["ada_b"].reshape(24, D),
            inp["state_sconv"][0, c],
        ], axis=0)
        r2 = np.concatenate([
            inp["ffn_conv_w"].reshape(12, DFF),
            inp["ffn_conv_b"],
            inp["state_ffn_conv"][:, c].reshape(8, DFF),
        ], axis=0)
        m = dict(shared)
        m.update({
            "xp": f(inp["x_prompt"][2 * c:2 * c + 2]),
            "xs": f(inp["x_sample"][c]),
            "cdk": f(inp["cache_diff_k"][0, c].reshape(SEQ, D)),
            "cdv": f(inp["cache_diff_v"][0, c].reshape(SEQ, D)),
            "csk": f(inp["cache_sb_k"][0, c].reshape(SEQ, D)),
            "csv": f(inp["cache_sb_v"][0, c].reshape(SEQ, D)),
            "r1": f(r1), "r2": f(r2),
        })
        maps.append(m)
    return maps


_NC_CACHE = {}


def get_program(**kw):
    key = repr(sorted(kw.items()))
    if key not in _NC_CACHE:
        _NC_CACHE[key] = KB(**kw).build()
    return _NC_CACHE[key]


def assemble(results):
    cat = lambda k: np.concatenate([r[k] for r in results], axis=0)
    stack = lambda k: np.stack([r[k] for r in results], axis=0)
    y_p = cat("y_p")
    y_s = stack("y_s")
    gmv_s = stack("gmv_s")[None]
    dk_p = cat("dk_p").reshape(1, 16, SEQ, 8, 2, 64)
    dv_p = cat("dv_p").reshape(1, 16, SEQ, 8, 128)
    dk_s = stack("dk_s").reshape(1, 8, DEC, 8, 2, 64)
    dv_s = stack("dv_s").reshape(1, 8, DEC, 8, 128)
    sc_p = cat("sc_p")[None]
    sc_s = stack("sc_s")[None]
    sbk_p = cat("sbk_p").reshape(1, 16, SEQ, 16, 64)
    sbv_p = cat("sbv_p").reshape(1, 16, SEQ, 16, 64)
    sbk_s = stack("sbk_s").reshape(1, 8, DEC, 16, 64)
    sbv_s = stack("sbv_s").reshape(1, 8, DEC, 16, 64)
    ffc_p = np.concatenate([r["ffc_p"] for r in results], axis=1)
    ffc_s = np.stack([r["ffc_s"] for r in results], axis=1)
    outs = (y_p, y_s, gmv_s, dk_p, dv_p, dk_s, dv_s, sc_p, sc_s, sbk_p, sbv_p, sbk_s, sbv_s, ffc_p, ffc_s)
    return tuple(np.ascontiguousarray(o, dtype=np.float32) for o in outs)


def kernel(**inputs):
    inp = {k: np.asarray(v) for k, v in inputs.items()}
    nc = get_program()
    maps = make_in_maps(inp)
    res = run_bass_kernel_spmd(nc, maps, core_ids=list(range(NCORES)))
    return assemble(res.results)
```

```python
import math
import contextlib
import numpy as np
import concourse.bass as bass
import concourse.mybir as mybir
from concourse.bass_utils import run_bass_kernel_spmd

F32 = mybir.dt.float32
BF16 = mybir.dt.bfloat16
ALU = mybir.AluOpType
AF = mybir.ActivationFunctionType
AX = mybir.AxisListType

NCORES = 8
D = 1024
KC = 8
DFF = 2816
FC = 22
SEQ = 2048
DEC = 16
EPS = 1e-6
LAM_INIT = 0.8 - 0.6 * math.exp(-0.3 * 1)


class Tile:
    __slots__ = ("name", "w", "r")

    def __init__(self, name):
        self.name = name
        self.w = None
        self.r = {}


class DTile(Tile):
    __slots__ = ("semkey", "count")

    def __init__(self, name, semkey):
        super().__init__(name)
        self.semkey = semkey
        self.count = 0


class _Eng:
    def __init__(self, name, obj, semkey):
        self.name = name
        self.obj = obj
        self.semkey = semkey
        self.count = 0
        self.known = {}
        self.ninst = 0


class Sched:
    def __init__(self, nc, stack):
        self.nc = nc
        self.stack = stack
        self.sems = []
        self.eng = {}
        for name, obj in (("pe", nc.tensor), ("act", nc.scalar), ("dve", nc.vector),
                          ("pool", nc.gpsimd), ("sp", nc.sync)):
            k = self.new_sem("e_" + name)
            self.eng[name] = _Eng(name, obj, k)
        self.dtiles = []

    def new_sem(self, name):
        h = self.stack.enter_context(self.nc.semaphore(name))
        self.sems.append(h)
        return len(self.sems) - 1

    def tile(self, name):
        return Tile(name)

    def dtile(self, name):
        t = DTile(name, self.new_sem("d_" + name))
        self.dtiles.append(t)
        return t

    def retire(self, srcs, dsts):
        for d in dsts:
            for s in srcs:
                if s is d:
                    continue
                if s.w is not None:
                    if d.r.get(s.w[0], 0) < s.w[1]:
                        d.r[s.w[0]] = s.w[1]
                for k, v in s.r.items():
                    if d.r.get(k, 0) < v:
                        d.r[k] = v

    def _deps(self, E, reads, writes):
        deps = {}

        def add(k, v, raw):
            if k == E.semkey and not raw:
                if E.name == "pe" or v > E.count:
                    return
            if E.known.get(k, 0) >= v:
                return
            if deps.get(k, 0) < v:
                deps[k] = v

        for t in reads:
            if t.w is not None:
                add(t.w[0], t.w[1], True)
        for t in writes:
            if t.w is not None:
                add(t.w[0], t.w[1], False)
            for k, v in t.r.items():
                add(k, v, False)
        return deps

    def _emit(self, E, deps, fn):
        items = list(deps.items())
        for k, v in items:
            if k == E.semkey:
                assert v <= E.count, f"self-wait on future signal {E.name} {v} {E.count}"
        for k, v in items[:-1]:
            E.obj.wait_ge(self.sems[k], v)
            E.known[k] = v
            E.ninst += 1
        inst = fn(E.obj)
        E.ninst += 1
        if items:
            k, v = items[-1]
            inst._wait_ge(self.sems[k], v)
            E.known[k] = v
        return inst

    def op(self, eng, fn, reads=(), writes=(), signal=True):
        E = self.eng[eng]
        deps = self._deps(E, reads, writes)
        inst = self._emit(E, deps, fn)
        if signal:
            inst.then_inc(self.sems[E.semkey], 1)
            E.count += 1
            val = E.count
        else:
            val = E.count + 1
        for t in reads:
            if t.r.get(E.semkey, 0) < val:
                t.r[E.semkey] = val
        for t in writes:
            t.w = (E.semkey, val)
            t.r = {}
        return inst

    def dma(self, q, out_ap, in_ap, dt, load, extra_reads=(), extra_writes=(), **kw):
        E = self.eng[q]
        if load:
            deps = self._deps(E, extra_reads, [dt] + list(extra_writes))
        else:
            deps = self._deps(E, [dt] + list(extra_reads), extra_writes)
        inst = self._emit(E, deps, lambda e: e.dma_start(out=out_ap, in_=in_ap, **kw))
        inst.then_inc(self.sems[dt.semkey], 16)
        dt.count += 16
        if load:
            dt.w = (dt.semkey, dt.count)
            dt.r = {}
            for t in extra_writes:
                t.w = (dt.semkey, dt.count)
                t.r = {}
        else:
            dt.r[dt.semkey] = dt.count
            for t in extra_reads:
                if t.r.get(dt.semkey, 0) < dt.count:
                    t.r[dt.semkey] = dt.count
        return inst

    def finish(self):
        E = self.eng["sp"]
        for t in self.dtiles:
            if t.count > 0 and E.known.get(t.semkey, 0) < t.count:
                E.obj.wait_ge(self.sems[t.semkey], t.count)
        for name in ("pe", "act", "dve", "pool"):
            X = self.eng[name]
            if X.count > 0:
                E.obj.wait_ge(self.sems[X.semkey], X.count)


class Buf:
    def __init__(self, t, rowlen, dtype):
        self.t = t
        self.rowlen = rowlen
        self.dtype = dtype

    def v(self, off, dims, p0=0, np_=128):
        return bass.AP(self.t, p0 * self.rowlen + off, [[self.rowlen, np_]] + [[s, n] for s, n in dims])


class SubBuf:
    def __init__(self, parent, base):
        self.parent = parent
        self.base = base
        self.rowlen = parent.rowlen
        self.dtype = parent.dtype

    def v(self, off, dims, p0=0, np_=128):
        return self.parent.v(self.base + off, dims, p0=p0, np_=np_)


class SeqCfg:
    def __init__(self, name, T, b, sample, idx):
        self.name = name
        self.T = T
        self.b = b
        self.sample = sample
        self.idx = idx
        self.nh = 1 if sample else 2
        self.TH = T // self.nh
        self.BW = min(512, self.TH)
        self.nb = self.TH // self.BW
        self.TW = min(128, self.TH)
        self.nt = self.TH // self.TW


class StopBuild(Exception):
    pass


class KB:
    def __init__(self, dbg=None, nlayers=4, seqs=("p0", "p1", "s"), cut=None, warm=1):
        self.cut = cut
        self.warm = warm
        self.dbg = dbg or {}
        self.nlayers = nlayers
        self.seq_names = seqs
        self.nc = bass.Bass("TRN2", target_bir_lowering=False)
        self.stack = contextlib.ExitStack()

    def din(self, name, shape):
        return self.nc.dram_tensor(name, list(shape), F32, kind="ExternalInput").ap()

    def dout(self, name, shape):
        return self.nc.dram_tensor(name, list(shape), F32, kind="ExternalOutput").ap()

    def sb(self, name, rowlen, dtype, np_=128):
        t = self.stack.enter_context(self.nc.sbuf_tensor(name, [np_, rowlen], dtype))
        return Buf(t, rowlen, dtype)

    def declare(self):
        S = self.S = Sched(self.nc, self.stack)
        self.i = {}
        I = self.i
        I["xp"] = self.din("xp", [2, SEQ, D])
        I["xs"] = self.din("xs", [DEC, D])
        I["cdk"] = self.din("cdk", [SEQ, D])
        I["cdv"] = self.din("cdv", [SEQ, D])
        I["csk"] = self.din("csk", [SEQ, D])
        I["csv"] = self.din("csv", [SEQ, D])
        I["r1"] = self.din("r1", [48, D])
        I["r2"] = self.din("r2", [24, DFF])
        I["cst"] = self.din("cst", [128, 512])
        I["rope"] = self.din("rope", [17 * 128, 16])
        I["ada_w"] = self.din("ada_w", [4, D, 6 * D])
        I["gm_w_in"] = self.din("gm_w_in", [D, 2 * D])
        I["gm_ln_g"] = self.din("gm_ln_g", [1, D])
        I["gm_ln_b"] = self.din("gm_ln_b", [1, D])
        I["gm_ws"] = self.din("gm_ws", [8, 128, 128])
        I["gm_bs"] = self.din("gm_bs", [1, D])
        I["gm_w_out"] = self.din("gm_w_out", [D, D])
        I["diff_w_qkv"] = self.din("diff_w_qkv", [8, 128, 3 * D])
        I["diff_lambda"] = self.din("diff_lambda", [1, 256])
        I["diff_subln_g"] = self.din("diff_subln_g", [1, 128])
        I["diff_w_out"] = self.din("diff_w_out", [D, D])
        I["sc_w_in"] = self.din("sc_w_in", [D, 3 * D])
        I["sc_w_out"] = self.din("sc_w_out", [D, D])
        I["sb_w_qkv"] = self.din("sb_w_qkv", [8, 128, 3 * D])
        I["sb_w_out"] = self.din("sb_w_out", [D, D])
        I["ffn_w_in"] = self.din("ffn_w_in", [4, D, 2 * DFF])
        I["ffn_w_out"] = self.din("ffn_w_out", [4, 8, 128, DFF])
        self.o = {}
        O = self.o
        O["y_p"] = self.dout("y_p", [2, SEQ, D])
        O["y_s"] = self.dout("y_s", [DEC, D])
        O["gmv_s"] = self.dout("gmv_s", [DEC, D])
        O["dk_p"] = self.dout("dk_p", [2, SEQ, D])
        O["dv_p"] = self.dout("dv_p", [2, SEQ, D])
        O["dk_s"] = self.dout("dk_s", [DEC, D])
        O["dv_s"] = self.dout("dv_s", [DEC, D])
        O["sc_p"] = self.dout("sc_p", [2, 2, D])
        O["sc_s"] = self.dout("sc_s", [2, D])
        O["sbk_p"] = self.dout("sbk_p", [2, SEQ, D])
        O["sbv_p"] = self.dout("sbv_p", [2, SEQ, D])
        O["sbk_s"] = self.dout("sbk_s", [DEC, D])
        O["sbv_s"] = self.dout("sbv_s", [DEC, D])
        O["ffc_p"] = self.dout("ffc_p", [4, 2, 2, DFF])
        O["ffc_s"] = self.dout("ffc_s", [4, 2, DFF])
        for name, shape in self.dbg.items():
            O["dbg_" + name] = self.dout("dbg_" + name, shape)

        self.X = self.sb("X", 8 * SEQ, F32)
        self.Xt = [S.tile("X0"), S.tile("X1")]
        self.HB = self.sb("HB", 8 * SEQ, BF16)
        self.HBt = [S.tile("HB0"), S.tile("HB1")]
        self.BIG = self.sb("BIG", FC * 1024, BF16)
        self.BIGa = S.tile("BIGa")
        self.BIGb = S.tile("BIGb")
        self.WS = [self.sb(f"WS{k}", 3072, BF16) for k in range(3)]
        self.WSt = [S.dtile(f"WS{k}") for k in range(3)]
        self.wsi = 0
        self.ATT = self.sb("ATT", 8704, BF16)
        self.ATTq = S.dtile("ATTq")
        self.ATTk = S.dtile("ATTk")
        self.ATTv = S.dtile("ATTv")
        self.WS += [SubBuf(self.ATT, 0), SubBuf(self.ATT, 3072)]
        self.WSt += [S.dtile("WS3"), S.dtile("WS4")]
        self.wide = False
        self.WF = [self.sb(f"WF{k}", 512, F32) for k in range(4)]
        self.WFt = [S.tile(f"WF{k}") for k in range(4)]
        self.WB = [self.sb(f"WB{k}", 512, BF16) for k in range(4)]
        self.WBt = [S.tile(f"WB{k}") for k in range(4)]
        self.GS = self.sb("GS", 1026, F32)
        self.GSt = S.tile("GS")
        self.LAMB = self.sb("LAMB", 512, BF16)
        self.LAMBt = S.dtile("LAMB")
        self.QKVs = [self.sb(f"QKV{k}", 384, F32) for k in range(2)]
        self.QKVts = [S.tile(f"QKV{k}") for k in range(2)]
        self.QKB = [SubBuf(self.LAMB, 0), SubBuf(self.LAMB, 256)]
        self.QKBt = [S.tile("QKB0"), S.tile("QKB1")]
        self.MODL = self.sb("MODL", 144, F32)
        self.MODLt = S.tile("MODL")
        self.V1T = self.sb("V1T", 8 * 48, F32)
        self.V1Tt = S.tile("V1T")
        self.V2T = self.sb("V2T", FC * 24, F32)
        self.V2Tt = S.tile("V2T")
        self.SC = self.sb("SC", 576, F32)
        self.SCt = S.tile("SC")
        self.CF = self.sb("CF", 512, F32)
        self.CFt = S.dtile("CF")
        self.CB = self.sb("CB", 7 * 128, BF16)
        self.CBt = S.tile("CB")
        self.ROPE = self.sb("ROPE", 17 * 16, F32)
        self.ROPEt = S.dtile("ROPE")
        self.GSUB = self.sb("GSUB", 128, F32)
        self.GSUBt = S.dtile("GSUB")
        self.SM = self.sb("SM", 64, F32)
        self.SMt = S.tile("SM")
        self.CSB = self.sb("CSB", 24, BF16)
        self.CSBt = S.tile("CSB")
        self.GCAR = self.sb("GCAR", FC * 2, F32)
        self.GCARt = S.tile("GCAR")
        self.PCAR = self.sb("PCAR", 16, F32)
        self.PCARt = S.tile("PCAR")
        self.HAL = self.sb("HAL", 6, F32)
        self.HALt = [S.tile("HAL0"), S.tile("HAL1"), S.tile("HALZ")]
        self.STG = [S.dtile("STG0"), S.dtile("STG1")]
        self.KSTt = S.dtile("KST")
        self.VSTt = S.dtile("VST")
        self.OST = S.dtile("OST")
        self.PS = []
        self.PSt = []
        for k in range(8):
            t = self.stack.enter_context(self.nc.psum_tensor(f"PS{k}", [128, 512], F32))
            self.PS.append(Buf(t, 512, F32))
            self.PSt.append(S.tile(f"PS{k}"))

    def Xv(self, kc, c0, n, np_=128):
        return self.X.v(kc * SEQ + c0, [(1, n)], np_=np_)

    def Hv(self, kc, c0, n):
        return self.HB.v(kc * SEQ + c0, [(1, n)])

    def Fv(self, mo, c0, n):
        return self.HB.v(0, [(1, 8 * SEQ)]).bitcast(F32)[:, mo * 1024 + c0: mo * 1024 + c0 + n]

    def Av(self, j, c0, n):
        return self.BIG.v(j * 1024 + c0, [(1, n)])

    def Ov(self, kc, c0, n):
        return self.BIG.v(kc * SEQ + c0, [(1, n)])

    def bigf(self, e0, n):
        return self.BIG.v(e0, [(1, 2 * n)]).bitcast(F32)

    def attf(self, e0, n):
        return self.ATT.v(e0, [(1, 2 * n)]).bitcast(F32)

    def gsf(self):
        return self.GS

    def sc(self, seq, i, s, kc):
        off = ((i * 3 + seq.b) * 6 + s) * 8 + kc
        return self.SC.v(off, [(1, 1)])

    def cb(self, which, k=128, m=128, p0=0):
        idx = {"ones_s": 0, "ones": 1, "tri": 2, "zero": 3, "msu": 4, "ident": 5, "trile": 6}[which]
        return self.CB.v(idx * 128, [(1, m)], p0=p0, np_=k)

    def idf(self, k):
        return self.CF.v(0, [(1, k)], np_=k)

    def load_panel(self, pieces, nk):
        S = self.S
        k = self.wsi % (5 if self.wide else 3)
        self.wsi += 1
        slot, st = self.WS[k], self.WSt[k]
        tot = sum(n for _, n in pieces)
        assert nk * tot <= 3072
        c = 0
        for ap, n in pieces:
            dst = slot.v(c, [(1, n)]) if nk == 1 else slot.v(c, [(tot, nk), (1, n)])
            S.dma("pool", dst, ap, st, True)
            c += n
        return slot, st, tot

    @staticmethod
    def wview(w2d, nk):
        return w2d.rearrange("(kc p) n -> p kc n", p=128)

    def cp(self, name):
        if self.cut == name:
            raise StopBuild()

    def set_wide(self, on):
        att = [self.ATTq, self.ATTk, self.ATTv]
        ext = [self.WSt[3], self.WSt[4]]
        if on:
            self.S.retire(att, ext)
        else:
            self.S.retire(ext, att)
        self.wide = on

    def dump(self, name, src_ap, tiles):
        if name not in self.dbg:
            return
        S = self.S
        dt = S.dtile("dbg_" + name)
        S.dma("sp", self.o["dbg_" + name], src_ap, dt, False, extra_reads=tiles)

    def setup(self):
        S, I = self.S, self.i
        S.dma("sp", self.CF.v(0, [(1, 512)]), I["cst"][:, :], self.CFt, True)
        S.op("pool", lambda e: e.memset(self.CB.v(0, [(1, 128)]), 1.0 / 1024.0), writes=[self.CBt])
        S.op("pool", lambda e: e.memset(self.CB.v(128, [(1, 128)]), 1.0), writes=[self.CBt])
        S.op("pool", lambda e: e.memset(self.CB.v(384, [(1, 128)]), 0.0), writes=[self.CBt])
        S.op("dve", lambda e: e.tensor_copy(self.CB.v(256, [(1, 128)]), self.CF.v(128, [(1, 128)])), reads=[self.CFt], writes=[self.CBt])
        S.op("dve", lambda e: e.tensor_copy(self.CB.v(512, [(1, 128)]), self.CF.v(256, [(1, 128)])), reads=[self.CFt], writes=[self.CBt])
        S.op("dve", lambda e: e.tensor_copy(self.CB.v(640, [(1, 128)]), self.CF.v(0, [(1, 128)])), reads=[self.CFt], writes=[self.CBt])
        S.op("dve", lambda e: e.tensor_scalar(self.CB.v(768, [(1, 128)]), self.CF.v(128, [(1, 128)]), -1.0, 1.0, ALU.mult, ALU.add),
             reads=[self.CFt], writes=[self.CBt])
        S.op("pool", lambda e: e.memset(self.GCAR.v(0, [(1, FC * 2)]), 0.0), writes=[self.GCARt])
        S.op("pool", lambda e: e.memset(self.PCAR.v(0, [(1, 16)]), 0.0), writes=[self.PCARt])
        S.op("pool", lambda e: e.memset(self.HAL.v(0, [(1, 6)]), 0.0), writes=self.HALt)
        S.dma("sp", self.ROPE.v(0, [(16, 17), (1, 16)]), I["rope"].rearrange("(t p) c -> p t c", p=128), self.ROPEt, True)
        st0 = self.bigf(0, 1024)
        S.dma("sp", st0[0:48, :], I["r1"][:, :], self.STG[0], True)
        ps = self.PS[0]
        for kc in range(8):
            S.op("pe", lambda e, kc=kc: e.transpose(ps.v(kc * 48, [(1, 48)]), st0[0:48, kc * 128:(kc + 1) * 128], self.idf(48)),
                 reads=[self.STG[0], self.CFt], writes=[self.PSt[0]])
        S.op("dve", lambda e: e.tensor_copy(self.V1T.v(0, [(1, 384)]), ps.v(0, [(1, 384)])), reads=[self.PSt[0]], writes=[self.V1Tt])
        st2 = self.bigf(2048, DFF)
        S.dma("sp", st2[0:24, :], I["r2"][:, :], self.STG[1], True)
        for grp in range(2):
            psg = self.PS[1 + grp]
            for jj in range(11):
                j = grp * 11 + jj
                S.op("pe", lambda e, j=j, jj=jj, psg=psg: e.transpose(psg.v(jj * 24, [(1, 24)]), st2[0:24, j * 128:(j + 1) * 128], self.idf(24)),
                     reads=[self.STG[1], self.CFt], writes=[self.PSt[1 + grp]])
            S.op("dve", lambda e, grp=grp, psg=psg: e.tensor_copy(self.V2T.v(grp * 264, [(1, 264)]), psg.v(0, [(1, 264)])),
                 reads=[self.PSt[1 + grp]], writes=[self.V2Tt])
        S.retire(self.STG, [self.BIGa])
        S.op("act", lambda e: e.activation(self.CSB.v(0, [(3, 8), (1, 3)]), self.V1T.v(16, [(48, 8), (1, 3)]), AF.Silu),
             reads=[self.V1Tt], writes=[self.CSBt])
        self.ada_layer(0)
        self.ada_next = 1
        LAMf = self.LAMB.v(0, [(1, 512)]).bitcast(F32)
        S.dma("sp", LAMf, I["diff_lambda"].partition_broadcast(128), self.LAMBt, True)
        S.dma("sp", self.GSUB.v(0, [(1, 128)]), I["diff_subln_g"].partition_broadcast(128), self.GSUBt, True)
        wf = self.WF[0]
        S.op("dve", lambda e: e.tensor_tensor(wf.v(0, [(1, 64)]), LAMf[:, 0:64], LAMf[:, 64:128], ALU.mult),
             reads=[self.LAMBt], writes=[self.WFt[0]])
        S.op("dve", lambda e: e.tensor_tensor(wf.v(64, [(1, 64)]), LAMf[:, 128:192], LAMf[:, 192:256], ALU.mult),
             reads=[self.LAMBt], writes=[self.WFt[0]])
        S.op("dve", lambda e: e.reduce_sum(self.SM.v(1, [(1, 1)]), wf.v(0, [(1, 64)]), axis=AX.X), reads=[self.WFt[0]], writes=[self.SMt])
        S.op("dve", lambda e: e.reduce_sum(self.SM.v(2, [(1, 1)]), wf.v(64, [(1, 64)]), axis=AX.X), reads=[self.WFt[0]], writes=[self.SMt])
        S.op("act", lambda e: e.activation(self.SM.v(3, [(1, 2)]), self.SM.v(1, [(1, 2)]), AF.Exp), reads=[self.SMt], writes=[self.SMt])
        S.op("dve", lambda e: e.scalar_tensor_tensor(self.SM.v(0, [(1, 1)]), self.SM.v(4, [(1, 1)]), -LAM_INIT, self.SM.v(3, [(1, 1)]),
                                                     ALU.add, ALU.subtract), reads=[self.SMt], writes=[self.SMt])
        S.op("dve", lambda e: e.tensor_scalar(self.GSUB.v(0, [(1, 128)]), self.GSUB.v(0, [(1, 128)]), 1.0 - LAM_INIT, None, ALU.mult),
             reads=[self.GSUBt], writes=[self.GSUBt])
        S.retire([self.LAMBt], self.QKBt)

    def ada_layer(self, i):
        S, I = self.S, self.i
        MOD, modt = self.MODL, self.MODLt
        psa = self.PS[3]
        psat = self.PSt[3]
        wv = self.wview(I["ada_w"][i], 8)
        for p in range(16):
            slot, st, tot = self.load_panel([(wv[:, :, p * 384:(p + 1) * 384], 384)], 8)
            for ml in range(3):
                m = p * 3 + ml
                for kc in range(8):
                    S.op("pe", lambda e, kc=kc, ml=ml, m=m, slot=slot: e.matmul(
                        psa.v(m * 3, [(1, 3)]), slot.v(kc * 384 + ml * 128, [(1, 128)]), self.CSB.v(kc * 3, [(1, 3)]),
                        start=(kc == 0), stop=(kc == 7)),
                        reads=[st, self.CSBt], writes=[psat], signal=(kc == 7))
        for kind in range(6):
            S.op("dve", lambda e, kind=kind: e.tensor_tensor(
                MOD.v(kind * 24, [(3, 8), (1, 3)]),
                psa.v(kind * 24, [(3, 8), (1, 3)]),
                self.V1T.v(22 + i * 6 + kind, [(48, 8), (0, 3)]), ALU.add),
                reads=[psat, self.V1Tt], writes=[modt])
        for b in range(3):
            def modv(kind):
                return MOD.v(kind * 24 + b, [(3, 8)])

            def scv(s_):
                return self.SC.v(((i * 3 + b) * 6 + s_) * 8, [(1, 8)])

            def ng(s_):
                return self.V1T.v(i * 4 + s_, [(48, 8)])
            S.op("dve", lambda e, modv=modv, scv=scv, ng=ng: e.scalar_tensor_tensor(scv(0), modv(1), 1.0, ng(0), ALU.add, ALU.mult),
                 reads=[modt, self.V1Tt], writes=[self.SCt])
            S.op("dve", lambda e, modv=modv, scv=scv: e.tensor_copy(scv(1), modv(0)), reads=[modt], writes=[self.SCt])
            S.op("dve", lambda e, modv=modv, scv=scv, ng=ng: e.tensor_tensor(scv(2), modv(2), ng(1), ALU.mult),
                 reads=[modt, self.V1Tt], writes=[self.SCt])
            S.op("dve", lambda e, modv=modv, scv=scv, ng=ng: e.scalar_tensor_tensor(scv(3), modv(4), 1.0, ng(2), ALU.add, ALU.mult),
                 reads=[modt, self.V1Tt], writes=[self.SCt])
            S.op("dve", lambda e, modv=modv, scv=scv: e.tensor_copy(scv(4), modv(3)), reads=[modt], writes=[self.SCt])
            S.op("dve", lambda e, modv=modv, scv=scv, ng=ng: e.tensor_tensor(scv(5), modv(5), ng(3), ALU.mult),
                 reads=[modt, self.V1Tt], writes=[self.SCt])

    def load_x(self, seq):
        S, I = self.S, self.i
        S.retire([self.BIGa], self.STG)
        src = I["xs"] if seq.sample else I["xp"][seq.idx]
        ntile = seq.T // seq.TW
        for t in range(ntile):
            k = t % 2
            stg = self.bigf(k * 2048, 1024)
            S.dma("sp", stg[0:seq.TW, :], src[t * seq.TW:(t + 1) * seq.TW, :], self.STG[k], True)
            for g in range(2):
                ps, pst = self.PS[(t % 2) * 2 + g], self.PSt[(t % 2) * 2 + g]
                for q in range(4):
                    kc = g * 4 + q
                    S.op("pe", lambda e, kc=kc, q=q, ps=ps, stg=stg: e.transpose(
                        ps.v(q * 128, [(1, seq.TW)]), stg[0:seq.TW, kc * 128:(kc + 1) * 128], self.idf(seq.TW)),
                        reads=[self.STG[k], self.CFt], writes=[pst], signal=(q == 3))
                h = (t * seq.TW) // seq.TH
                eng = "act" if g == 0 else "dve"
                dst = self.X.v(g * 4 * SEQ + t * seq.TW, [(SEQ, 4), (1, seq.TW)])
                srcp = ps.v(0, [(128, 4), (1, seq.TW)])
                if eng == "act":
                    S.op("act", lambda e, dst=dst, srcp=srcp: e.copy(dst, srcp), reads=[pst], writes=[self.Xt[h]])
                else:
                    S.op("dve", lambda e, dst=dst, srcp=srcp: e.tensor_copy(dst, srcp), reads=[pst], writes=[self.Xt[h]])
        S.retire(self.STG, [self.BIGa])

    def store_y(self, seq):
        S, O = self.S, self.o
        S.retire([self.BIGa], self.STG)
        dst = O["y_s"] if seq.sample else O["y_p"][seq.idx]
        ntile = seq.T // seq.TW
        for t in range(ntile):
            k = t % 2
            h = (t * seq.TW) // seq.TH
            stg = self.bigf(k * 2048, 1024)
            for g in range(2):
                ps, pst = self.PS[(t % 2) * 2 + g], self.PSt[(t % 2) * 2 + g]
                for q in range(4):
                    kc = g * 4 + q
                    S.op("pe", lambda e, kc=kc, q=q, ps=ps: e.transpose(
                        ps.v(q * 128, [(1, 128)], np_=seq.TW), self.Xv(kc, t * seq.TW, seq.TW), self.idf(128)),
                        reads=[self.Xt[h], self.CFt], writes=[pst], signal=(q == 3))
                if g == 0:
                    S.op("act", lambda e, ps=ps, stg=stg: e.copy(stg[0:seq.TW, 0:512], ps.v(0, [(1, 512)], np_=seq.TW)),
                         reads=[pst], writes=[self.STG[k]])
                else:
                    S.op("dve", lambda e, ps=ps, stg=stg: e.tensor_copy(stg[0:seq.TW, 512:1024], ps.v(0, [(1, 512)], np_=seq.TW)),
                         reads=[pst], writes=[self.STG[k]])
            S.dma("sp", dst[t * seq.TW:(t + 1) * seq.TW, :], stg[0:seq.TW, :], self.STG[k], False)
        S.retire(self.STG, [self.BIGa])

    def norm_mod(self, seq, i, s_gs, s_sh, c0, ncols, xh):
        S = self.S
        BW = min(512, ncols)
        xts = [self.Xt[h] for h in xh]
        hts = [self.HBt[h] for h in xh]
        for b0 in range(c0, c0 + ncols, BW):
            pss, psst = self.PS[4], self.PSt[4]
            for kc in range(8):
                sq, sqt = self.WB[kc % 2], self.WBt[kc % 2]
                S.op("act", lambda e, kc=kc, sq=sq: e.activation(sq.v(0, [(1, BW)]), self.Xv(kc, b0, BW), AF.Square),
                     reads=xts, writes=[sqt])
                S.op("pe", lambda e, kc=kc, sq=sq: e.matmul(pss.v(0, [(1, BW)]), self.cb("ones_s"), sq.v(0, [(1, BW)]),
                                                            start=(kc == 0), stop=(kc == 7)),
                     reads=[sqt, self.CBt], writes=[psst], signal=True)
            rs, rst = self.WF[0], self.WFt[0]
            S.op("act", lambda e: e.activation(rs.v(0, [(1, BW)]), pss.v(0, [(1, BW)]), AF.Ln, bias=self.eps_ap()), reads=[psst, self.SMt], writes=[rst])
            S.op("act", lambda e: e.activation(rs.v(0, [(1, BW)]), rs.v(0, [(1, BW)]), AF.Exp, scale=-0.5), reads=[rst], writes=[rst])
            for kc in range(8):
                tmp, tmpt = self.WF[1 + kc % 2], self.WFt[1 + kc % 2]
                S.op("dve", lambda e, kc=kc, tmp=tmp: e.scalar_tensor_tensor(
                    tmp.v(0, [(1, BW)]), self.Xv(kc, b0, BW), self.sc(seq, i, s_gs, kc), rs.v(0, [(1, BW)]), ALU.mult, ALU.mult),
                    reads=xts + [rst, self.SCt], writes=[tmpt])
                S.op("act", lambda e, kc=kc, tmp=tmp: e.activation(
                    self.Hv(kc, b0, BW), tmp.v(0, [(1, BW)]), AF.Identity, bias=self.sc(seq, i, s_sh, kc), scale=1.0),
                    reads=[tmpt, self.SCt], writes=hts)

    def eps_ap(self):
        return self.SM.v(8, [(1, 1)])

    def proj_res(self, seq, i, s_gg, w2d, nk, src, src_tiles, half):
        S = self.S
        BW, nb = seq.BW, seq.nb
        packed = (nk != 8)
        wv = None if packed else self.wview(w2d, nk)
        npm = 3 if nk == 8 else 1
        fts = self.HBt
        ss = [self.PS[4 + b] for b in range(nb)]
        sst = [self.PSt[4 + b] for b in range(nb)]
        mo = 0
        cnt = 0
        while mo < 8:
            n_here = min(npm, 8 - mo)
            if packed:
                slot, st, _ = self.load_panel([(w2d[mo], nk * 128)], 1)
                tot = 128
            else:
                slot, st, tot = self.load_panel([(wv[:, :, mo * 128:(mo + n_here) * 128], n_here * 128)], nk)
            for ml in range(n_here):
                for b in range(nb):
                    ps, pst = self.PS[cnt % 4], self.PSt[cnt % 4]
                    cnt += 1
                    for kc in range(nk):
                        S.op("pe", lambda e, kc=kc, ml=ml, b=b, ps=ps, slot=slot, tot=tot: e.matmul(
                            ps.v(0, [(1, BW)]), slot.v(kc * tot + ml * 128, [(1, 128)]), src(kc, b * BW, BW),
                            start=(kc == 0), stop=(kc == nk - 1)),
                            reads=[st] + src_tiles, writes=[pst], signal=(kc == nk - 1))
                    m = mo + ml
                    S.op("act", lambda e, m=m, b=b, ps=ps: e.copy(self.Fv(m, b * BW, BW), ps.v(0, [(1, BW)])),
                         reads=[pst], writes=fts)
                    sq, sqt = self.WB[cnt % 4], self.WBt[cnt % 4]
                    S.op("act", lambda e, ps=ps, sq=sq: e.activation(sq.v(0, [(1, BW)]), ps.v(0, [(1, BW)]), AF.Square),
                         reads=[pst], writes=[sqt])
                    S.op("pe", lambda e, b=b, sq=sq, m=m: e.matmul(ss[b].v(0, [(1, BW)]), self.cb("ones_s"), sq.v(0, [(1, BW)]),
                                                                  start=(m == 0), stop=(m == 7)),
                         reads=[sqt, self.CBt], writes=[sst[b]], signal=True)
            mo += n_here
        xt = [self.Xt[half]]
        for b in range(nb):
            rs, rst = self.WF[0], self.WFt[0]
            S.op("act", lambda e, b=b: e.activation(rs.v(0, [(1, BW)]), ss[b].v(0, [(1, BW)]), AF.Ln, bias=self.eps_ap()),
                 reads=[sst[b], self.SMt], writes=[rst])
            S.op("act", lambda e: e.activation(rs.v(0, [(1, BW)]), rs.v(0, [(1, BW)]), AF.Exp, scale=-0.5), reads=[rst], writes=[rst])
            for m in range(8):
                tmp, tmpt = self.WF[1 + m % 3], self.WFt[1 + m % 3]
                S.op("dve", lambda e, m=m, b=b, tmp=tmp: e.scalar_tensor_tensor(
                    tmp.v(0, [(1, BW)]), self.Fv(m, b * BW, BW), self.sc(seq, i, s_gg, m), rs.v(0, [(1, BW)]), ALU.mult, ALU.mult),
                    reads=fts + [rst, self.SCt], writes=[tmpt])
                xc = half * seq.TH + b * BW
                S.op("pool" if m in (1, 3, 5) else "dve",
                     lambda e, m=m, xc=xc, tmp=tmp: e.tensor_tensor(self.Xv(m, xc, BW), self.Xv(m, xc, BW), tmp.v(0, [(1, BW)]), ALU.add),
                     reads=[tmpt] + xt, writes=xt)

    def ffn_half(self, seq, i, half):
        S, I, O = self.S, self.i, self.o
        BW, nb, TH = seq.BW, seq.nb, seq.TH
        c0 = half * TH
        self.set_wide(True)
        self.norm_mod(seq, i, 3, 4, c0, TH, [half])
        wv = self.wview(I["ffn_w_in"][i], 8)
        big = [self.BIGa, self.BIGb]
        hts = [self.HBt[half]]
        gs, gst = self.GS, self.GSt
        cw = lambda k, j: self.V2T.v(j * 24 + i * 3 + k, [(1, 1)])
        cbias = lambda j: self.V2T.v(j * 24 + 12 + i, [(1, 1)])
        j = 0
        cnt = 0
        while j < FC:
            nj = min(3, FC - j)
            gslot, gstile, gtot = self.load_panel([(wv[:, :, j * 128:(j + nj) * 128], nj * 128)], 8)
            uslot, ustile, utot = self.load_panel([(wv[:, :, DFF + j * 128:DFF + (j + nj) * 128], nj * 128)], 8)
            for jl in range(nj):
                jj = j + jl
                for b in range(nb):
                    par = cnt % 2
                    gp, gpt = self.PS[par * 2], self.PSt[par * 2]
                    up, upt = self.PS[par * 2 + 1], self.PSt[par * 2 + 1]
                    cnt += 1
                    for kc in range(8):
                        S.op("pe", lambda e, kc=kc, jl=jl, b=b, gp=gp: e.matmul(
                            gp.v(0, [(1, BW)]), gslot.v(kc * gtot + jl * 128, [(1, 128)]), self.Hv(kc, c0 + b * BW, BW),
                            start=(kc == 0), stop=(kc == 7)), reads=[gstile] + hts, writes=[gpt], signal=(kc == 7))
                    for kc in range(8):
                        S.op("pe", lambda e, kc=kc, jl=jl, b=b, up=up: e.matmul(
                            up.v(0, [(1, BW)]), uslot.v(kc * utot + jl * 128, [(1, 128)]), self.Hv(kc, c0 + b * BW, BW),
                            start=(kc == 0), stop=(kc == 7)), reads=[ustile] + hts, writes=[upt], signal=(kc == 7))
                    acc, acct = self.WF[par * 2], self.WFt[par * 2]
                    sg, sgt = self.WF[par * 2 + 1], self.WFt[par * 2 + 1]
                    if b == 0:
                        if half == 0:
                            if seq.sample:
                                hal, halt = self.V2T.v(jj * 24 + 16 + i * 2, [(1, 2)]), self.V2Tt
                            else:
                                hal, halt = self.HAL.v(4, [(1, 2)]), self.HALt[2]
                        else:
                            hal, halt = self.GCAR.v(jj * 2, [(1, 2)]), self.GCARt
                    else:
                        hal, halt = self.HAL.v(((cnt - 2) % 2) * 2, [(1, 2)]), self.HALt[(cnt - 2) % 2]
                    S.op("act", lambda e, gp=gp, jj=jj, acc=acc: e.activation(acc.v(0, [(1, BW)]), gp.v(0, [(1, BW)]), AF.Identity,
                                                                               bias=cbias(jj), scale=cw(2, jj)),
                         reads=[gpt, self.V2Tt], writes=[acct])
                    if b == nb - 1:
                        S.op("act", lambda e, gp=gp, jj=jj: e.copy(self.GCAR.v(jj * 2, [(1, 2)]), gp.v(BW - 2, [(1, 2)])),
                             reads=[gpt], writes=[self.GCARt])
                    else:
                        S.op("act", lambda e, gp=gp, par=par: e.copy(self.HAL.v(par * 2, [(1, 2)]), gp.v(BW - 2, [(1, 2)])),
                             reads=[gpt], writes=[self.HALt[par]])
                    S.op("dve", lambda e, gp=gp, jj=jj, acc=acc: e.scalar_tensor_tensor(
                        acc.v(1, [(1, BW - 1)]), gp.v(0, [(1, BW - 1)]), cw(1, jj), acc.v(1, [(1, BW - 1)]), ALU.mult, ALU.add),
                        reads=[gpt, acct, self.V2Tt], writes=[acct])
                    S.op("dve", lambda e, gp=gp, jj=jj, acc=acc: e.scalar_tensor_tensor(
                        acc.v(2, [(1, BW - 2)]), gp.v(0, [(1, BW - 2)]), cw(0, jj), acc.v(2, [(1, BW - 2)]), ALU.mult, ALU.add),
                        reads=[gpt, acct, self.V2Tt], writes=[acct])
                    S.op("dve", lambda e, hal=hal, jj=jj, acc=acc: e.scalar_tensor_tensor(
                        acc.v(0, [(1, 2)]), hal, cw(0, jj), acc.v(0, [(1, 2)]), ALU.mult, ALU.add),
                        reads=[halt, acct, self.V2Tt], writes=[acct])
                    S.op("dve", lambda e, hal=hal, jj=jj, acc=acc: e.scalar_tensor_tensor(
                        acc.v(0, [(1, 1)]), hal[:, 1:2], cw(1, jj), acc.v(0, [(1, 1)]), ALU.mult, ALU.add),
                        reads=[halt, acct, self.V2Tt], writes=[acct])
                    S.op("act", lambda e, acc=acc, sg=sg: e.activation(sg.v(0, [(1, BW)]), acc.v(0, [(1, BW)]), AF.Silu), reads=[acct], writes=[sgt])
                    S.op("dve", lambda e, b=b, jj=jj, up=up, sg=sg: e.tensor_tensor(self.Av(jj, b * BW, BW), sg.v(0, [(1, BW)]), up.v(0, [(1, BW)]), ALU.mult),
                         reads=[sgt, upt], writes=big)
            j += nj
        if half == seq.nh - 1:
            self.emit_state_rows(self.GCAR, self.GCARt, FC,
                                 (O["ffc_s"][i] if seq.sample else O["ffc_p"][i, seq.idx]))
        self.proj_res(seq, i, 5, I["ffn_w_out"][i], FC, lambda kc, cc, n: self.Av(kc, cc, n), big, half)
        self.set_wide(False)

    def emit_state_rows(self, car, cart, nchunk, dst):
        S = self.S
        for g0 in range(0, nchunk, 4):
            ng = min(4, nchunk - g0)
            ps, pst = self.PS[6], self.PSt[6]
            for q in range(ng):
                S.op("pe", lambda e, q=q, g0=g0: e.transpose(ps.v(q * 128, [(1, 128)], np_=2), car.v((g0 + q) * 2, [(1, 2)]), self.idf(128)),
                     reads=[cart, self.CFt], writes=[pst], signal=(q == ng - 1))
            ost = self.WF[3]
            S.op("act", lambda e, ng=ng: e.copy(ost.v(0, [(1, ng * 128)], np_=2), ps.v(0, [(1, ng * 128)], np_=2)),
                 reads=[pst], writes=[self.WFt[3], self.OST])
            S.dma("sp", dst[:, g0 * 128:(g0 + ng) * 128], ost.v(0, [(1, ng * 128)], np_=2), self.OST, False, extra_reads=[self.WFt[3]])

    def gelu(self, dst, ps_ap, pst, n, np_, dst_tiles, wa, wb):
        S = self.S
        a, at = self.WF[wa], self.WFt[wa]
        b, bt = self.WF[wb], self.WFt[wb]
        S.op("act", lambda e: e.activation(a.v(0, [(1, n)], np_=np_), ps_ap, AF.Square), reads=[pst], writes=[at])
        S.op("dve", lambda e: e.tensor_scalar(a.v(0, [(1, n)], np_=np_), a.v(0, [(1, n)], np_=np_), 0.044715, 1.0, ALU.mult, ALU.add),
             reads=[at], writes=[at])
        S.op("dve", lambda e: e.tensor_tensor(a.v(0, [(1, n)], np_=np_), a.v(0, [(1, n)], np_=np_), ps_ap, ALU.mult), reads=[at, pst], writes=[at])
        S.op("act", lambda e: e.activation(b.v(0, [(1, n)], np_=np_), a.v(0, [(1, n)], np_=np_), AF.Sigmoid, scale=1.5957691216057308),
             reads=[at], writes=[bt])
        S.op("dve", lambda e: e.tensor_tensor(dst, b.v(0, [(1, n)], np_=np_), ps_ap, ALU.mult), reads=[bt, pst], writes=dst_tiles)

    def gm_consts(self):
        S, I = self.S, self.i
        LNG = self.attf(0, 1024)
        LNB = self.attf(2048, 1024)
        BSB = self.attf(4224, 1024)
        S.dma("sp", LNG, I["gm_ln_g"].partition_broadcast(128), self.ATTq, True)
        S.dma("sp", LNB, I["gm_ln_b"].partition_broadcast(128), self.ATTk, True)
        S.dma("sp", BSB, I["gm_bs"].partition_broadcast(128), self.ATTv, True)
        S.retire([self.BIGa], [self.STG[0]])
        stg = self.bigf(0, 1024)
        S.dma("sp", stg.rearrange("p (g s) -> p g s", g=8), I["gm_ws"].rearrange("g t s -> t g s"), self.STG[0], True)
        ps, pst = self.PS[0], self.PSt[0]
        ps2, pst2 = self.PS[1], self.PSt[1]
        for g in range(8):
            pp, ppt = (ps, pst) if g < 4 else (ps2, pst2)
            S.op("pe", lambda e, g=g, pp=pp: e.transpose(pp.v((g % 4) * 128, [(1, 128)]), stg[:, g * 128:(g + 1) * 128], self.idf(128)),
                 reads=[self.STG[0], self.CFt], writes=[ppt], signal=(g % 4 == 3))
        m, mt = self.WF[0], self.WFt[0]
        S.op("dve", lambda e: e.tensor_scalar(m.v(0, [(1, 128)]), self.CF.v(128, [(1, 128)]), -1.0, 1.0, ALU.mult, ALU.add),
             reads=[self.CFt], writes=[mt])
        for hgrp in range(2):
            pp, ppt = (ps, pst) if hgrp == 0 else (ps2, pst2)
            S.op("dve", lambda e, hgrp=hgrp, pp=pp: e.tensor_tensor(
                self.ATT.v(6272 + hgrp * 512, [(128, 4), (1, 128)]), pp.v(0, [(128, 4), (1, 128)]), m.v(0, [(0, 4), (1, 128)]), ALU.mult),
                reads=[ppt, mt], writes=[self.ATTv])
        S.retire([self.STG[0]], [self.BIGa])

    def gmlp_half(self, seq, i, half):
        S, I, O = self.S, self.i, self.o
        TH, BW, nb, TW, nt = seq.TH, seq.BW, seq.nb, seq.TW, seq.nt
        c0 = half * TH
        self.norm_mod(seq, i, 0, 1, c0, TH, [half])
        self.cp("gm_norm")
        hts = [self.HBt[half]]
        wv = self.wview(I["gm_w_in"], 8)
        LNG = self.attf(0, 1024)
        LNB = self.attf(2048, 1024)
        BSB = self.attf(4224, 1024)
        big = [self.BIGa]
        panels = []
        for (cs, n) in ((1024, 384), (1408, 384), (1792, 256)):
            panels.append(self.load_panel([(wv[:, :, cs:cs + n], n)], 8) + (cs - 1024,))
        vf, vft = self.GS, self.GSt
        for t in range(nt):
            for pi, (slot, st, tot, co) in enumerate(panels):
                ps, pst = self.PS[pi], self.PSt[pi]
                for kc in range(8):
                    S.op("pe", lambda e, kc=kc, t=t, ps=ps, slot=slot, tot=tot: e.matmul(
                        ps.v(0, [(1, tot)], np_=TW), self.Hv(kc, c0 + t * TW, TW), slot.v(kc * tot, [(1, tot)]),
                        start=(kc == 0), stop=(kc == 7)), reads=[st] + hts, writes=[pst], signal=(kc == 7))
                self.gelu(vf.v(co, [(1, tot)], np_=TW), ps.v(0, [(1, tot)], np_=TW), pst, tot, TW, [vft], (pi % 2) * 2, (pi % 2) * 2 + 1)
            sm = self.SM
            S.op("dve", lambda e: e.reduce_sum(sm.v(16, [(1, 1)], np_=TW), vf.v(0, [(1, 1024)], np_=TW), axis=AX.X), reads=[vft], writes=[self.SMt])
            S.op("dve", lambda e: e.tensor_scalar(sm.v(17, [(1, 1)], np_=TW), sm.v(16, [(1, 1)], np_=TW), 1.0 / 1024.0, None, ALU.mult),
                 reads=[self.SMt], writes=[self.SMt])
            S.op("dve", lambda e: e.tensor_scalar(vf.v(0, [(1, 1024)], np_=TW), vf.v(0, [(1, 1024)], np_=TW), sm.v(17, [(1, 1)], np_=TW), None, ALU.subtract),
                 reads=[vft, self.SMt], writes=[vft])
            junk, junkt = self.WB[0], self.WBt[0]
            for hh in range(2):
                S.op("act", lambda e, hh=hh: e.activation(junk.v(0, [(1, 512)], np_=TW), vf.v(hh * 512, [(1, 512)], np_=TW), AF.Square,
                                                            accum_out=sm.v(18 + hh, [(1, 1)], np_=TW)),
                     reads=[vft], writes=[junkt, self.SMt])
            S.op("dve", lambda e: e.tensor_tensor(sm.v(20, [(1, 1)], np_=TW), sm.v(18, [(1, 1)], np_=TW), sm.v(19, [(1, 1)], np_=TW), ALU.add),
                 reads=[self.SMt], writes=[self.SMt])
            S.op("act", lambda e: e.activation(sm.v(21, [(1, 1)], np_=TW), sm.v(20, [(1, 1)], np_=TW), AF.Ln, bias=self.eps_ap()[0:TW, :], scale=1.0 / 1024.0),
                 reads=[self.SMt], writes=[self.SMt])
            S.op("act", lambda e: e.activation(sm.v(22, [(1, 1)], np_=TW), sm.v(21, [(1, 1)], np_=TW), AF.Exp, scale=-0.5),
                 reads=[self.SMt], writes=[self.SMt])
            S.op("dve", lambda e: e.scalar_tensor_tensor(vf.v(0, [(1, 1024)], np_=TW), vf.v(0, [(1, 1024)], np_=TW), sm.v(22, [(1, 1)], np_=TW),
                                                         LNG[0:TW, :], ALU.mult, ALU.mult),
                 reads=[vft, self.SMt, self.ATTq], writes=[vft])
            if seq.sample:
                S.op("dve", lambda e: e.tensor_tensor(vf.v(0, [(1, 1024)], np_=TW), vf.v(0, [(1, 1024)], np_=TW), LNB[0:TW, :], ALU.add),
                     reads=[vft, self.ATTk], writes=[vft])
                od = S.dtile("gmv_out")
                S.dma("sp", O["gmv_s"][:, :], vf.v(0, [(1, 1024)], np_=TW), od, False, extra_reads=[vft])
                S.op("act", lambda e, t=t: e.copy(self.BIG.v(8192 + t * 1024, [(1, 1024)], np_=TW), vf.v(0, [(1, 1024)], np_=TW)),
                     reads=[vft], writes=big)
            else:
                S.op("dve", lambda e, t=t: e.tensor_tensor(self.BIG.v(8192 + t * 1024, [(1, 1024)], np_=TW), vf.v(0, [(1, 1024)], np_=TW), LNB[0:TW, :], ALU.add),
                     reads=[vft, self.ATTk], writes=big)
        self.cp("gm_v")
        upan = {}
        cnt = 0
        for g in range(8):
            pidx = g // 3
            if pidx not in upan:
                n = min(384, 1024 - pidx * 384)
                upan[pidx] = self.load_panel([(wv[:, :, pidx * 384:pidx * 384 + n], n)], 8)
            slot, st, tot = upan[pidx]
            gl = g % 3
            for b in range(nb):
                ups, upst = self.PS[(cnt % 2) * 2], self.PSt[(cnt % 2) * 2]
                mps, mpst = self.PS[(cnt % 2) * 2 + 1], self.PSt[(cnt % 2) * 2 + 1]
                cnt += 1
                for kc in range(8):
                    S.op("pe", lambda e, kc=kc, gl=gl, b=b, ups=ups, slot=slot, tot=tot: e.matmul(
                        ups.v(0, [(1, BW)]), slot.v(kc * tot + gl * 128, [(1, 128)]), self.Hv(kc, c0 + b * BW, BW),
                        start=(kc == 0), stop=(kc == 7)), reads=[st] + hts, writes=[upst], signal=(kc == 7))
                ntb = BW // TW
                for tt in range(ntb):
                    t = b * ntb + tt
                    S.op("pe", lambda e, tt=tt, t=t, g=g, mps=mps: e.matmul(
                        mps.v(tt * TW, [(1, TW)]), self.BIG.v(8192 + t * 1024 + g * 128, [(1, 128)], np_=TW),
                        self.ATT.v(6272 + g * 128, [(1, TW)], np_=TW), start=True, stop=True),
                        reads=big + [self.ATTv], writes=[mpst], signal=(tt == ntb - 1))
                if cnt % 2 == 0:
                    ug, ugt = self.WF[2], self.WFt[2]
                    mx, mxt = self.WF[3], self.WFt[3]
                else:
                    ug, ugt = SubBuf(self.GS, 0), self.GSt
                    mx, mxt = SubBuf(self.GS, 512), self.GSt
                self.gelu(ug.v(0, [(1, BW)]), ups.v(0, [(1, BW)]), upst, BW, 128, [ugt], 0, 1)
                S.op("dve", lambda e, g=g, mps=mps, ntb=ntb: e.tensor_tensor(
                    mx.v(0, [(TW, ntb), (1, TW)]), mps.v(0, [(TW, ntb), (1, TW)]), self._bsb(g, ntb, TW), ALU.add),
                    reads=[mpst, self.ATTv], writes=[mxt])
                S.op("dve", lambda e, g=g, b=b: e.tensor_tensor(self.BIG.v(g * 1024 + b * BW, [(1, BW)]), mx.v(0, [(1, BW)]), ug.v(0, [(1, BW)]), ALU.mult),
                     reads=[mxt, ugt], writes=big)
        self.cp("gm_u")
        self.proj_res(seq, i, 2, I["gm_w_out"], 8, lambda kc, cc, n: self.BIG.v(kc * 1024 + cc, [(1, n)]), big, half)
        self.cp("gm_proj")

    def _bsb(self, g, ntb, TW):
        return self.ATT.v(4224 + 2 * g * 128, [(0, ntb), (1, 2 * TW)]).bitcast(F32)

    def sconv_half(self, seq, i, half):
        S, I, O = self.S, self.i, self.o
        TH, BW, nb = seq.TH, seq.BW, seq.nb
        c0 = half * TH
        self.set_wide(True)
        self.norm_mod(seq, i, 0, 1, c0, TH, [half])
        hts = [self.HBt[half]]
        wv = self.wview(I["sc_w_in"], 8)
        big = [self.BIGa]
        gs, gst = self.GS, self.GSt
        cw = lambda k, m: self.V1T.v(m * 48 + 19 + k, [(1, 1)])
        cnt = 0
        grp = {}
        for m in range(8):
            if m % 3 == 0:
                ng = min(3, 8 - m)
                grp = [self.load_panel([(wv[:, :, q * D + m * 128:q * D + (m + ng) * 128], ng * 128)], 8) for q in range(3)]
            ml = m % 3
            if half == 0:
                if seq.sample:
                    S.op("pool", lambda e, m=m: e.tensor_copy(gs.v(0, [(1, 2)]), self.V1T.v(m * 48 + 46, [(1, 2)])), reads=[self.V1Tt], writes=[gst])
                else:
                    S.op("pool", lambda e: e.memset(gs.v(0, [(1, 2)]), 0.0), writes=[gst])
            else:
                S.op("pool", lambda e, m=m: e.tensor_copy(gs.v(0, [(1, 2)]), self.PCAR.v(m * 2, [(1, 2)])), reads=[self.PCARt], writes=[gst])
            for b in range(nb):
                base = (cnt % 2) * 3
                cnt += 1
                pp = [self.PS[base + q] for q in range(3)]
                ppt = [self.PSt[base + q] for q in range(3)]
                for q in range(3):
                    slot, st, tot = grp[q]
                    for kc in range(8):
                        S.op("pe", lambda e, kc=kc, q=q, b=b, pp=pp, slot=slot, tot=tot, ml=ml: e.matmul(
                            pp[q].v(0, [(1, BW)]), slot.v(kc * tot + ml * 128, [(1, 128)]), self.Hv(kc, c0 + b * BW, BW),
                            start=(kc == 0), stop=(kc == 7)), reads=[st] + hts, writes=[ppt[q]], signal=(kc == 7))
                xs_, xst = self.WF[0], self.WFt[0]
                S.op("act", lambda e, pp=pp: e.copy(xs_.v(0, [(1, BW)]), pp[2].v(0, [(1, BW)])), reads=[ppt[2]], writes=[xst])
                S.op("dve", lambda e, b=b, pp=pp: e.tensor_tensor(gs.v(2 + b * BW, [(1, BW)]), pp[1].v(0, [(1, BW)]), xs_.v(0, [(1, BW)]), ALU.mult),
                     reads=[ppt[1], xst], writes=[gst])
                yy, yyt = self.WF[1], self.WFt[1]
                S.op("act", lambda e, b=b, m=m: e.activation(yy.v(0, [(1, BW)]), gs.v(2 + b * BW, [(1, BW)]), AF.Identity, scale=cw(2, m)),
                     reads=[gst, self.V1Tt], writes=[yyt])
                S.op("dve", lambda e, b=b, m=m: e.scalar_tensor_tensor(yy.v(0, [(1, BW)]), gs.v(1 + b * BW, [(1, BW)]), cw(1, m), yy.v(0, [(1, BW)]), ALU.mult, ALU.add),
                     reads=[gst, yyt, self.V1Tt], writes=[yyt])
                S.op("dve", lambda e, b=b, m=m: e.scalar_tensor_tensor(yy.v(0, [(1, BW)]), gs.v(b * BW, [(1, BW)]), cw(0, m), yy.v(0, [(1, BW)]), ALU.mult, ALU.add),
                     reads=[gst, yyt, self.V1Tt], writes=[yyt])
                S.op("dve", lambda e, b=b, m=m, pp=pp: e.tensor_tensor(self.BIG.v(m * 1024 + b * BW, [(1, BW)]), yy.v(0, [(1, BW)]), pp[0].v(0, [(1, BW)]), ALU.mult),
                     reads=[yyt, ppt[0]], writes=big)
            S.op("pool", lambda e, m=m: e.tensor_copy(self.PCAR.v(m * 2, [(1, 2)]), gs.v(TH, [(1, 2)])), reads=[gst], writes=[self.PCARt])
        if half == seq.nh - 1:
            self.emit_state_rows(self.PCAR, self.PCARt, 8, (O["sc_s"] if seq.sample else O["sc_p"][seq.idx]))
        self.proj_res(seq, i, 2, I["sc_w_out"], 8, lambda kc, cc, n: self.BIG.v(kc * 1024 + cc, [(1, n)]), big, half)
        self.set_wide(False)

    def attn_layer(self, seq, i, kind):
        S, I, O = self.S, self.i, self.o
        T, TW = seq.T, seq.TW
        self.norm_mod(seq, i, 0, 1, 0, T, list(range(seq.nh)))
        hts = self.HBt[:seq.nh]
        wqkv = I["diff_w_qkv"] if kind == "diff" else I["sb_w_qkv"]
        if kind == "diff":
            ok = O["dk_s"] if seq.sample else O["dk_p"][seq.idx]
            ov = O["dv_s"] if seq.sample else O["dv_p"][seq.idx]
            ck, cv = I["cdk"], I["cdv"]
        else:
            ok = O["sbk_s"] if seq.sample else O["sbk_p"][seq.idx]
            ov = O["sbv_s"] if seq.sample else O["sbv_p"][seq.idx]
            ck, cv = I["csk"], I["csv"]
        ntile = T // TW
        nkt_cache = 16 if seq.sample else 0
        S.retire([self.BIGa, self.BIGb], [self.KSTt, self.VSTt])
        QT0, KT0, VB0 = 0, 2048, 4224
        vw = 129 if kind == "diff" else 256
        big = [self.BIGa]
        if kind == "sb":
            S.op("pool", lambda e: e.memset(self.ATT.v(VB0, [(1, 17 * 256)]), 0.0), writes=[self.ATTv])
        else:
            S.op("pool", lambda e: e.memset(self.ATT.v(VB0, [(1, 17 * 129)]), 1.0), writes=[self.ATTv])
        def qkv_panel(hh):
            slot_, st_, _ = self.load_panel([(wqkv[hh], 3 * D)], 1)
            return slot_, st_, 384
        nxt = qkv_panel(0)
        for h in range(8):
            slot, st, tot = nxt
            if seq.sample:
                kst = self.bigf(16384, 2048).rearrange("p (t c) -> p t c", t=16)
                S.dma("sp", kst, ck.rearrange("(t p) c -> p t c", p=128)[:, :, h * 128:(h + 1) * 128], self.KSTt, True, extra_writes=[self.VSTt])
                for t in range(16):
                    ps, pst = self.PS[t % 2], self.PSt[t % 2]
                    S.op("pe", lambda e, t=t, ps=ps: e.transpose(ps.v(0, [(1, 128)]), kst[:, t, :], self.idf(128)),
                         reads=[self.KSTt, self.CFt], writes=[pst])
                    S.op("act" if t % 2 == 0 else "dve",
                         (lambda e, t=t, ps=ps: e.copy(self.ATT.v(KT0 + t * 128, [(1, 128)]), ps.v(0, [(1, 128)]))) if t % 2 == 0 else
                         (lambda e, t=t, ps=ps: e.tensor_copy(self.ATT.v(KT0 + t * 128, [(1, 128)]), ps.v(0, [(1, 128)]))),
                         reads=[pst], writes=[self.ATTk])
                cvv = cv.rearrange("(t p) c -> p t c", p=128)
                if kind == "diff":
                    S.dma("pool", self.ATT.v(VB0, [(129, 16), (1, 128)]), cvv[:, :, h * 128:(h + 1) * 128], self.ATTv, True)
                else:
                    S.dma("pool", self.ATT.v(VB0, [(256, 16), (1, 64)]), cvv[:, :, h * 128:h * 128 + 64], self.ATTv, True)
                    S.dma("pool", self.ATT.v(VB0 + 128 + 64, [(256, 16), (1, 64)]), cvv[:, :, h * 128 + 64:(h + 1) * 128], self.ATTv, True)
            def qkv_mm(t):
                ps, pst = self.PS[t % 2], self.PSt[t % 2]
                for kc in range(8):
                    S.op("pe", lambda e, kc=kc, t=t, ps=ps, slot=slot, tot=tot: e.matmul(
                        ps.v(0, [(1, 384)], np_=TW), self.Hv(kc, t * TW, TW), slot.v(kc * tot, [(1, 384)]),
                        start=(kc == 0), stop=(kc == 7)), reads=[st] + hts, writes=[pst], signal=(kc == 7))

            def qkv_post(t):
                ps, pst = self.PS[t % 2], self.PSt[t % 2]
                qk, qkt = self.QKVs[t % 2], self.QKVts[t % 2]
                S.op("act", lambda e, ps=ps: e.copy(qk.v(0, [(1, 384)], np_=TW), ps.v(0, [(1, 384)], np_=TW)), reads=[pst], writes=[qkt])
                if kind == "diff":
                    rt = nkt_cache + t if seq.sample else t
                    cosv = self.ROPE.v(rt * 16, [(0, 4), (1, 8)], np_=TW)
                    sinv = self.ROPE.v(rt * 16 + 8, [(0, 4), (1, 8)], np_=TW)
                    x1 = qk.v(0, [(64, 4), (1, 8)], np_=TW)
                    x2 = qk.v(8, [(64, 4), (1, 8)], np_=TW)
                    tm, tmt = self.WF[t % 2], self.WFt[t % 2]
                    t1 = tm.v(0, [(8, 4), (1, 8)], np_=TW)
                    t2 = tm.v(32, [(8, 4), (1, 8)], np_=TW)
                    t3 = tm.v(64, [(8, 4), (1, 8)], np_=TW)
                    t4 = tm.v(96, [(8, 4), (1, 8)], np_=TW)
                    S.op("dve", lambda e: e.tensor_tensor(t1, x1, cosv, ALU.mult), reads=[qkt, self.ROPEt], writes=[tmt])
                    S.op("dve", lambda e: e.tensor_tensor(t2, x2, sinv, ALU.mult), reads=[qkt, self.ROPEt], writes=[tmt])
                    S.op("dve", lambda e: e.tensor_tensor(t3, x2, cosv, ALU.mult), reads=[qkt, self.ROPEt], writes=[tmt])
                    S.op("dve", lambda e: e.tensor_tensor(t4, x1, sinv, ALU.mult), reads=[qkt, self.ROPEt], writes=[tmt])
                    S.op("dve", lambda e: e.tensor_tensor(x1, t1, t2, ALU.subtract), reads=[tmt], writes=[qkt])
                    S.op("dve", lambda e: e.tensor_tensor(x2, t3, t4, ALU.add), reads=[tmt], writes=[qkt])
                if seq.sample:
                    osb, osbt = self.WF[3], self.WFt[3]
                    S.op("act", lambda e: e.copy(osb.v(0, [(1, 256)], np_=TW), qk.v(128, [(1, 256)], np_=TW)), reads=[qkt], writes=[osbt, self.OST])
                    S.dma("sp", ok[:, h * 128:(h + 1) * 128], osb.v(0, [(1, 128)], np_=TW), self.OST, False, extra_reads=[osbt])
                    S.dma("sp", ov[:, h * 128:(h + 1) * 128], osb.v(128, [(1, 128)], np_=TW), self.OST, False, extra_reads=[osbt])
                else:
                    tl = t % 8
                    kstv = self.bigf(16384, 1024)
                    vstv = self.bigf(18432, 1024)
                    S.op("act", lambda e, tl=tl: e.copy(kstv[:, tl * 128:(tl + 1) * 128], qk.v(128, [(1, 128)])), reads=[qkt], writes=[self.KSTt])
                    S.op("pool", lambda e, tl=tl: e.tensor_copy(vstv[:, tl * 128:(tl + 1) * 128], qk.v(256, [(1, 128)])), reads=[qkt], writes=[self.VSTt])
                    if tl == 7:
                        r0 = (t - 7) * 128
                        S.dma("sp", ok[r0:r0 + 1024, :].rearrange("(t p) c -> p t c", p=128)[:, :, h * 128:(h + 1) * 128],
                              kstv.rearrange("p (t c) -> p t c", t=8), self.KSTt, False)
                        S.dma("sp", ov[r0:r0 + 1024, :].rearrange("(t p) c -> p t c", p=128)[:, :, h * 128:(h + 1) * 128],
                              vstv.rearrange("p (t c) -> p t c", t=8), self.VSTt, False)
                kt_idx = nkt_cache + t
                qb16, qb16t = self.QKB[t % 2], self.QKBt[t % 2]
                S.op("dve", lambda e: e.tensor_copy(qb16.v(0, [(1, 256)], np_=TW), qk.v(0, [(1, 256)], np_=TW)), reads=[qkt], writes=[qb16t])
                for which, off, dst0, dtile in ((0, 0, QT0 + t * TW, self.ATTq), (1, 128, KT0 + kt_idx * 128, self.ATTk)):
                    ps2, pst2 = self.PS[2 + which + 2 * (t % 2)], self.PSt[2 + which + 2 * (t % 2)]
                    pv16 = ps2.v(0, [(1, (TW + 1) // 2)]).bitcast(BF16)[:, 0:TW]
                    S.op("pe", lambda e, off=off, pv16=pv16: e.transpose(pv16, qb16.v(off, [(1, 128)], np_=TW), self.cb("ident", k=TW, m=TW)),
                         reads=[qb16t, self.CBt], writes=[pst2])
                    if which == 0:
                        S.op("act", lambda e, dst0=dst0, pv16=pv16: e.copy(self.ATT.v(dst0, [(1, TW)]), pv16), reads=[pst2], writes=[dtile])
                    else:
                        S.op("dve", lambda e, dst0=dst0, pv16=pv16: e.tensor_copy(self.ATT.v(dst0, [(1, TW)]), pv16), reads=[pst2], writes=[dtile])
                if kind == "diff":
                    S.op("pool", lambda e, kt_idx=kt_idx: e.tensor_copy(self.ATT.v(VB0 + kt_idx * 129, [(1, 128)], np_=TW), qk.v(256, [(1, 128)], np_=TW)),
                         reads=[qkt], writes=[self.ATTv])
                else:
                    S.op("pool", lambda e, kt_idx=kt_idx: e.tensor_copy(self.ATT.v(VB0 + kt_idx * 256, [(1, 64)], np_=TW), qk.v(256, [(1, 64)], np_=TW)),
                         reads=[qkt], writes=[self.ATTv])
                    S.op("pool", lambda e, kt_idx=kt_idx: e.tensor_copy(self.ATT.v(VB0 + kt_idx * 256 + 192, [(1, 64)], np_=TW), qk.v(320, [(1, 64)], np_=TW)),
                         reads=[qkt], writes=[self.ATTv])

            qkv_mm(0)
            for t in range(ntile):
                if t + 1 < ntile:
                    qkv_mm(t + 1)
                qkv_post(t)
            if h + 1 < 8:
                nxt = qkv_panel(h + 1)
            if kind == "diff":
                self.diff_attend(seq, h, QT0, KT0, VB0, nkt_cache)
            else:
                self.sb_attend(seq, h, QT0, KT0, VB0, nkt_cache)
        S.retire([self.KSTt, self.VSTt], [self.BIGb])
        wout = I["diff_w_out"] if kind == "diff" else I["sb_w_out"]
        for half in range(seq.nh):
            hc = half * seq.TH
            self.proj_res(seq, i, 2, wout, 8, lambda kc, cc, n, hc=hc: self.Ov(kc, hc + cc, n), big, half)

    def diff_attend(self, seq, h, QT0, KT0, VB0, nkc):
        S = self.S
        T, TW = seq.T, seq.TW
        QB = min(512, T)
        nqb = T // QB
        nqt = QB // TW
        big = [self.BIGa]
        scale = 0.125
        ops_ = [self.PS[5], self.PS[6], self.PS[7]]
        opst = [self.PSt[5], self.PSt[6], self.PSt[7]]

        def oacc(c, qt):
            a = c * nqt + qt
            return ops_[a // 3].v((a % 3) * 160, [(1, 129)], np_=TW), opst[a // 3]

        pending = []

        def flush_pending():
            while pending:
                pending.pop(0)()

        for qb in range(nqb):
            if seq.sample:
                ktiles = [(kt, 128) for kt in range(16)] + [(16, 16)]
            else:
                ktiles = [(kt, 128) for kt in range(qb * 4 + 4)]
            nbank = (2 * nqt + 2) // 3
            for rep in range(self.warm if not seq.sample else 1):
                for k in range(nbank):
                    S.op("pe", lambda e, k=k: e.matmul(ops_[k].v(0, [(1, 512)]), self.cb("zero"), self.CB.v(0, [(1, 512)]), start=True, stop=False, skip_group_check=True),
                         reads=[self.CBt], writes=[opst[k]], signal=False)
            units = []
            for ki, (kt, kr) in enumerate(ktiles):
                for c in range(2):
                    units.append((ki, kt, kr, c))

            def geom(kt):
                j = kt - qb * 4 if not seq.sample else -1
                cstart = j * 128 if j >= 0 else 0
                return j, cstart, QB - cstart

            def s_mm(u):
                ki, kt, kr, c = u
                j, cstart, ncol = geom(kt)
                sp, spt = self.PS[c * 2 + (ki % 2)], self.PSt[c * 2 + (ki % 2)]
                S.op("pe", lambda e: e.matmul(
                    sp.v(cstart, [(1, ncol)], np_=kr), self.ATT.v(KT0 + kt * 128, [(1, kr)], p0=c * 64, np_=64),
                    self.ATT.v(QT0 + qb * QB + cstart, [(1, ncol)], p0=c * 64, np_=64), start=True, stop=True),
                    reads=[self.ATTq, self.ATTk], writes=[spt])

            def ex(u):
                ki, kt, kr, c = u
                j, cstart, ncol = geom(kt)
                sp, spt = self.PS[c * 2 + (ki % 2)], self.PSt[c * 2 + (ki % 2)]
                pb, pbt = self.WB[c * 2 + (ki % 2)], self.WBt[c * 2 + (ki % 2)]
                S.op("act", lambda e: e.activation(
                    pb.v(cstart, [(1, ncol)], np_=kr), sp.v(cstart, [(1, ncol)], np_=kr), AF.Exp, scale=scale),
                    reads=[spt], writes=[pbt])
                if j >= 0:
                    S.op("pool", lambda e: e.memset(pb.v(cstart, [(1, 64)], p0=64, np_=64), 0.0), writes=[pbt])

            def pv(u):
                ki, kt, kr, c = u
                j, cstart, ncol = geom(kt)
                pb, pbt = self.WB[c * 2 + (ki % 2)], self.WBt[c * 2 + (ki % 2)]
                for qt in range(nqt):
                    if j >= 0 and qt < j:
                        continue
                    gqt = qb * nqt + qt
                    last_kt = (16 if seq.sample else gqt)
                    oap, oat = oacc(c, qt)
                    S.op("pe", lambda e, oap=oap, qt=qt, last_kt=last_kt: e.matmul(
                        oap, pb.v(qt * TW, [(1, TW)], np_=kr), self.ATT.v(VB0 + kt * 129, [(1, 129)], np_=kr),
                        start=False, stop=(kt == last_kt), skip_group_check=True),
                        reads=[pbt, self.ATTv], writes=[oat], signal=(kt == last_kt))

            nu_ = len(units)
            s_mm(units[0])
            s_mm(units[1])
            for idx in range(nu_ + 1):
                if idx % 2 == 1 and idx + 1 < nu_:
                    s_mm(units[idx + 1])
                    s_mm(units[idx + 2])
                if idx >= 1:
                    pv(units[idx - 1])
                if idx < nu_:
                    ex(units[idx])
                if idx == 6:
                    flush_pending()
            flush_pending()
            nbank = (2 * nqt + 2) // 3
            for k in range(nbank):
                S.op("act", lambda e, k=k: e.copy(self.WF[1 + k].v(0, [(1, 480)], np_=TW), ops_[k].v(0, [(1, 480)], np_=TW)),
                     reads=[opst[k]], writes=[self.WFt[1 + k]])

            def sacc(c, qt):
                a_ = c * nqt + qt
                return self.WF[1 + a_ // 3].v((a_ % 3) * 160, [(1, 129)], np_=TW), self.WFt[1 + a_ // 3]
            for qt in range(nqt):
                o0, o0t = sacc(0, qt)
                o1, o1t = sacc(1, qt)
                sm = self.SM
                av = o0[:, 0:128]
                at = o0t
                S.op("dve", lambda e, o0=o0: e.reciprocal(sm.v(24, [(1, 1)], np_=TW), o0[:, 128:129]), reads=[o0t], writes=[self.SMt])
                S.op("dve", lambda e, o1=o1: e.reciprocal(sm.v(25, [(1, 1)], np_=TW), o1[:, 128:129]), reads=[o1t], writes=[self.SMt])
                S.op("dve", lambda e: e.tensor_tensor(sm.v(26, [(1, 1)], np_=TW), sm.v(25, [(1, 1)], np_=TW), sm.v(0, [(1, 1)], np_=TW), ALU.mult),
                     reads=[self.SMt], writes=[self.SMt])
                S.op("dve", lambda e, av=av: e.tensor_scalar(av, av, sm.v(24, [(1, 1)], np_=TW), None, ALU.mult),
                     reads=[o0t, self.SMt], writes=[at])
                S.op("dve", lambda e, o1=o1, av=av: e.scalar_tensor_tensor(av, o1[:, 0:128], sm.v(26, [(1, 1)], np_=TW), av, ALU.mult, ALU.add),
                     reads=[o1t, self.SMt, at], writes=[at])
                jk, jkt = self.WF[0], self.WFt[0]
                S.op("act", lambda e, av=av: e.activation(jk.v(0, [(1, 128)], np_=TW), av, AF.Square,
                                                          accum_out=sm.v(27, [(1, 1)], np_=TW)), reads=[at], writes=[jkt, self.SMt])
                S.op("act", lambda e: e.activation(sm.v(28, [(1, 1)], np_=TW), sm.v(27, [(1, 1)], np_=TW), AF.Ln, bias=self.eps_ap()[0:TW, :], scale=1.0 / 128.0),
                     reads=[self.SMt], writes=[self.SMt])
                S.op("act", lambda e: e.activation(sm.v(29, [(1, 1)], np_=TW), sm.v(28, [(1, 1)], np_=TW), AF.Exp, scale=-0.5),
                     reads=[self.SMt], writes=[self.SMt])
                S.op("dve", lambda e, av=av: e.scalar_tensor_tensor(av, av, sm.v(29, [(1, 1)], np_=TW),
                                                                    self.GSUB.v(0, [(1, 128)], np_=TW), ALU.mult, ALU.mult),
                     reads=[at, self.SMt, self.GSUBt], writes=[at])
                col = qb * QB + qt * TW

                def tr(av=av, col=col, at=at):
                    tp, tpt = self.PS[4], self.PSt[4]
                    S.op("pe", lambda e: e.transpose(tp.v(0, [(1, TW)]), av, self.idf(TW)), reads=[at, self.CFt], writes=[tpt])
                    S.op("act", lambda e: e.copy(self.Ov(h, col, TW), tp.v(0, [(1, TW)])), reads=[tpt], writes=big)
                pending.append(tr)
        flush_pending()

    def sb_attend(self, seq, h, QT0, KT0, VB0, nkc):
        S = self.S
        T, TW = seq.T, seq.TW
        QB = min(512, T)
        nqb = T // QB
        big = [self.BIGa]
        ET = [self.WF[0], self.WF[1]]
        ETt = [self.WFt[0], self.WFt[1]]
        NL = [self.WF[2], self.WF[3]]
        NLt = [self.WFt[2], self.WFt[3]]
        NB = [self.WB[0], self.WB[1]]
        NBt = [self.WBt[0], self.WBt[1]]
        WT = [self.WB[2], self.WB[3]]
        WTt = [self.WBt[2], self.WBt[3]]
        for qb in range(nqb):
            if seq.sample:
                ktiles = [(16, 16, 0)] + [(kt, 128, -1) for kt in range(15, -1, -1)]
            else:
                ktiles = [(kt, 128, kt - qb * 4) for kt in range(qb * 4 + 3, -1, -1)]
            op_, opt = self.PS[6 + (qb % 2)], self.PSt[6 + (qb % 2)]
            for rep in range(self.warm if not seq.sample else 1):
                S.op("pe", lambda e: e.matmul(op_.v(0, [(1, QB)]), self.cb("zero"), self.CB.v(0, [(1, QB)]), start=True, stop=False, skip_group_check=True),
                     reads=[self.CBt], writes=[opt], signal=False)
                for c in range(2):
                    S.op("pe", lambda e, c=c: e.matmul(self.PS[4 + c].v(0, [(1, QB)]), self.cb("zero"), self.CB.v(0, [(1, QB)]), start=True, stop=False, skip_group_check=True),
                         reads=[self.CBt], writes=[self.PSt[4 + c]], signal=False)
            units = []
            for ki, (kt, kr, j) in enumerate(ktiles):
                for c in range(2):
                    units.append((ki, kt, kr, j, c))
            nu = len(units)

            def geom(u):
                ki, kt, kr, j, c = u
                cstart = j * 128 if (j >= 0 and not seq.sample) else 0
                return cstart, QB - cstart

            def zmm(u):
                ki, kt, kr, j, c = u
                cstart, ncol = geom(u)
                zp, zpt = self.PS[c * 2 + (ki % 2)], self.PSt[c * 2 + (ki % 2)]
                S.op("pe", lambda e: e.matmul(
                    zp.v(cstart, [(1, ncol)], np_=kr), self.ATT.v(KT0 + kt * 128, [(1, kr)], p0=c * 64, np_=64),
                    self.ATT.v(QT0 + qb * QB + cstart, [(1, ncol)], p0=c * 64, np_=64), start=True, stop=True),
                    reads=[self.ATTq, self.ATTk], writes=[zpt])

            def a_act(u):
                ki, kt, kr, j, c = u
                cstart, ncol = geom(u)
                zp, zpt = self.PS[c * 2 + (ki % 2)], self.PSt[c * 2 + (ki % 2)]
                sl = lambda B: B.v(cstart, [(1, ncol)], np_=kr)
                S.op("act", lambda e: e.activation(sl(ET[c]), sl(zp), AF.Exp, scale=0.125), reads=[zpt], writes=[ETt[c]])
                S.op("act", lambda e: e.activation(sl(NL[c]), sl(ET[c]), AF.Ln, bias=self.SM.v(9, [(1, 1)], np_=kr)),
                     reads=[ETt[c], self.SMt], writes=[NLt[c]])

            def a_dve(u):
                ki, kt, kr, j, c = u
                cstart, ncol = geom(u)
                zp, zpt = self.PS[c * 2 + (ki % 2)], self.PSt[c * 2 + (ki % 2)]
                sl = lambda B: B.v(cstart, [(1, ncol)], np_=kr)
                if j >= 0:
                    msk = self.CF.v(256, [(1, TW)], np_=kr)
                    S.op("dve", lambda e: e.tensor_tensor(
                        NL[c].v(cstart, [(1, TW)], np_=kr), NL[c].v(cstart, [(1, TW)], np_=kr), msk, ALU.mult),
                        reads=[NLt[c], self.CFt], writes=[NLt[c]])
                S.op("dve", lambda e: e.tensor_copy(sl(NB[c]), sl(NL[c])), reads=[NLt[c]], writes=[NBt[c]])
                S.op("dve", lambda e: e.scalar_tensor_tensor(sl(ET[c]), sl(zp), 0.125, sl(NL[c]), ALU.mult, ALU.subtract),
                     reads=[zpt, NLt[c]], writes=[ETt[c]])

            def stage_b1(u):
                ki, kt, kr, j, c = u
                cstart, ncol = geom(u)
                sp_, spt_ = self.PS[4 + c], self.PSt[4 + c]
                S.op("pe", lambda e: e.matmul(
                    sp_.v(cstart, [(1, ncol)], np_=kr), self.cb("tri", k=kr, m=kr), NB[c].v(cstart, [(1, ncol)], np_=kr), start=False, stop=False, skip_group_check=True),
                    reads=[NBt[c], self.CBt], writes=[spt_])
                sl = lambda B: B.v(cstart, [(1, ncol)], np_=kr)
                S.op("dve", lambda e: e.tensor_tensor(sl(ET[c]), sl(ET[c]), sl(sp_), ALU.subtract), reads=[ETt[c], spt_], writes=[ETt[c]])

            def stage_b2(u, idx):
                ki, kt, kr, j, c = u
                cstart, ncol = geom(u)
                sp_, spt_ = self.PS[4 + c], self.PSt[4 + c]
                sl = lambda B: B.v(cstart, [(1, ncol)], np_=kr)
                S.op("pe", lambda e: e.matmul(
                    sp_.v(cstart, [(1, ncol)]), self.cb("trile", k=kr, m=128), NB[c].v(cstart, [(1, ncol)], np_=kr), start=False, stop=(idx >= nu - 2), skip_group_check=True),
                    reads=[NBt[c], self.CBt], writes=[spt_])
                S.op("act", lambda e: e.activation(sl(WT[c]), sl(ET[c]), AF.Exp), reads=[ETt[c]], writes=[WTt[c]])
                if j >= 0:
                    mskb = self.cb("msu", k=kr, m=TW)
                    S.op("pool", lambda e: e.tensor_tensor(
                        WT[c].v(cstart, [(1, TW)], np_=kr), WT[c].v(cstart, [(1, TW)], np_=kr), mskb, ALU.mult), reads=[WTt[c], self.CBt], writes=[WTt[c]])

            def stage_c(u, idx):
                ki, kt, kr, j, c = u
                cstart, ncol = geom(u)
                last = (idx == nu - 1)
                S.op("pe", lambda e: e.matmul(
                    op_.v(cstart, [(1, ncol)]), self.ATT.v(VB0 + kt * 256 + c * 128, [(1, 128)], np_=kr), WT[c].v(cstart, [(1, ncol)], np_=kr),
                    start=False, stop=last, skip_group_check=True), reads=[WTt[c], self.ATTv], writes=[opt], signal=last)

            zmm(units[0])
            zmm(units[1])
            for s_ in range(nu + 2):
                if s_ % 2 == 1 and s_ + 1 < nu:
                    zmm(units[s_ + 1])
                    zmm(units[s_ + 2])
                if s_ < nu:
                    a_act(units[s_])
                if 0 <= s_ - 1 < nu:
                    stage_b1(units[s_ - 1])
                if s_ < nu:
                    a_dve(units[s_])
                if 0 <= s_ - 2 < nu:
                    stage_c(units[s_ - 2], s_ - 2)
                if 0 <= s_ - 1 < nu:
                    stage_b2(units[s_ - 1], s_ - 1)
            S.op("act", lambda e, qb=qb: e.copy(self.Ov(h, qb * QB, QB), op_.v(0, [(1, QB)])), reads=[opt], writes=big)

    def run_seq(self, seq):
        self.load_x(seq)
        self.dump(f"{seq.name}_x", self.X.v(0, [(1, 8 * SEQ)]), self.Xt)
        for i in range(self.nlayers):
            kind = i % 4
            if kind == 0:
                self.gm_consts()
                self.cp("gm_consts")
                for half in range(seq.nh):
                    self.gmlp_half(seq, i, half)
            elif kind == 1:
                self.attn_layer(seq, i, "diff")
            elif kind == 2:
                for half in range(seq.nh):
                    self.sconv_half(seq, i, half)
            else:
                self.attn_layer(seq, i, "sb")
            if self.ada_next == i + 1 and i + 1 < 4:
                self.ada_layer(i + 1)
                self.ada_next = i + 2
            self.dump(f"{seq.name}_xm{i}", self.X.v(0, [(1, 8 * SEQ)]), self.Xt)
            for half in range(seq.nh):
                self.ffn_half(seq, i, half)
            self.dump(f"{seq.name}_xf{i}", self.X.v(0, [(1, 8 * SEQ)]), self.Xt)
        self.store_y(seq)

    def build(self):
        self.declare()
        S = self.S
        S.op("pool", lambda e: e.memset(self.SM.v(8, [(1, 1)]), EPS), writes=[self.SMt])
        S.op("pool", lambda e: e.memset(self.SM.v(9, [(1, 1)]), 1.0), writes=[self.SMt])
        self.setup()
        cfgs = {"p0": SeqCfg("p0", SEQ, 0, False, 0), "p1": SeqCfg("p1", SEQ, 1, False, 1), "s": SeqCfg("s", DEC, 2, True, 0)}
        try:
            for n in self.seq_names:
                self.run_seq(cfgs[n])
        except StopBuild:
            pass
        for name, (fn) in getattr(self, "dumps", {}).items():
            pass
        S.finish()
        self.stack.close()
        return self.nc


def _consts():
    p = np.arange(128)
    ident = (p[:, None] == p[None, :]).astype(np.float32)
    tri = (p[:, None] > p[None, :]).astype(np.float32)
    msu = (p[:, None] < p[None, :]).astype(np.float32)
    cst = np.concatenate([ident, tri, msu, np.zeros((128, 128), np.float32)], axis=1)
    pos = np.arange(17 * 128, dtype=np.float32)
    inv = (500000.0 ** (-np.arange(0, 16, 2, dtype=np.float32) / 16.0)).astype(np.float32)
    ang = pos[:, None] * inv[None, :]
    rope = np.concatenate([np.cos(ang), np.sin(ang)], axis=1).astype(np.float32)
    return np.ascontiguousarray(cst), np.ascontiguousarray(rope)


def _pack_qkv(w):
    w = np.asarray(w, dtype=np.float32).reshape(8, 128, 3, 8, 128)
    return np.ascontiguousarray(w.transpose(3, 1, 0, 2, 4).reshape(8, 128, 3 * D))


def _pack_wout(w):
    w = np.asarray(w, dtype=np.float32).reshape(4, FC, 128, 8, 128)
    return np.ascontiguousarray(w.transpose(0, 3, 2, 1, 4).reshape(4, 8, 128, DFF))


def make_in_maps(inp):
    f = lambda a: np.ascontiguousarray(np.asarray(a, dtype=np.float32))
    cst, rope = _consts()
    shared = {
        "cst": cst, "rope": rope,
        "ada_w": f(inp["ada_w"]),
        "gm_w_in": f(inp["gm_w_in"][0]), "gm_ln_g": f(inp["gm_ln_g"][0:1]), "gm_ln_b": f(inp["gm_ln_b"][0:1]),
        "gm_ws": f(inp["gm_ws"][0]), "gm_bs": f(inp["gm_bs"][0].reshape(1, D)), "gm_w_out": f(inp["gm_w_out"][0]),
        "diff_w_qkv": _pack_qkv(inp["diff_w_qkv"][0]), "diff_lambda": f(inp["diff_lambda"][0].reshape(1, 256)),
        "diff_subln_g": f(inp["diff_subln_g"][0:1]), "diff_w_out": f(inp["diff_w_out"][0]),
        "sc_w_in": f(inp["sc_w_in"][0]), "sc_w_out": f(inp["sc_w_out"][0]),
        "sb_w_qkv": _pack_qkv(inp["sb_w_qkv"][0]), "sb_w_out": f(inp["sb_w_out"][0]),
        "ffn_w_in": f(inp["ffn_w_in"]), "ffn_w_out": _pack_wout(inp["ffn_w_out"]),
    }
    maps = []
    for c in range(NCORES):
        r1 = np.concatenate([
            inp["norm_g"].reshape(16, D),
            inp["c_prompt"][2 * c:2 * c + 2], inp["c_sample"][c:c + 1],
            inp["sc_conv_w"][0],
            inp["ada_b"].reshape(24, D),
            inp["state_sconv"][0, c],
        ], axis=0)
        r2 = np.concatenate([
            inp["ffn_conv_w"].reshape(12, DFF),
            inp["ffn_conv_b"],
            inp["state_ffn_conv"][:, c].reshape(8, DFF),
        ], axis=0)
        m = dict(shared)
        m.update({
            "xp": f(inp["x_prompt"][2 * c:2 * c + 2]),
            "xs": f(inp["x_sample"][c]),
            "cdk": f(inp["cache_diff_k"][0, c].reshape(SEQ, D)),
            "cdv": f(inp["cache_diff_v"][0, c].reshape(SEQ, D)),
            "csk": f(inp["cache_sb_k"][0, c].reshape(SEQ, D)),
            "csv": f(inp["cache_sb_v"][0, c].reshape(SEQ, D)),
            "r1": f(r1), "r2": f(r2),
        })
        maps.append(m)
    return maps


_NC_CACHE = {}


def get_program(**kw):
    key = repr(sorted(kw.items()))
    if key not in _NC_CACHE:
        _NC_CACHE[key] = KB(**kw).build()
    return _NC_CACHE[key]


def assemble(results):
    cat = lambda k: np.concatenate([r[k] for r in results], axis=0)
    stack = lambda k: np.stack([r[k] for r in results], axis=0)
    y_p = cat("y_p")
    y_s = stack("y_s")
    gmv_s = stack("gmv_s")[None]
    dk_p = cat("dk_p").reshape(1, 16, SEQ, 8, 2, 64)
    dv_p = cat("dv_p").reshape(1, 16, SEQ, 8, 128)
    dk_s = stack("dk_s").reshape(1, 8, DEC, 8, 2, 64)
    dv_s = stack("dv_s").reshape(1, 8, DEC, 8, 128)
    sc_p = cat("sc_p")[None]
    sc_s = stack("sc_s")[None]
    sbk_p = cat("sbk_p").reshape(1, 16, SEQ, 16, 64)
    sbv_p = cat("sbv_p").reshape(1, 16, SEQ, 16, 64)
    sbk_s = stack("sbk_s").reshape(1, 8, DEC, 16, 64)
    sbv_s = stack("sbv_s").reshape(1, 8, DEC, 16, 64)
    ffc_p = np.concatenate([r["ffc_p"] for r in results], axis=1)
    ffc_s = np.stack([r["ffc_s"] for r in results], axis=1)
    outs = (y_p, y_s, gmv_s, dk_p, dv_p, dk_s, dv_s, sc_p, sc_s, sbk_p, sbv_p, sbk_s, sbv_s, ffc_p, ffc_s)
    return tuple(np.ascontiguousarray(o, dtype=np.float32) for o in outs)


def kernel(**inputs):
    inp = {k: np.asarray(v) for k, v in inputs.items()}
    nc = get_program()
    maps = make_in_maps(inp)
    res = run_bass_kernel_spmd(nc, maps, core_ids=list(range(NCORES)))
    return assemble(res.results)
```
